# Optimizing a Trainium2 kernel written in Bass

```python
import jax, jax.numpy as jnp
from jax import lax
import numpy as np

D_MODEL = 1024
BATCH = 2
SEQ = 8192
DEPTH = 4

POOL_WINDOWS = (2, 4, 8, 16)
POOL_GROUPS = 4
POOL_W = D_MODEL // 2
POOL_GC = POOL_W // POOL_GROUPS
HEAD_DIM = 64
B_HEADS = 8
B_W = B_HEADS * HEAD_DIM
IDX_HEADS = 8
IDX_DIM = 32
DSA_TOPK = 256
C_HEADS = 8
C_KV_GROUPS = 2
C_W = C_HEADS * HEAD_DIM
KVW = C_KV_GROUPS * HEAD_DIM
CMP_LEN = 32
CMP_STRIDE = 16
SLC_LEN = 64
SLC_N = 16
WIN = 512
FORCE_BONUS = 1e4
N_BRANCH = 3
QBLK = 128
LN_EPS = 1e-5
NEG = -1e30
ALPHA = (2 * DEPTH) ** 0.25
BETA = (8 * DEPTH) ** -0.25

IN_WIDTHS = (POOL_W, POOL_W,
             B_W, HEAD_DIM, HEAD_DIM, B_W,
             IDX_HEADS * IDX_DIM, IDX_DIM, IDX_HEADS,
             C_W, KVW, KVW, KVW, KVW, KVW, KVW,
             C_HEADS * 3, C_W,
             N_BRANCH * D_MODEL)
N_IN = sum(IN_WIDTHS)

kernel_name = "hybrid_pool_dsa_nsa_deepnorm"


def _layernorm(x, g, b):
    xf = x.astype(jnp.float32)
    mu = jnp.mean(xf, axis=-1, keepdims=True)
    var = jnp.mean(jnp.square(xf - mu), axis=-1, keepdims=True)
    return ((xf - mu) * lax.rsqrt(var + LN_EPS) * g + b).astype(x.dtype)


def _masked_softmax(s, mask):
    p = jax.nn.softmax(jnp.where(mask, s.astype(jnp.float32), NEG), axis=-1)
    return p * mask


def _pool_mixer(xa, w, b, scale):
    B_, S, _ = xa.shape
    xg = xa.reshape(B_, S, POOL_GROUPS, POOL_GC)
    c = jnp.pad(jnp.cumsum(xg.astype(jnp.float32), axis=1), ((0, 0), (1, 0), (0, 0), (0, 0)))
    pos = jnp.arange(1, S + 1, dtype=jnp.float32)
    outs = []
    for g, wnd in enumerate(POOL_WINDOWS):
        cg = c[:, :, g]
        lo = jnp.pad(cg[:, :S + 1 - wnd], ((0, 0), (wnd - 1, 0), (0, 0)))
        mean = (cg[:, 1:] - lo) / jnp.minimum(pos, float(wnd))[None, :, None]
        outs.append(mean - xg[:, :, g].astype(jnp.float32))
    pooled = jnp.stack(outs, axis=2).astype(xa.dtype)
    y = jnp.einsum('bsgc,gcd->bsgd', pooled, w) + b
    return y.reshape(B_, S, POOL_W) * scale


def _dsa_mixer(q, k, v, iq, ik, iw):
    B_, S = q.shape[:2]
    topk = min(DSA_TOPK, S // 4)
    nb = S // QBLK
    key_pos = jnp.arange(S)
    gather = jax.vmap(lambda kk, ii: kk[ii])

    def block(i):
        q0 = i * QBLK
        t = q0 + jnp.arange(QBLK)
        qb = lax.dynamic_slice_in_dim(q, q0, QBLK, axis=1)
        iqb = lax.dynamic_slice_in_dim(iq, q0, QBLK, axis=1)
        iwb = lax.dynamic_slice_in_dim(iw, q0, QBLK, axis=1)
        rel = jax.nn.relu(jnp.einsum('bthd,bsd->bths', iqb, ik))
        score = jnp.einsum('bths,bth->bts', rel, iwb).astype(jnp.float32)
        causal = key_pos[None, :] <= t[:, None]
        score = jnp.where(causal[None], score, NEG)
        _, idx = lax.top_k(score, topk)
        kg = gather(k, idx)
        vg = gather(v, idx)
        s = jnp.einsum('bthd,btkd->bthk', qb, kg) * (HEAD_DIM ** -0.5)
        valid = (idx <= t[None, :, None])[:, :, None, :]
        p = _masked_softmax(s, valid).astype(v.dtype)
        return jnp.einsum('bthk,btkd->bthd', p, vg)

    o = lax.map(block, jnp.arange(nb))
    return o.transpose(1, 0, 2, 3, 4).reshape(B_, S, B_W)


def _nsa_mixer(q, kc_tok, vc_tok, ks, vs, kw, vw, gates, pos_k, pos_v, w1k, w2k, w1v, w2v):
    B_, S = q.shape[:2]
    G, R, dh = C_KV_GROUPS, C_HEADS // C_KV_GROUPS, HEAD_DIM
    nb = S // QBLK
    n_c = (S - CMP_LEN) // CMP_STRIDE + 1
    n_s = S // SLC_LEN
    n_sel = min(SLC_N, n_s)
    nk_sel = n_sel * SLC_LEN
    c_start = jnp.arange(n_c) * CMP_STRIDE
    blk_idx = c_start[:, None] + jnp.arange(CMP_LEN)[None, :]

    def compress(tok, pos, w1, w2):
        blk = tok[:, blk_idx] + pos[:, None, :]
        blk = blk.transpose(0, 1, 3, 2, 4).reshape(B_, n_c, G, CMP_LEN * dh)
        h = jax.nn.silu(jnp.einsum('bngf,fe->bnge', blk, w1))
        return jnp.einsum('bnge,ef->bngf', h, w2)

    kcmp = compress(kc_tok, pos_k, w1k, w2k)
    vcmp = compress(vc_tok, pos_v, w1v, w2v)
    cmp_end = c_start + CMP_LEN - 1
    s_start = jnp.arange(n_s) * SLC_LEN
    overlap = ((c_start[:, None] <= s_start[None, :] + SLC_LEN - 1)
               & (cmp_end[:, None] >= s_start[None, :])).astype(jnp.float32)
    ksT = ks.transpose(0, 2, 1, 3)
    vsT = vs.transpose(0, 2, 1, 3)
    kw_pad = jnp.pad(kw, ((0, 0), (WIN, 0), (0, 0), (0, 0)))
    vw_pad = jnp.pad(vw, ((0, 0), (WIN, 0), (0, 0), (0, 0)))
    gather2 = jax.vmap(jax.vmap(lambda kk, ii: kk[ii]))
    sel_j = jnp.arange(n_s)
    scale = HEAD_DIM ** -0.5

    def block(i):
        q0 = i * QBLK
        t = q0 + jnp.arange(QBLK)
        qb = lax.dynamic_slice_in_dim(q, q0, QBLK, axis=1).reshape(B_, QBLK, G, R, dh) * scale
        s_c = jnp.einsum('btgrd,bngd->btgrn', qb, kcmp)
        m_c = (cmp_end[None, :] <= t[:, None])[None, :, None, None, :]
        p_c = _masked_softmax(s_c, m_c)
        o_c = jnp.einsum('btgrn,bngd->btgrd', p_c.astype(vcmp.dtype), vcmp)
        imp = jnp.einsum('btgrn,ns->btgs', p_c, overlap)
        blk_t = t // SLC_LEN
        forced = (sel_j[None, :] == 0) | (sel_j[None, :] == blk_t[:, None]) | (sel_j[None, :] == blk_t[:, None] - 1)
        admissible = s_start[None, :] <= t[:, None]
        imp = imp + jnp.where(forced, FORCE_BONUS, 0.0)[None, :, None, :]
        imp = jnp.where(admissible[None, :, None, :], imp, NEG)
        _, sel = lax.top_k(imp, n_sel)
        tok = (sel[..., None] * SLC_LEN + jnp.arange(SLC_LEN)).reshape(B_, QBLK, G, nk_sel)
        tokT = tok.transpose(0, 2, 1, 3).reshape(B_, G, QBLK * nk_sel)
        kg = gather2(ksT, tokT).reshape(B_, G, QBLK, nk_sel, dh)
        vg = gather2(vsT, tokT).reshape(B_, G, QBLK, nk_sel, dh)
        s_s = jnp.einsum('btgrd,bgtkd->btgrk', qb, kg)
        m_s = (tok <= t[None, :, None, None])[:, :, :, None, :]
        p_s = _masked_softmax(s_s, m_s).astype(vg.dtype)
        o_s = jnp.einsum('btgrk,bgtkd->btgrd', p_s, vg)
        kwb = lax.dynamic_slice_in_dim(kw_pad, q0, QBLK + WIN, axis=1)
        vwb = lax.dynamic_slice_in_dim(vw_pad, q0, QBLK + WIN, axis=1)
        kpos = q0 - WIN + jnp.arange(QBLK + WIN)
        m_w = (kpos[None, :] >= 0) & (kpos[None, :] <= t[:, None]) & (kpos[None, :] > t[:, None] - WIN)
        s_w = jnp.einsum('btgrd,bkgd->btgrk', qb, kwb)
        p_w = _masked_softmax(s_w, m_w[None, :, None, None, :]).astype(vwb.dtype)
        o_w = jnp.einsum('btgrk,bkgd->btgrd', p_w, vwb)
        gb = jax.nn.sigmoid(lax.dynamic_slice_in_dim(gates, q0, QBLK, axis=1)).reshape(B_, QBLK, G, R, 3)
        o = gb[..., 0:1] * o_c + gb[..., 1:2] * o_s + gb[..., 2:3] * o_w
        return o.reshape(B_, QBLK, C_W)

    o = lax.map(block, jnp.arange(nb))
    return o.transpose(1, 0, 2, 3).reshape(B_, S, C_W)


def _layer(x, w_in, b_in, pool_w, pool_b, pool_scale, pos_k, pos_v, w1k, w2k, w1v, w2v,
           w_pa, w_pb, w_pc, w_o, ln_g, ln_b):
    B_, S, _ = x.shape
    u = jnp.einsum('bsd,de->bse', x, w_in) + b_in
    points = [int(p) for p in np.cumsum(IN_WIDTHS)[:-1]]
    (a_x, a_z, b_q, b_k, b_v, b_z, i_q, i_k, i_w, c_q, c_kc, c_vc, c_ks, c_vs,
     c_kw, c_vw, c_g, c_z, g_merge) = jnp.split(u, points, axis=-1)
    y_a = _pool_mixer(a_x, pool_w, pool_b, pool_scale) * jax.nn.silu(a_z)
    y_b = _dsa_mixer(b_q.reshape(B_, S, B_HEADS, HEAD_DIM), b_k, b_v,
                     i_q.reshape(B_, S, IDX_HEADS, IDX_DIM), i_k,
                     i_w * (IDX_HEADS ** -0.5 * IDX_DIM ** -0.5)) * jax.nn.silu(b_z)
    kv = lambda t_: t_.reshape(B_, S, C_KV_GROUPS, HEAD_DIM)
    y_c = _nsa_mixer(c_q.reshape(B_, S, C_HEADS, HEAD_DIM), kv(c_kc), kv(c_vc), kv(c_ks), kv(c_vs),
                     kv(c_kw), kv(c_vw), c_g.reshape(B_, S, C_HEADS, 3),
                     pos_k, pos_v, w1k, w2k, w1v, w2v) * jax.nn.silu(c_z)
    g = jax.nn.sigmoid(g_merge).reshape(B_, S, N_BRANCH, D_MODEL)
    m = (g[:, :, 0] * jnp.einsum('bse,ed->bsd', y_a, w_pa)
         + g[:, :, 1] * jnp.einsum('bse,ed->bsd', y_b, w_pb)
         + g[:, :, 2] * jnp.einsum('bse,ed->bsd', y_c, w_pc))
    out = jnp.einsum('bsd,de->bse', m, w_o)
    return _layernorm(ALPHA * x + out, ln_g, ln_b)


def setup_inputs(seed: int = 0) -> dict:
    key = jax.random.key(seed)
    ks = jax.random.split(key, 19)
    nrm = lambda k, shape, s: jax.random.normal(k, shape, jnp.float32) * s
    L, D = DEPTH, D_MODEL
    return {
        "x": nrm(ks[0], (BATCH, SEQ, D), 1.0),
        "w_in": nrm(ks[1], (L, D, N_IN), D ** -0.5),
        "b_in": nrm(ks[2], (L, N_IN), 0.02),
        "pool_w": nrm(ks[3], (L, POOL_GROUPS, POOL_GC, POOL_GC), POOL_GC ** -0.5),
        "pool_b": nrm(ks[4], (L, POOL_GROUPS, POOL_GC), 0.02),
        "pool_scale": 1.0 + nrm(ks[5], (L, POOL_W), 0.02),
        "cmp_pos_k": nrm(ks[6], (L, CMP_LEN, HEAD_DIM), 0.1),
        "cmp_pos_v": nrm(ks[7], (L, CMP_LEN, HEAD_DIM), 0.1),
        "cmp_w1_k": nrm(ks[8], (L, CMP_LEN * HEAD_DIM, HEAD_DIM), (CMP_LEN * HEAD_DIM) ** -0.5),
        "cmp_w2_k": nrm(ks[9], (L, HEAD_DIM, HEAD_DIM), HEAD_DIM ** -0.5),
        "cmp_w1_v": nrm(ks[10], (L, CMP_LEN * HEAD_DIM, HEAD_DIM), (CMP_LEN * HEAD_DIM) ** -0.5),
        "cmp_w2_v": nrm(ks[11], (L, HEAD_DIM, HEAD_DIM), HEAD_DIM ** -0.5),
        "w_proj_a": nrm(ks[12], (L, POOL_W, D), POOL_W ** -0.5 * BETA),
        "w_proj_b": nrm(ks[13], (L, B_W, D), B_W ** -0.5 * BETA),
        "w_proj_c": nrm(ks[14], (L, C_W, D), C_W ** -0.5 * BETA),
        "w_o": nrm(ks[15], (L, D, D), D ** -0.5 * BETA),
        "ln_g": 1.0 + nrm(ks[16], (L, D), 0.02),
        "ln_b": nrm(ks[17], (L, D), 0.02),
    }


def reference(x, w_in, b_in, pool_w, pool_b, pool_scale, cmp_pos_k, cmp_pos_v, cmp_w1_k, cmp_w2_k,
              cmp_w1_v, cmp_w2_v, w_proj_a, w_proj_b, w_proj_c, w_o, ln_g, ln_b):
    h = x
    for l in range(DEPTH):
        h = _layer(h, w_in[l], b_in[l], pool_w[l], pool_b[l], pool_scale[l],
                   cmp_pos_k[l], cmp_pos_v[l], cmp_w1_k[l], cmp_w2_k[l], cmp_w1_v[l], cmp_w2_v[l],
                   w_proj_a[l], w_proj_b[l], w_proj_c[l], w_o[l], ln_g[l], ln_b[l])
    return h
```

```python
import numpy as np
import ml_dtypes
from contextlib import ExitStack
import concourse.bass as bass
import concourse.mybir as mybir
from concourse.bass_utils import run_bass_kernel_spmd

F32 = mybir.dt.float32
BF16 = mybir.dt.bfloat16
ALU = mybir.AluOpType
AF = mybir.ActivationFunctionType
AX = mybir.AxisListType
NPBF = ml_dtypes.bfloat16

D = 1024
S = 8192
DEPTH = 4
NLOC = 2048
NBLK = 16
NFC = 57
NF = NFC * 128
NTM = 352
ALPHA = (2 * DEPTH) ** 0.25
LN_EPS = 1e-5
BIGM = 30000.0
NBIS = 16
USE_XB = False
EARLY_BV = True
TOPK = 256


class Prog:
    ENG = ("pe", "act", "dve", "pool", "sp")

    def __init__(self, nc, ndma=8):
        self.nc = nc
        self.lists = {e: [] for e in self.ENG}
        self.cnt = {e: 0 for e in self.ENG}
        self.seen = {e: {} for e in self.ENG}
        self.state = {}
        self.ndma = ndma
        self.dma_i = {"sp": 0, "act": 0}
        self.dma_val = {}
        self.phase = 0
        self.keys = set()
        self.ncc = 0

    def ek(self, eng):
        k = "%s@%d" % (eng, self.phase)
        self.keys.add(k)
        return k

    def _need(self, eng, tok, waits):
        if tok is None:
            return
        k, v = tok
        if self.seen[eng].get(k, 0) >= v:
            return
        if eng == "pe" and k.startswith("pe@"):
            return
        waits[k] = max(waits.get(k, 0), v)

    def _deps(self, eng, reads, writes):
        waits = {}
        for b in reads:
            st = self.state.get(b)
            if st:
                self._need(eng, st[0], waits)
        for b in writes:
            st = self.state.get(b)
            if st:
                self._need(eng, st[0], waits)
                for r in st[1]:
                    self._need(eng, r, waits)
        for k, v in waits.items():
            self.seen[eng][k] = v
        return list(waits.items())

    def _commit(self, tok, reads, writes):
        for b in reads:
            st = self.state.setdefault(b, [None, []])
            st[1].append(tok)
            if len(st[1]) > 16:
                mx = {}
                for k, v in st[1]:
                    mx[k] = max(mx.get(k, 0), v)
                st[1] = list(mx.items())
        for b in writes:
            self.state[b] = [tok, []]

    def op(self, eng, fn, reads=(), writes=()):
        waits = self._deps(eng, reads, writes)
        self.cnt[eng] += 1
        key = self.ek(eng)
        tok = (key, self.cnt[eng])
        self.lists[eng].append(("op", fn, waits, key))
        self._commit(tok, reads, writes)
        return tok

    def dma(self, fn, reads=(), writes=(), q="sp"):
        waits = self._deps(q, reads, writes)
        i = self.dma_i[q]
        self.dma_i[q] += 1
        key = "dma_%s_%d" % (q, i % self.ndma)
        self.keys.add(key)
        prev = self.dma_val.get(key, 0)
        if prev > 0 and self.seen[q].get(key, 0) < prev:
            waits.append((key, prev))
            self.seen[q][key] = prev
        val = prev + 16
        self.dma_val[key] = val
        tok = (key, val)
        self.lists[q].append(("dma", fn, waits, key))
        self._commit(tok, reads, writes)
        return tok

    def cc(self, fn, reads=(), writes=()):
        waits = self._deps("pool", reads, writes)
        self.ncc += 1
        self.keys.add("cc")
        tok = ("cc", self.ncc)
        self.lists["pool"].append(("op", fn, waits, "cc"))
        for b in reads:
            st = self.state.setdefault(b, [None, []])
            st[1].append(tok)
        for b in writes:
            st = self.state.get(b)
            if st and st[0] and st[0][0] == "cc":
                st[0] = tok
            else:
                self.state[b] = [tok, []]
        return tok

    def barrier(self):
        cur = {e: (self.ek(e), self.cnt[e]) for e in self.ENG}
        for e in self.ENG:
            waits = {}
            for f in self.ENG:
                if f == "sp" or cur[f][1] == 0:
                    continue
                if f == e and e == "pe":
                    continue
                self._need(e, cur[f], waits)
            for k, v in self.dma_val.items():
                self._need(e, (k, v), waits)
            if self.ncc:
                self._need(e, ("cc", self.ncc), waits)
            for k, v in waits.items():
                self.seen[e][k] = v
            self.lists[e].append(("wait", None, list(waits.items()), None))
        self.state = {}
        self.phase += 1
        self.cnt = {e: 0 for e in self.ENG}

    def finish(self, eng="sp"):
        waits = {}
        for k, v in self.dma_val.items():
            self._need(eng, (k, v), waits)
        self.lists[eng].append(("wait", None, list(waits.items()), None))

    def emit(self, block, es):
        nc = self.nc
        sems = {k: es.enter_context(nc.semaphore("s_" + k.replace("@", "_"))) for k in sorted(self.keys)}

        def run(lst):
            def body(e):
                for kind, fn, waits, key in lst:
                    for k, v in waits:
                        e.wait_ge(sems[k], v)
                    if kind == "wait":
                        continue
                    ins = fn(e)
                    ins.then_inc(sems[key], 1 if kind == "op" else 16)
            return body
        block.tensor(run(self.lists["pe"]))
        block.scalar(run(self.lists["act"]))
        block.vector(run(self.lists["dve"]))
        block.gpsimd(run(self.lists["pool"]))
        block.sync(run(self.lists["sp"]))


_IN_W = (512, 512, 512, 64, 64, 512, 256, 32, 8, 512, 128, 128, 128, 128, 128, 128, 24, 512, 3072)
_NAMES = ("a_x", "a_z", "b_q", "b_k", "b_v", "b_z", "i_q", "i_k", "i_w", "c_q", "c_kc", "c_vc", "c_ks", "c_vs",
          "c_kw", "c_vw", "c_g", "c_z", "g_merge")
_OFF = {}
_o = 0
for _n, _w in zip(_NAMES, _IN_W):
    _OFF[_n] = _o
    _o += _w

_CPERM = np.concatenate([np.concatenate([np.arange(64) + 64 * j, np.arange(64) + 64 * (4 + j)]) for j in range(4)])


def packed_cols():
    cols = []
    r = lambda n: list(range(_OFF[n], _OFF[n] + _IN_W[_NAMES.index(n)]))
    cols += r("a_x") + r("a_z") + r("b_z")
    cz = np.array(r("c_z"))
    cols += list(cz[_CPERM])
    cols += r("g_merge")
    bq = r("b_q"); iq = r("i_q")
    for h in range(8):
        cols += bq[64 * h:64 * h + 64] + iq[32 * h:32 * h + 32] + [-1] * 32
    cq = np.array(r("c_q"))
    cols += list(cq[_CPERM])
    cols += r("b_k") + r("i_k") + [-1] * 32
    cols += r("c_ks") + r("c_kw") + r("c_kc") + r("c_vc")
    assert len(cols) == NF, len(cols)
    tm = r("b_v") + r("c_vs") + r("c_vw") + r("i_w") + r("c_g")
    assert len(tm) == NTM
    return np.array(cols), np.array(tm)


_PCOLS, _TCOLS = packed_cols()


def pack_w(w_in_l, b_in_l):
    Wp = np.zeros((D, NF), np.float32)
    bpk = np.zeros((NF,), np.float32)
    m = _PCOLS >= 0
    Wp[:, m] = w_in_l[:, _PCOLS[m]]
    bpk[m] = b_in_l[_PCOLS[m]]
    Wv = np.ascontiguousarray(w_in_l[:, _TCOLS])
    bt = np.ascontiguousarray(np.tile(b_in_l[_TCOLS][None, :], (128, 1)))
    return Wp, np.ascontiguousarray(bpk.reshape(NFC, 128).T), Wv, bt


def core_tokens(c):
    r = c % 4
    idx = np.concatenate([np.arange(128) + 128 * (4 * i + r) for i in range(NBLK)])
    return c // 4, idx


def make_consts(r):
    t = np.arange(128)[:, None]
    s = np.arange(128)[None, :]
    caus = np.zeros((128, 4, 128), np.float32)
    for j in range(4):
        if j == r:
            caus[:, j, :] = np.where(s > t, -1.0, 0.0)
        elif j > r:
            caus[:, j, :] = -1.0
    causbig = (caus * 1e30).reshape(128, 512).astype(np.float32)
    idbig = np.tile(np.eye(128, dtype=np.float32) * BIGM, (1, 4))
    col = np.arange(1024)[None, :]
    patc = np.where(16 * (col - 512 - 8 * r) + 31 <= t, 0.0, -1.0).astype(np.float32)
    colf = np.arange(256)[None, :] - 128
    blk = 2 * r + (t >= 64)
    fp = np.zeros((128, 256), np.float32)
    fp = np.where((colf == blk) | (colf == blk - 1), 1e4, fp)
    fp = np.where(colf > blk, -1e30, fp).astype(np.float32)
    f0 = np.full((128, NBLK), 1e4, np.float32)
    if r == 0:
        f0[:, 0] = 0.0
    wm = np.zeros((128, 8, 128), np.float32)
    for jj in range(8):
        dl = 128 * (r + 4 - jj) + t - s
        wm[:, jj, :] = np.where((dl >= 0) & (dl <= 511), 0.0, -1.0)
    n = np.arange(512)[:, None]
    jn = np.arange(128)[None, :]
    ovl = ((n >= 4 * jn - 1) & (n <= 4 * jn + 3) & (n <= 510)).astype(np.float32)
    ovl = ovl.reshape(4, 128, 128).transpose(1, 0, 2)
    invw = np.zeros((128, 4, 128), np.float32)
    for g, w in enumerate((2, 4, 8, 16)):
        if r == 0:
            invw[:, g, :] = 1.0 / np.minimum(np.arange(128) + 1, w)[None, :]
        else:
            invw[:, g, :] = 1.0 / w
    bisw = np.tile((0.5 ** (np.arange(NBIS) + 1))[None, :], (128, 1)).astype(np.float32)
    return {
        "c_caus": caus.astype(NPBF), "c_causbig": causbig, "c_idbig": idbig.astype(NPBF),
        "c_patc": patc.astype(NPBF), "c_fp": fp, "c_f0": f0, "c_wm": wm.astype(NPBF),
        "c_ovl": np.ascontiguousarray(ovl).astype(NPBF), "c_invw": invw, "c_bisw": bisw,
        "c_ident": np.eye(128, dtype=np.float32).astype(NPBF),
        "c_onesm": np.full((128, 128), 1.0 / D, np.float32),
        "c_sel": np.tile(np.eye(4, dtype=np.float32)[(r - 1) % 4][None, :], (128, 1)),
    }


def build_F(debug=False):
    nc = bass.Bass("TRN2", target_bir_lowering=False)
    di = lambda n, s, d: nc.dram_tensor(n, s, d, kind="ExternalInput").ap()
    XT0 = di("XT0", [D, NLOC], F32)
    WP = di("WP", [DEPTH, NFC, 128, 8 * 128], F32)
    BP = di("BP", [DEPTH, 128, NFC], F32)
    WV = di("WV", [DEPTH, D, NTM], F32)
    BT = di("BT", [DEPTH, 128, NTM], F32)
    WSRC = di("WSRC", [DEPTH, 2560, D], F32)
    POOLW_ = di("POOLW", [DEPTH, 128, 4, 128], F32)
    PVEC_ = di("PVEC", [DEPTH, 128, 8], F32)
    W1K_ = di("W1K", [DEPTH, 128, 32, 128], F32)
    W1V_ = di("W1V", [DEPTH, 128, 32, 128], F32)
    W2_ = di("W2", [DEPTH, 128, 2, 128], F32)
    POS_ = di("POS", [DEPTH, 128, 2, 32], F32)
    LNV_ = di("LNV", [DEPTH, 128, 16], F32)
    cst = {
        "c_caus": di("c_caus", [128, 4, 128], BF16), "c_causbig": di("c_causbig", [128, 512], F32),
        "c_idbig": di("c_idbig", [128, 512], BF16), "c_patc": di("c_patc", [128, 1024], BF16),
        "c_fp": di("c_fp", [128, 256], F32), "c_f0": di("c_f0", [128, NBLK], F32),
        "c_wm": di("c_wm", [128, 8, 128], BF16), "c_ovl": di("c_ovl", [128, 4, 128], BF16),
        "c_invw": di("c_invw", [128, 4, 128], F32), "c_bisw": di("c_bisw", [128, NBIS], F32),
        "c_ident": di("c_ident", [128, 128], BF16), "c_onesm": di("c_onesm", [128, 128], F32),
        "c_sel": di("c_sel", [128, 4], F32),
    }
    YT = nc.dram_tensor("YT", [D, NLOC], F32, kind="ExternalOutput").ap()
    DBG = nc.dram_tensor("DBG", [3, 128, 4, NLOC], BF16, kind="ExternalOutput").ap() if debug else None
    OWNF = nc.dram_tensor("OWNS", [52 * 128, NLOC], BF16).ap()
    XS = nc.dram_tensor("XS", [D, NLOC], F32).ap()
    XB = nc.dram_tensor("XB", [D, NLOC], BF16).ap()
    W16A = nc.dram_tensor("W16A", [DEPTH, 2560, D], BF16).ap()
    NCH = 10
    GS = nc.dram_tensor("GS", [NCH * 256, 1024], BF16).ap()
    GD = nc.dram_tensor("GD", [NCH * 4 * 256, 1024], BF16).ap()
    flat = lambda ap: ap.rearrange("a c -> (a c)")
    gdr = lambda k, r: GD[(k * 4 + r) * 256:(k * 4 + r + 1) * 256, :]
    GSF = [GS[q * 256:(q + 1) * 256, :].rearrange("(p a) c -> p (a c)", a=2) for q in range(5)]
    GSTk = [GS[(5 + k) * 256:(6 + k) * 256, :].rearrange("a (h f) -> (a h) f", h=2) for k in range(4)]
    GSA = flat(GS[9 * 256:9 * 256 + 128, :]).rearrange("(c p i s) -> p c i s", c=4, p=128, i=NBLK)
    GDF = [[gdr(q, r).rearrange("(p a) c -> p (a c)", a=2) for q in range(5)] for r in range(4)]
    GDTk = [[gdr(5 + k, r).rearrange("a (h f) -> (a h) f", h=2) for k in range(4)] for r in range(4)]
    GDA = [flat(gdr(9, r)[0:128, :]).rearrange("(c p i s) -> p c i s", c=4, p=128, i=NBLK) for r in range(4)]
    with ExitStack() as es:
        sb = lambda n, s, d: es.enter_context(nc.sbuf_tensor(n, s, d))
        ps = lambda n, s, d: es.enter_context(nc.psum_tensor(n, s, d))
        kk = sb("kk", [128, S], BF16)
        kk32 = kk[:].bitcast(F32)
        ks = sb("ks", [128, S], BF16)
        ks32 = ks[:].bitcast(F32)
        bv1f = sb("bv1", [128, 64 * 65], BF16)
        bv1 = bv1f[:, :].rearrange("p (c d) -> p c d", d=65)
        vs1f = sb("vs1", [128, 64 * 2 * 65], BF16)
        vs1 = vs1f[:, :].rearrange("p (c g d) -> p c g d", g=2, d=65)
        kcmp = sb("kcmp", [128, 512], BF16)
        vcmp1 = sb("vcmp1", [128, 4, 2, 65], BF16)
        scr = sb("scr", [128, S], F32)
        scr16 = scr[:].bitcast(BF16)
        C = {k: sb("t_" + k, list(v.shape), v.dtype) for k, v in cst.items()}
        wrot = [sb("wrot%d" % j, [128, 4, D], BF16) for j in range(2)]
        poolw = sb("poolw", [128, 4, 128], BF16)
        w2b = sb("w2b", [128, 2, 128], BF16)
        pvec = sb("pvec", [128, 8], F32)
        pbs = sb("pbs", [128, 4], F32)
        lnv = sb("lnv", [128, 16], F32)
        post = sb("post", [128, 2, 32], F32)
        posb = sb("posb", [128, 2, 32], BF16)
        cbias = sb("cbias", [128, 2], F32)
        hT = sb("hT", [128, 2, 512], BF16)
        own = sb("own", [128, 52, 128], BF16)
        cand = sb("cand", [128, 4, 4, 16], BF16)
        bpt = sb("bpt", [128, NFC], F32)
        btm = sb("btm", [128, NTM], F32)
        tmo = [sb("tmo%d" % j, [128, 320], BF16) for j in range(2)]
        iwcg = sb("iwcg", [128, NBLK, 32], F32)
        xt = sb("xt", [128, 8, 128], F32)
        kwb = sb("kwb", [128, 8, 128], BF16)
        vw1 = sb("vw1", [128, 8, 2, 65], BF16)
        relu = [sb("relu%d" % j, [128, 512], F32) for j in range(2)]
        pT = [sb("pT%d" % j, [128, 1024], BF16) for j in range(2)]
        mch = [sb("mch%d" % j, [128, 128], BF16) for j in range(4)]
        sm = sb("sm", [128, 64], F32)
        wkall = sb("wkall", [128, NBIS], F32)
        junk = sb("junk", [128, S], mybir.dt.uint8)
        wrot2 = junk[:].bitcast(BF16).rearrange("p (c d) -> p c d", c=4)
        axp = sb("axp", [128, 4, 144], BF16)
        axq = sb("axq", [128, 4, 144], F32)
        axr = sb("axr", [128, 4, 144], F32)
        pooled = sb("pooled", [128, 4, 128], BF16)
        yaT = sb("yaT", [128, 4, 128], BF16)
        ybT = sb("ybT", [128, 4, 128], BF16)
        ycT = sb("ycT", [128, 4, 128], BF16)
        gat = sb("gat", [128, 12, 128], BF16)
        sgm = sb("sgm", [128, 24, 128], BF16)
        ytm = sb("ytm", [128, 512], BF16)
        oc = sb("oc", [128, 512], F32)
        otmp = sb("otmp", [128, 512], F32)
        rd = sb("rd", [128, 40], F32)
        imp = sb("imp", [128, 2, 128], F32)
        impw = sb("impw", [128, 128], F32)
        mx = sb("mx", [128, 16], F32)
        selm = sb("selm", [128, 2, 128], BF16)
        mT = sb("mT", [128, 8, 128], F32)
        mTb = sb("mTb", [128, 8, 128], BF16)
        zt = sb("zt", [128, 8, 128], F32)
        zc = sb("zc", [128, 8, 128], F32)
        rstd = sb("rstd", [128, 128], F32)
        psS = [ps("psS%d" % j, [128, 1024], F32) for j in range(2)]
        psO = ps("psO", [128, 2, 512], F32)
        psI = [ps("psI%d" % j, [128, 512], F32) for j in range(2)]

        P = Prog(nc)
        block = es.enter_context(nc.Block())
        mm = lambda out, l, r, st, sp: (lambda e: e.matmul(out, l, r, start=st, stop=sp, skip_group_check=True))

        for k in cst:
            P.dma(lambda e, k=k: e.dma_start(out=C[k][:], in_=cst[k]), writes=[k])
        P.op("pool", lambda e: e.memset(bv1[:], 1.0), writes=["bv1"])
        xs = [kk32[:, 0:2048], kk32[:, 2048:4096]]
        xb = scr16[:, :].rearrange("p (k t) -> p k t", k=8)
        wsA = [ks32[:, 0:1024].rearrange("p (k f) -> p k f", k=8), ks32[:, 1024:2048].rearrange("p (k f) -> p k f", k=8)]
        wbA = [ks[:, 4096:5120].rearrange("p (k f) -> p k f", k=8), ks[:, 5120:6144].rearrange("p (k f) -> p k f", k=8)]
        otA = [vs1f[:, 0:2048], vs1f[:, 2048:4096]]
        wvA = vs1f[:, 4096:4096 + 8 * NTM].rearrange("p (k f) -> p k f", k=8)
        ppA = [psS[0][:, 0:512], psS[0][:, 512:1024], psS[1][:, 0:512], psS[1][:, 512:1024],
               psO[:, 0, :], psO[:, 1, :], psI[0][:, :], psI[1][:, :]]
        ppN = ["psS0", "psS0b", "psS1", "psS1b", "psO", "psOb", "psI0", "psI1"]
        for l in range(DEPTH):
            for rc in range(20):
                j = rc % 2
                P.dma(lambda e, l=l, rc=rc, j=j: e.dma_start(out=xs[j][:, 0:1024], in_=WSRC[l, rc * 128:(rc + 1) * 128, :]), writes=["xs%d" % j])
                P.op("pool" if j else "dve", lambda e, j=j: e.tensor_copy(otA[j][:, 0:1024], xs[j][:, 0:1024]), reads=["xs%d" % j], writes=["ot%d" % j])
                P.dma(lambda e, l=l, rc=rc, j=j: e.dma_start(out=W16A[l, rc * 128:(rc + 1) * 128, :], in_=otA[j][:, 0:1024]), reads=["ot%d" % j], q="act")
        P.barrier()

        def phase_A(l):
            gs_names = []

            def gsn():
                gs_names.append("GS_%d" % len(gs_names))
                return [gs_names[-1]]
            xsrc = XT0 if l == 0 else XS
            P.dma(lambda e: e.dma_start(out=bpt[:], in_=BP[l]), writes=["bpt"])
            P.dma(lambda e: e.dma_start(out=btm[:], in_=BT[l]), writes=["btm"])
            for kc in range(8):
                j = kc % 2
                if l == 0 or not USE_XB:
                    P.dma(lambda e, kc=kc, j=j: e.dma_start(out=xs[j], in_=xsrc[kc * 128:(kc + 1) * 128, :]), writes=["xs%d" % j])
                    P.op("dve", lambda e, kc=kc, j=j: e.tensor_copy(xb[:, kc, :], xs[j]), reads=["xs%d" % j], writes=["xb%d" % kc])
                else:
                    P.dma(lambda e, kc=kc: e.dma_start(out=xb[:, kc, :], in_=XB[kc * 128:(kc + 1) * 128, :]), writes=["xb%d" % kc])
            for kc in range(8):
                j = kc % 2
                P.dma(lambda e, kc=kc, j=j: e.dma_start(out=wsA[j][:, 0:3, :].rearrange("p a f -> p (a f)")[:, 0:NTM],
                                                        in_=WV[l, kc * 128:(kc + 1) * 128, :]), writes=["ws%d" % j])
                P.op("dve", lambda e, kc=kc, j=j: e.tensor_copy(wvA[:, kc, :], wsA[j][:, 0:3, :].rearrange("p a f -> p (a f)")[:, 0:NTM]),
                     reads=["ws%d" % j], writes=["wv"])
            for i in range(NBLK):
                pt = ppA[i % 8]
                pn = ppN[i % 8]
                j = i % 2
                for kc in range(8):
                    P.op("pe", mm(pt[:, 0:NTM], xb[:, kc, i * 128:(i + 1) * 128], wvA[:, kc, :], kc == 0, kc == 7),
                         reads=["xb%d" % kc, "wv"], writes=[pn])
                P.op("dve", lambda e, pt=pt, j=j: e.tensor_tensor(out=tmo[j][:, :], in0=pt[:, 0:320], in1=btm[:, 0:320], op=ALU.add),
                     reads=[pn, "btm"], writes=["tmo%d" % j])
                P.op("dve", lambda e, pt=pt, i=i: e.tensor_tensor(out=iwcg[:, i, :], in0=pt[:, 320:352], in1=btm[:, 320:352], op=ALU.add),
                     reads=[pn, "btm"], writes=["iwcg"])
                P.dma(lambda e, i=i, j=j: e.dma_start(out=GSTk[i // 4][(i % 4) * 128:(i % 4 + 1) * 128, 0:320], in_=tmo[j][:, :]), reads=["tmo%d" % j], writes=gsn(), q="act")
            fc_order = list(range(52, NFC)) + list(range(0, 52))
            for fi, fc in enumerate(fc_order):
                j = fi % 2
                if fi == 9:
                    for k in range(NCH):
                        P.cc(lambda e, k=k: e.collective_compute("AllGather", ALU.bypass, replica_groups=[[0, 1, 2, 3], [4, 5, 6, 7]],
                                                                 ins=[GS[k * 256:(k + 1) * 256, :].opt()], outs=[GD[k * 1024:(k + 1) * 1024, :].opt()]),
                             reads=list(gs_names), writes=["GD"])
                P.dma(lambda e, fc=fc, j=j: e.dma_start(out=wsA[j], in_=WP[l, fc].rearrange("p (k f) -> p k f", k=8)), writes=["ws%d" % j])
                P.op("dve", lambda e, j=j: e.tensor_copy(wbA[j], wsA[j]), reads=["ws%d" % j], writes=["wb%d" % j])
                for tg in range(4):
                    pt = ppA[j * 4 + tg]
                    pn = ppN[j * 4 + tg]
                    for kc in range(8):
                        P.op("pe", mm(pt[:, :], wbA[j][:, kc, :], xb[:, kc, tg * 512:(tg + 1) * 512], kc == 0, kc == 7),
                             reads=["wb%d" % j, "xb%d" % kc], writes=[pn])
                    if tg % 2 == 0:
                        P.op("act", lambda e, pt=pt, j=j, tg=tg, fc=fc: e.activation(otA[j][:, tg * 512:(tg + 1) * 512], pt[:, :], AF.Identity,
                                                                                     bias=bpt[:, fc:fc + 1], scale=1.0),
                             reads=[pn, "bpt"], writes=["ot%d_%d" % (j, tg)])
                    else:
                        P.op("dve", lambda e, pt=pt, j=j, tg=tg, fc=fc: e.tensor_scalar(out=otA[j][:, tg * 512:(tg + 1) * 512], in0=pt[:, :],
                                                                                        scalar1=bpt[:, fc:fc + 1], scalar2=None, op0=ALU.add),
                             reads=[pn, "bpt"], writes=["ot%d_%d" % (j, tg)])
                otn = ["ot%d_%d" % (j, t) for t in range(4)]
                if fc < 52:
                    P.dma(lambda e, fc=fc, j=j: e.dma_start(out=OWNF[fc * 128:(fc + 1) * 128, :], in_=otA[j]), reads=otn, writes=["OWNS"], q="act")
                else:
                    P.dma(lambda e, fc=fc, j=j: e.dma_start(out=GSF[fc - 52], in_=otA[j]), reads=otn, writes=gsn(), q="act")
                if fc < 4:
                    P.dma(lambda e, fc=fc, j=j: e.dma_start(out=GSA[:, fc, :, :], in_=otA[j].rearrange("p (i t) -> p i t", t=128)[:, :, 112:128]),
                          reads=otn, writes=gsn(), q="act")

            def load_bv():
                bv4 = bv1f[:, :].rearrange("p (i r d) -> p i r d", r=4, d=65)
                for r in range(4):
                    for k in range(4):
                        P.dma(lambda e, r=r, k=k: e.dma_start(out=bv4[:, 4 * k:4 * k + 4, r, 0:64],
                                                              in_=GDTk[r][k][:, 0:64].rearrange("(i p) d -> p i d", p=128)),
                              reads=["GD"], writes=["bv1"])
            if EARLY_BV:
                load_bv()
            return load_bv

        def phase_B(l, load_bv):
            if not EARLY_BV:
                load_bv()
            POOLW, PVEC, W1K, W1V, W2, POS, LNV = POOLW_[l], PVEC_[l], W1K_[l], W1V_[l], W2_[l], POS_[l], LNV_[l]
            W16 = W16A[l]
            XT = XT0 if l == 0 else XS
            YOUT = YT if l == DEPTH - 1 else XS
            KC, VC = None, None
            kk4 = kk[:, :].rearrange("p (i r t) -> p i r t", r=4, t=128)
            ks4 = ks[:, :].rearrange("p (i r t) -> p i r t", r=4, t=128)
            for r in range(4):
                P.dma(lambda e, r=r: e.dma_start(out=kk4[:, :, r, :], in_=GDF[r][0].rearrange("p (i t) -> p i t", t=128)), reads=["GD"], writes=["kk"])
                P.dma(lambda e, r=r: e.dma_start(out=ks4[:, :, r, :], in_=GDF[r][1].rearrange("p (i t) -> p i t", t=128)), reads=["GD"], writes=["ks"])
            P.op("pool", lambda e: e.memset(vs1[:], 1.0), writes=["vs1"])
            P.op("pool", lambda e: e.memset(vw1[:], 1.0), writes=["vw1"])
            P.op("pool", lambda e: e.memset(vcmp1[:], 1.0), writes=["vcmp1"])
            P.op("pool", lambda e: e.memset(hT[:], 0.0), writes=["hT"])
            P.op("pool", lambda e: e.memset(kcmp[:], 0.0), writes=["kcmp"])
            bv4 = bv1f[:, :].rearrange("p (i r d) -> p i r d", r=4, d=65)
            vs4 = vs1f[:, :].rearrange("p (i r g d) -> p i r g d", r=4, g=2, d=65)
            for r in range(4):
                for k in range(4):
                    for g in range(2):
                        P.dma(lambda e, r=r, g=g, k=k: e.dma_start(out=vs4[:, 4 * k:4 * k + 4, r, g, 0:64],
                                                                   in_=GDTk[r][k][:, 64 + 64 * g:128 + 64 * g].rearrange("(i p) d -> p i d", p=128)),
                              reads=["GD"], writes=["vs1"])
            P.dma(lambda e: e.dma_start(out=pvec[:], in_=PVEC), writes=["pvec"])
            P.dma(lambda e: e.dma_start(out=lnv[:], in_=LNV), writes=["lnv"])
            P.dma(lambda e: e.dma_start(out=post[:], in_=POS), writes=["post"])
            P.op("dve", lambda e: e.tensor_copy(posb[:], post[:]), reads=["post"], writes=["posb"])
            P.op("dve", lambda e: e.tensor_tensor(out=pbs[:], in0=pvec[:, 0:4], in1=pvec[:, 4:8], op=ALU.mult),
                 reads=["pvec"], writes=["pbs"])
            st32 = scr[:, 0:4096].rearrange("p (l e) -> p l e", e=128)
            P.dma(lambda e: e.dma_start(out=st32[:, 0:4, :], in_=POOLW), writes=["scr"])
            P.dma(lambda e: e.dma_start(out=st32[:, 4:6, :], in_=W2), writes=["scr"])
            P.op("dve", lambda e: e.tensor_copy(poolw[:], st32[:, 0:4, :]), reads=["scr"], writes=["poolw"])
            P.op("dve", lambda e: e.tensor_copy(w2b[:], st32[:, 4:6, :]), reads=["scr"], writes=["w2b"])

            w1b = wrot[0][:].rearrange("p a b -> p (a b)")[:, 0:4096].rearrange("p (l e) -> p l e", e=128)
            for kv, (W1, SRC) in enumerate(((W1K, KC), (W1V, VC))):
                P.dma(lambda e, W1=W1: e.dma_start(out=st32[:, :, :], in_=W1), writes=["scr"])
                P.op("dve", lambda e: e.tensor_copy(w1b, st32[:, :, :]), reads=["scr"], writes=["wrot0"])
                src16 = scr16[:, 8192:16384]
                src4 = src16.rearrange("p (i r t) -> p i r t", r=4, t=128)
                for r in range(4):
                    P.dma(lambda e, r=r, kv=kv: e.dma_start(out=src4[:, :, r, :], in_=GDF[r][3 + kv].rearrange("p (i t) -> p i t", t=128)),
                          reads=["GD"], writes=["scrhi"])
                for l in range(32):
                    P.op("pe", mm(psI[0][:, 0:1], w1b[:, l, :], posb[:, kv, l:l + 1], l == 0, l == 31),
                         reads=["wrot0", "posb"], writes=["psI0"])
                P.op("dve", lambda e, kv=kv: e.tensor_copy(cbias[:, kv:kv + 1], psI[0][:, 0:1]), reads=["psI0"], writes=["cbias"])
                for l in range(32):
                    rhs = src16[:, l:min(l + 16 * 511, 8192):16]
                    P.op("pe", mm(psI[1][:, 0:511], w1b[:, l, :], rhs, l == 0, l == 31),
                         reads=["wrot0", "scrhi"], writes=["psI1"])
                P.op("act", lambda e, kv=kv: e.activation(hT[:, kv, 0:511], psI[1][:, 0:511], AF.Silu, bias=cbias[:, kv:kv + 1], scale=1.0),
                     reads=["psI1", "cbias"], writes=["hT"])
                if kv == 0:
                    P.op("pe", mm(psI[0][:, 0:511], w2b[:, 0, :], hT[:, 0, 0:511], True, True), reads=["w2b", "hT"], writes=["psI0"])
                    P.op("dve", lambda e: e.tensor_copy(kcmp[:, 0:511], psI[0][:, 0:511]), reads=["psI0"], writes=["kcmp"])
                else:
                    for c in range(4):
                        P.op("pe", mm(psI[0][:, c * 128:(c + 1) * 128], hT[:, 1, c * 128:(c + 1) * 128], w2b[:, 1, :], True, True),
                             reads=["w2b", "hT"], writes=["psI0"])
                    P.op("dve", lambda e: e.tensor_copy(vcmp1[:, :, :, 0:64],
                                                        psI[0][:, :].rearrange("p (c g d) -> p c g d", c=4, g=2)),
                         reads=["psI0"], writes=["vcmp1"])

            OWNV = OWNF.rearrange("(c p) t -> p c t", p=128)
            XTV = XT.rearrange("(c p) t -> p c t", p=128)
            YTV = YOUT.rearrange("(c p) t -> p c t", p=128)
            W16V = W16.rearrange("(w c p) d -> w p c d", p=128, c=4)


            def attention(tag, i, chunks, kq, vfun, maskfun, kreads, between=None):
                n = len(chunks)

                def qk(ci):
                    c = chunks[ci]
                    j = ci % 2
                    pS = psS[j]
                    pn = "psS%d" % j
                    for half in range(2):
                        ml = maskfun(c, half)
                        for mi, (mlhs, mreads) in enumerate(ml):
                            P.op("pe", mm(pS[:, half * 512:(half + 1) * 512], mlhs, C["c_idbig"][:, :], mi == 0, False),
                                 reads=list(mreads) + ["c_idbig"], writes=[pn])
                        kl, qr = kq(c, half)
                        P.op("pe", mm(pS[:, half * 512:(half + 1) * 512], kl, qr, len(ml) == 0, True),
                             reads=list(kreads) + ["ownB"], writes=[pn])
                    P.op("act", lambda e, pS=pS, j=j: e.activation(pT[j][:, :], pS[:, :], AF.Exp, scale=0.125),
                         reads=[pn], writes=["pT%d" % j])

                def pv(ci):
                    c = chunks[ci]
                    j = ci % 2
                    for hh in range(8):
                        P.op("pe", mm(psO[:, hh // 4, (hh % 4) * 65:(hh % 4) * 65 + 65], pT[j][:, hh * 128:(hh + 1) * 128], vfun(c, hh),
                                      ci == 0 and hh % 4 == 0, ci == n - 1),
                             reads=["pT%d" % j, "bv1", "vs1", "vw1", "vcmp1"], writes=["psO"])

                qk(0)
                for ci in range(n):
                    if ci + 1 < n:
                        qk(ci + 1)
                    pv(ci)
                    if between is not None:
                        between(ci)

            def rden_of(col0):
                for hb in range(2):
                    P.op("dve", lambda e, hb=hb: e.tensor_scalar(out=rd[:, col0 + 4 * hb:col0 + 4 * hb + 4],
                                                                 in0=psO[:, hb, 0:260].rearrange("p (h d) -> p h d", d=65)[:, :, 64],
                                                                 scalar1=1e-30, scalar2=None, op0=ALU.max),
                         reads=["psO"], writes=["rd"])
                P.op("dve", lambda e: e.reciprocal(rd[:, col0:col0 + 8], rd[:, col0:col0 + 8]), reads=["rd"], writes=["rd"])

            def wmul(dst, wcol0, dname):
                for hb in range(2):
                    P.op("dve", lambda e, hb=hb: e.tensor_tensor(
                        out=dst[:, hb * 256:(hb + 1) * 256].rearrange("p (h d) -> p h d", d=64),
                        in0=psO[:, hb, 0:260].rearrange("p (h d) -> p h d", d=65)[:, :, 0:64],
                        in1=rd[:, wcol0 + 4 * hb:wcol0 + 4 * hb + 4].unsqueeze(2).to_broadcast([128, 4, 64]),
                        op=ALU.mult), reads=["psO", "rd"], writes=[dname])

            def to_featmajor(src, dstT, gate0, dname):
                pst = psI[0][:, 0:256].bitcast(BF16)
                for c in range(4):
                    P.op("pe", lambda e, c=c: e.transpose(pst[:, c * 128:(c + 1) * 128], src[:, c * 128:(c + 1) * 128], C["c_ident"][:, :]),
                         reads=["ytm", "c_ident"], writes=["psI0"])
                P.op("dve", lambda e: e.tensor_tensor(out=dstT[:].rearrange("p c t -> p (c t)"), in0=pst,
                                                      in1=gat[:, gate0:gate0 + 4, :].rearrange("p c t -> p (c t)"), op=ALU.mult),
                     reads=["psI0", "gat"], writes=[dname])

            def tail(tsl, last):
                P.dma(lambda e, tsl=tsl: e.dma_start(out=xt[:], in_=XTV[:, :, tsl]), reads=["XS"], writes=["xt"])
                if debug:
                    for bi, (yy, ynm) in enumerate(((yaT, "yaT"), (ybT, "ybT"), (ycT, "ycT"))):
                        P.dma(lambda e, bi=bi, yy=yy, tsl=tsl: e.dma_start(out=DBG[bi, :, :, tsl], in_=yy[:]), reads=[ynm])
                ys = (yaT, ybT, ycT)
                yn = ("yaT", "ybT", "ycT")
                for br in range(3):
                    wt = wrot[br][:] if br < 2 else wrot2
                    wn = ("wrot%d" % br) if br < 2 else "junk"
                    pM = psS[br % 2]
                    pmn = "psS%d" % (br % 2)
                    for dc in range(8):
                        for ec in range(4):
                            P.op("pe", mm(pM[:, dc * 128:(dc + 1) * 128], wt[:, ec, dc * 128:(dc + 1) * 128], ys[br][:, ec, :], ec == 0, ec == 3),
                                 reads=[wn, yn[br]], writes=[pmn])
                    if br == 0:
                        P.op("dve", lambda e, pM=pM: e.tensor_tensor(out=mT[:].rearrange("p c t -> p (c t)"), in0=pM[:, :],
                                                                     in1=sgm[:, 0:8, :].rearrange("p c t -> p (c t)"), op=ALU.mult),
                             reads=[pmn, "sgm"], writes=["mT"])
                    else:
                        P.op("dve", lambda e, pM=pM, br=br: e.tensor_tensor(out=zc[:].rearrange("p c t -> p (c t)"), in0=pM[:, :],
                                                                            in1=sgm[:, 8 * br:8 * br + 8, :].rearrange("p c t -> p (c t)"), op=ALU.mult),
                             reads=[pmn, "sgm"], writes=["zc"])
                        P.op("pool", lambda e: e.tensor_tensor(out=mT[:], in0=mT[:], in1=zc[:], op=ALU.add), reads=["mT", "zc"], writes=["mT"])
                P.op("dve", lambda e: e.tensor_copy(mTb[:], mT[:]), reads=["mT"], writes=["mTb"])
                pOut = psS[1]
                for hf in range(2):
                    wt = wrot[(3 + hf) % 2]
                    wn = "wrot%d" % ((3 + hf) % 2)
                    P.dma(lambda e, wt=wt, hf=hf: e.dma_start(out=wt[:], in_=W16V[3 + hf]), writes=[wn])
                for hf in range(2):
                    wt = wrot[1] if hf == 0 else wrot[0]
                    wn = "wrot1" if hf == 0 else "wrot0"
                    for ec in range(8):
                        for dc in range(4 * hf, 4 * hf + 4):
                            P.op("pe", mm(pOut[:, ec * 128:(ec + 1) * 128], wt[:, dc % 4, ec * 128:(ec + 1) * 128], mTb[:, dc, :],
                                          hf == 0 and dc == 0 and ec % 4 == 0, hf == 1 and dc == 7),
                                 reads=[wn, "mTb"], writes=["psS1"])
                P.op("dve", lambda e: e.scalar_tensor_tensor(out=zt[:].rearrange("p c t -> p (c t)"), in0=xt[:].rearrange("p c t -> p (c t)"),
                                                             scalar=float(ALPHA), in1=pOut[:, :], op0=ALU.mult, op1=ALU.add),
                     reads=["xt", "psS1"], writes=["zt"])
                for dc in range(8):
                    P.op("pe", mm(psI[0][:, 0:128], C["c_onesm"][:, :], zt[:, dc, :], dc == 0, dc == 7), reads=["c_onesm", "zt"], writes=["psI0"])
                P.op("dve", lambda e: e.tensor_tensor(out=zc[:], in0=zt[:], in1=psI[0][:, 0:128].unsqueeze(1).to_broadcast([128, 8, 128]), op=ALU.subtract),
                     reads=["zt", "psI0"], writes=["zc"])
                P.op("act", lambda e: e.activation(zt[:], zc[:], AF.Square), reads=["zc"], writes=["zt"])
                for dc in range(8):
                    P.op("pe", mm(psI[1][:, 0:128], C["c_onesm"][:, :], zt[:, dc, :], dc == 0, dc == 7), reads=["c_onesm", "zt"], writes=["psI1"])
                P.op("dve", lambda e: e.tensor_scalar(out=rstd[:, :], in0=psI[1][:, 0:128], scalar1=float(LN_EPS), scalar2=None, op0=ALU.add),
                     reads=["psI1"], writes=["rstd"])
                P.op("act", lambda e: e.activation(rstd[:, :], rstd[:, :], AF.Sqrt), reads=["rstd"], writes=["rstd"])
                P.op("dve", lambda e: e.reciprocal(rstd[:, :], rstd[:, :]), reads=["rstd"], writes=["rstd"])
                P.op("dve", lambda e: e.tensor_tensor(out=zc[:], in0=zc[:], in1=rstd[:, :].unsqueeze(1).to_broadcast([128, 8, 128]), op=ALU.mult),
                     reads=["zc", "rstd"], writes=["zc"])
                for dc in range(8):
                    P.op("dve" if dc % 2 else "pool", lambda e, dc=dc: e.tensor_scalar(out=zt[:, dc, :], in0=zc[:, dc, :], scalar1=lnv[:, dc:dc + 1], scalar2=lnv[:, 8 + dc:9 + dc],
                                                          op0=ALU.mult, op1=ALU.add), reads=["zc", "lnv"], writes=["zt"])
                P.dma(lambda e, tsl=tsl: e.dma_start(out=YTV[:, :, tsl], in_=zt[:]), reads=["zt"], writes=["XS"])
                if l < DEPTH - 1 and USE_XB:
                    P.op("pool", lambda e: e.tensor_copy(mTb[:], zt[:]), reads=["zt"], writes=["mTb"])
                    P.dma(lambda e, tsl=tsl: e.dma_start(out=XB.rearrange("(c p) t -> p c t", p=128)[:, :, tsl], in_=mTb[:]), reads=["mTb"], writes=["XB"])
                if not last:
                    for br in range(2):
                        P.dma(lambda e, br=br: e.dma_start(out=wrot[br][:], in_=W16V[br]), writes=["wrot%d" % br])

            for br in range(2):
                P.dma(lambda e, br=br: e.dma_start(out=wrot[br][:], in_=W16V[br]), writes=["wrot%d" % br])
            for i in range(NBLK):
                NK = 4 * i + 4
                tsl = slice(i * 128, (i + 1) * 128)
                if i == 0:
                    P.dma(lambda e, tsl=tsl: e.dma_start(out=own[:, 0:40, :], in_=OWNV[:, 0:40, tsl]), reads=["OWNS"], writes=["ownA"])
                    P.dma(lambda e, tsl=tsl: e.dma_start(out=own[:, 40:52, :], in_=OWNV[:, 40:52, tsl]), reads=["OWNS"], writes=["ownB"])
                j0 = 4 if i == 0 else 0
                kc0 = 4 * i - 4 + j0
                for jj in range(j0, 8):
                    rr, li = jj % 4, i - 1 + jj // 4
                    P.dma(lambda e, jj=jj, rr=rr, li=li: e.dma_start(out=kwb[:, jj, :], in_=GDF[rr][2][:, li * 128:(li + 1) * 128]),
                          reads=["GD"], writes=["kwb"])
                    P.dma(lambda e, jj=jj, rr=rr, li=li: e.dma_start(out=vw1[:, jj, :, 0:64],
                                                                     in_=GDTk[rr][li // 4][(li % 4) * 128:(li % 4 + 1) * 128, 192:320].rearrange("p (g d) -> p g d", g=2)),
                          reads=["GD"], writes=["vw1"])
                for k in range(4):
                    li = i if k < 3 else i - 1
                    if li < 0:
                        continue
                    P.dma(lambda e, k=k, li=li: e.dma_start(out=cand[:, k, :, :], in_=GDA[k][:, :, li, :]), reads=["GD"], writes=["cand"])
                for seg in range(i + 1):
                    last = seg == i
                    for h in range(8):
                        jj = (seg * 8 + h) % 2
                        P.op("pe", mm(psI[jj][:, :], own[64:96, 40 + h, :], kk[64:96, seg * 512:(seg + 1) * 512], True, True),
                             reads=["ownB", "kk"], writes=["psI%d" % jj])
                        P.op("act", lambda e, jj=jj: e.activation(relu[jj][:, :], psI[jj][:, :], AF.Relu),
                             reads=["psI%d" % jj], writes=["relu%d" % jj])
                        scs = scr[:, seg * 512:(seg + 1) * 512]
                        if h == 0 and last:
                            P.op("dve", lambda e, jj=jj, scs=scs, i=i: e.scalar_tensor_tensor(out=scs, in0=relu[jj][:, :], scalar=iwcg[:, i, 0:1],
                                                                                             in1=C["c_causbig"][:, :], op0=ALU.mult, op1=ALU.add),
                                 reads=["relu%d" % jj, "iwcg", "c_causbig"], writes=["scr"])
                        elif h == 0:
                            P.op("dve", lambda e, jj=jj, scs=scs, i=i: e.tensor_scalar(out=scs, in0=relu[jj][:, :], scalar1=iwcg[:, i, 0:1], scalar2=None,
                                                                                      op0=ALU.mult),
                                 reads=["relu%d" % jj, "iwcg"], writes=["scr"])
                        else:
                            P.op("dve", lambda e, jj=jj, scs=scs, i=i, h=h: e.scalar_tensor_tensor(out=scs, in0=relu[jj][:, :], scalar=iwcg[:, i, h:h + 1],
                                                                                                  in1=scs, op0=ALU.mult, op1=ALU.add),
                                 reads=["relu%d" % jj, "iwcg", "scr"], writes=["scr"])
                NC = NK * 128
                P.op("dve", lambda e, NC=NC: e.tensor_reduce(out=sm[:, 0:1], in_=scr[:, 0:NC], axis=AX.X, op=ALU.max), reads=["scr"], writes=["smb"])
                P.op("dve", lambda e, NC=NC: e.scalar_tensor_tensor(out=relu[0][:, :], in0=C["c_causbig"][:, :], scalar=-2.0,
                                                                    in1=scr[:, NC - 512:NC], op0=ALU.mult, op1=ALU.add),
                     reads=["scr", "c_causbig", "relu0"], writes=["relu0"])
                P.op("dve", lambda e: e.tensor_reduce(out=sm[:, 1:2], in_=relu[0][:, :], axis=AX.X, op=ALU.min), reads=["relu0"], writes=["smb"])
                if i > 0:
                    P.op("dve", lambda e, NC=NC: e.tensor_reduce(out=sm[:, 7:8], in_=scr[:, 0:NC - 512], axis=AX.X, op=ALU.min), reads=["scr"], writes=["smb"])
                    P.op("dve", lambda e: e.tensor_tensor(out=sm[:, 1:2], in0=sm[:, 1:2], in1=sm[:, 7:8], op=ALU.min), reads=["smb"], writes=["smb"])
                P.op("dve", lambda e: e.tensor_scalar(out=sm[:, 3:4], in0=sm[:, 1:2], scalar1=-1e-3, scalar2=None, op0=ALU.add), reads=["smb"], writes=["smb"])
                P.op("dve", lambda e: e.scalar_tensor_tensor(out=sm[:, 2:3], in0=sm[:, 0:1], scalar=2e-3, in1=sm[:, 3:4], op0=ALU.add, op1=ALU.subtract),
                     reads=["smb"], writes=["smb"])
                P.op("dve", lambda e: e.tensor_scalar(out=wkall[:, :], in0=C["c_bisw"][:, :], scalar1=sm[:, 2:3], scalar2=None, op0=ALU.mult),
                     reads=["smb", "c_bisw"], writes=["wkall"])
                bis_state = {"it": 0}

                def bis_iter(NC=NC):
                    it = bis_state["it"]
                    if it >= NBIS:
                        return
                    bis_state["it"] = it + 1
                    P.op("dve", lambda e, it=it: e.tensor_tensor(out=sm[:, 4:5], in0=sm[:, 3:4], in1=wkall[:, it:it + 1], op=ALU.add),
                         reads=["smb", "wkall"], writes=["smb"])
                    P.op("dve", lambda e, NC=NC: e.tensor_scalar(out=junk[:, 0:NC], in0=scr[:, 0:NC], scalar1=sm[:, 4:5], scalar2=None,
                                                                 op0=ALU.is_ge, op1=ALU.add, accum_out=sm[:, 5:6]),
                         reads=["scr", "smb"], writes=["junk", "smb"])
                    P.op("dve", lambda e, it=it: e.scalar_tensor_tensor(out=sm[:, 6:7], in0=sm[:, 5:6], scalar=float(TOPK) - 0.5, in1=wkall[:, it:it + 1],
                                                                        op0=ALU.is_ge, op1=ALU.mult), reads=["smb", "wkall"], writes=["smb"])
                    P.op("dve", lambda e: e.tensor_tensor(out=sm[:, 3:4], in0=sm[:, 3:4], in1=sm[:, 6:7], op=ALU.add), reads=["smb"], writes=["smb"])

                def bis_between(ci, per=(2 if NK < 12 else 1)):
                    for _ in range(per):
                        bis_iter()

                kl = [0, 1, 2] + ([3] if i > 0 else [])
                for kn, k in enumerate(kl):
                    if kn == 0:
                        P.op("dve", lambda e, k=k: e.tensor_scalar(out=axp[:, :, 0:16], in0=cand[:, k, :, :], scalar1=C["c_sel"][:, k:k + 1], scalar2=None,
                                                                   op0=ALU.mult), reads=["cand", "c_sel"], writes=["axp"])
                    else:
                        P.op("dve", lambda e, k=k: e.scalar_tensor_tensor(out=axp[:, :, 0:16], in0=cand[:, k, :, :], scalar=C["c_sel"][:, k:k + 1],
                                                                          in1=axp[:, :, 0:16], op0=ALU.mult, op1=ALU.add),
                             reads=["cand", "c_sel", "axp"], writes=["axp"])
                P.op("pool", lambda e: e.tensor_copy(axp[:, :, 16:144], own[:, 0:4, :]), reads=["ownA"], writes=["axp"])
                P.op("pool", lambda e: e.tensor_tensor(out=axq[:, 0:4, 1:144], in0=axp[:, 0:4, 1:144], in1=axp[:, 0:4, 0:143], op=ALU.add),
                     reads=["axp"], writes=["axq"])
                P.op("pool", lambda e: e.tensor_tensor(out=axr[:, 1:4, 3:144], in0=axq[:, 1:4, 3:144], in1=axq[:, 1:4, 1:142], op=ALU.add),
                     reads=["axq"], writes=["axr"])
                P.op("pool", lambda e: e.tensor_tensor(out=axq[:, 2:4, 7:144], in0=axr[:, 2:4, 7:144], in1=axr[:, 2:4, 3:140], op=ALU.add),
                     reads=["axr"], writes=["axq"])
                P.op("pool", lambda e: e.tensor_tensor(out=axr[:, 3:4, 15:144], in0=axq[:, 3:4, 15:144], in1=axq[:, 3:4, 7:136], op=ALU.add),
                     reads=["axq"], writes=["axr"])
                cur = {0: axq, 1: axr, 2: axq, 3: axr}
                for g, w in enumerate((2, 4, 8, 16)):
                    srcg = cur[g]
                    if i == 0:
                        P.op("dve", lambda e, g=g, srcg=srcg: e.tensor_tensor(out=otmp[:, 0:128], in0=srcg[:, g, 16:144],
                                                                              in1=C["c_invw"][:, g, :], op=ALU.mult),
                             reads=["axq", "axr", "c_invw"], writes=["otmp"])
                        P.op("dve", lambda e, g=g: e.tensor_tensor(out=pooled[:, g, :], in0=otmp[:, 0:128], in1=own[:, g, :], op=ALU.subtract),
                             reads=["otmp", "ownA"], writes=["pooled"])
                    else:
                        P.op("dve", lambda e, g=g, w=w, srcg=srcg: e.scalar_tensor_tensor(out=pooled[:, g, :], in0=srcg[:, g, 16:144], scalar=1.0 / w,
                                                                                          in1=own[:, g, :], op0=ALU.mult, op1=ALU.subtract),
                             reads=["axq", "axr", "ownA"], writes=["pooled"])
                if i > 0:
                    tail(slice((i - 1) * 128, i * 128), False)
                P.op("act", lambda e: e.activation(gat[:], own[:, 4:16, :], AF.Silu), reads=["ownA"], writes=["gat"])
                P.op("act", lambda e: e.activation(sgm[:], own[:, 16:40, :], AF.Sigmoid), reads=["ownA"], writes=["sgm"])
                P.op("act", lambda e, i=i: e.activation(rd[:, 0:24], iwcg[:, i, 8:32], AF.Sigmoid), reads=["iwcg"], writes=["rd"])

                for g in range(4):
                    P.op("pe", mm(psI[1][:, g * 128:(g + 1) * 128], poolw[:, g, :], pooled[:, g, :], True, True),
                         reads=["poolw", "pooled"], writes=["psI1"])
                for g in range(4):
                    P.op("act", lambda e, g=g: e.activation(otmp[:, g * 128:(g + 1) * 128], psI[1][:, g * 128:(g + 1) * 128], AF.Identity,
                                                            bias=pbs[:, g:g + 1], scale=pvec[:, 4 + g:5 + g]),
                         reads=["psI1", "pbs", "pvec"], writes=["otmp"])
                P.op("dve", lambda e: e.tensor_tensor(out=yaT[:].rearrange("p c t -> p (c t)"), in0=otmp[:, :],
                                                      in1=gat[:, 0:4, :].rearrange("p c t -> p (c t)"), op=ALU.mult),
                     reads=["otmp", "gat"], writes=["yaT"])

                if i + 1 < NBLK:
                    tn = slice((i + 1) * 128, (i + 2) * 128)
                    P.dma(lambda e, tn=tn: e.dma_start(out=own[:, 0:40, :], in_=OWNV[:, 0:40, tn]), reads=["OWNS"], writes=["ownA"])
                NCc = (32 * i + 30) // 128 + 1
                kq = lambda hh: own[64 * (hh // 4):64 * (hh // 4) + 64, 48 + (hh % 4), :]

                def cmp_mask(c, half, i=i):
                    o = 512 - 32 * i + 128 * c
                    return [(C["c_patc"][:, o:o + 128], ["c_patc"])]

                for ci in range(NCc):
                    c = ci
                    j = ci % 2
                    pS = psS[j]
                    pn = "psS%d" % j
                    for half in range(2):
                        (mlhs, mreads), = cmp_mask(c, half)
                        P.op("pe", mm(pS[:, half * 512:(half + 1) * 512], mlhs, C["c_idbig"][:, :], True, False),
                             reads=["c_patc", "c_idbig"], writes=[pn])
                        P.op("pe", mm(pS[:, half * 512:(half + 1) * 512], kcmp[64 * half:64 * half + 64, c * 128:(c + 1) * 128],
                                      own[64 * half:64 * half + 64, 48:52, :], False, True),
                             reads=["kcmp", "ownB"], writes=[pn])
                    P.op("act", lambda e, pS=pS, j=j: e.activation(pT[j][:, :], pS[:, :], AF.Exp, scale=0.125), reads=[pn], writes=["pT%d" % j])
                    for hh in range(8):
                        P.op("pe", mm(psO[:, hh // 4, (hh % 4) * 65:(hh % 4) * 65 + 65], pT[j][:, hh * 128:(hh + 1) * 128], vcmp1[:, c, hh // 4, :],
                                      ci == 0 and hh % 4 == 0, ci == NCc - 1), reads=["pT%d" % j, "vcmp1"], writes=["psO"])
                        P.op("pe", mm(psI[hh // 4][:, (hh % 4) * 128:(hh % 4) * 128 + 128], pT[j][:, hh * 128:(hh + 1) * 128], C["c_ovl"][:, c, :],
                                      ci == 0 and hh % 4 == 0, ci == NCc - 1), reads=["pT%d" % j, "c_ovl"], writes=["psI%d" % (hh // 4)])
                rden_of(32)
                sig3 = rd[:, 0:24].rearrange("p (h b) -> p h b", b=3)
                P.op("dve", lambda e: e.tensor_tensor(out=sm[:, 8:16], in0=sig3[:, :, 0], in1=rd[:, 32:40], op=ALU.mult), reads=["rd"], writes=["sm"])
                for hb in range(2):
                    P.op("dve", lambda e, hb=hb: e.tensor_tensor(
                        out=oc[:, hb * 256:(hb + 1) * 256].rearrange("p (h d) -> p h d", d=64),
                        in0=psO[:, hb, 0:260].rearrange("p (h d) -> p h d", d=65)[:, :, 0:64],
                        in1=sm[:, 8 + 4 * hb:12 + 4 * hb].unsqueeze(2).to_broadcast([128, 4, 64]), op=ALU.mult),
                        reads=["psO", "sm"], writes=["oc"])
                fo = 128 - 8 * i
                for g in range(2):
                    for r_ in range(4):
                        if r_ == 0:
                            P.op("dve", lambda e, g=g, fo=fo: e.scalar_tensor_tensor(out=imp[:, g, :], in0=psI[g][:, 0:128], scalar=rd[:, 32 + 4 * g:33 + 4 * g],
                                                                                     in1=C["c_fp"][:, fo:fo + 128], op0=ALU.mult, op1=ALU.add),
                                 reads=["psI%d" % g, "rd", "c_fp"], writes=["imp"])
                        else:
                            P.op("dve", lambda e, g=g, r_=r_: e.scalar_tensor_tensor(out=imp[:, g, :], in0=psI[g][:, r_ * 128:(r_ + 1) * 128],
                                                                                     scalar=rd[:, 32 + 4 * g + r_:33 + 4 * g + r_],
                                                                                     in1=imp[:, g, :], op0=ALU.mult, op1=ALU.add),
                                 reads=["psI%d" % g, "rd", "imp"], writes=["imp"])
                    P.op("dve", lambda e, g=g, i=i: e.tensor_tensor(out=imp[:, g, 0:1], in0=imp[:, g, 0:1], in1=C["c_f0"][:, i:i + 1], op=ALU.add),
                         reads=["imp", "c_f0"], writes=["imp"])
                    P.op("dve", lambda e, g=g: e.max(out=mx[:, 0:8], in_=imp[:, g, :]), reads=["imp"], writes=["mx"])
                    P.op("dve", lambda e, g=g: e.match_replace(out=impw[:, :], in_to_replace=mx[:, 0:8], in_values=imp[:, g, :], imm_value=-3e30),
                         reads=["imp", "mx"], writes=["impw"])
                    P.op("dve", lambda e: e.max(out=mx[:, 8:16], in_=impw[:, :]), reads=["impw"], writes=["mx"])
                    P.op("dve", lambda e, g=g: e.tensor_scalar(out=selm[:, g, :], in0=imp[:, g, :], scalar1=mx[:, 15:16], scalar2=-1.0,
                                                               op0=ALU.is_ge, op1=ALU.add), reads=["imp", "mx"], writes=["selm"])

                def win_mask(c, half, i=i):
                    return [(C["c_wm"][:, c, :], ["c_wm"])]

                attention("win", i, list(range(j0, 8)),
                          kq=lambda c, g: (kwb[64 * g:64 * g + 64, c, :], own[64 * g:64 * g + 64, 48:52, :]),
                          vfun=lambda c, hh: vw1[:, c, hh // 4, :],
                          maskfun=win_mask, kreads=["kwb"], between=bis_between)
                rden_of(32)
                P.op("dve", lambda e: e.tensor_tensor(out=sm[:, 8:16], in0=sig3[:, :, 2], in1=rd[:, 32:40], op=ALU.mult), reads=["rd"], writes=["sm"])
                for hb in range(2):
                    P.op("dve", lambda e, hb=hb: e.tensor_tensor(
                        out=otmp[:, hb * 256:(hb + 1) * 256].rearrange("p (h d) -> p h d", d=64),
                        in0=psO[:, hb, 0:260].rearrange("p (h d) -> p h d", d=65)[:, :, 0:64],
                        in1=sm[:, 8 + 4 * hb:12 + 4 * hb].unsqueeze(2).to_broadcast([128, 4, 64]), op=ALU.mult),
                        reads=["psO", "sm"], writes=["otmp"])
                P.op("pool", lambda e: e.tensor_tensor(out=oc[:, :], in0=oc[:, :], in1=otmp[:, :], op=ALU.add), reads=["oc", "otmp"], writes=["oc"])
                def slc_mask(c, half, i=i):
                    g = half
                    m = mch[(2 * c + half) % 4]
                    mn = "mch%d" % ((2 * c + half) % 4)
                    P.op("pool", lambda e, m=m, c=c, g=g: e.tensor_copy(
                        m[:, :].rearrange("p (b s) -> p b s", s=64),
                        selm[:, g, 2 * c:2 * c + 2].unsqueeze(2).to_broadcast([128, 2, 64])), reads=["selm"], writes=[mn])
                    res = [(m[:, :], [mn])]
                    jl = c - 4 * i
                    if jl >= 0:
                        res.append((C["c_caus"][:, jl, :], ["c_caus"]))
                    return res

                attention("slc", i, list(range(NK)),
                          kq=lambda c, g: (ks[64 * g:64 * g + 64, c * 128:(c + 1) * 128], own[64 * g:64 * g + 64, 48:52, :]),
                          vfun=lambda c, hh: vs1[:, c, hh // 4, :],
                          maskfun=slc_mask, kreads=["ks"], between=bis_between)
                rden_of(32)
                P.op("dve", lambda e: e.tensor_tensor(out=sm[:, 8:16], in0=sig3[:, :, 1], in1=rd[:, 32:40], op=ALU.mult), reads=["rd"], writes=["sm"])
                for hb in range(2):
                    P.op("dve", lambda e, hb=hb: e.tensor_tensor(
                        out=otmp[:, hb * 256:(hb + 1) * 256].rearrange("p (h d) -> p h d", d=64),
                        in0=psO[:, hb, 0:260].rearrange("p (h d) -> p h d", d=65)[:, :, 0:64],
                        in1=sm[:, 8 + 4 * hb:12 + 4 * hb].unsqueeze(2).to_broadcast([128, 4, 64]), op=ALU.mult),
                        reads=["psO", "sm"], writes=["otmp"])
                P.op("pool", lambda e: e.tensor_tensor(out=oc[:, :], in0=oc[:, :], in1=otmp[:, :], op=ALU.add), reads=["oc", "otmp"], writes=["oc"])

                ocv = oc[:, :].rearrange("p (g j d) -> p g j d", g=2, j=4)
                P.op("dve", lambda e: e.tensor_copy(ytm[:, :].rearrange("p (j g d) -> p j g d", j=4, g=2),
                                                    oc[:, :].rearrange("p (g j d) -> p j g d", g=2, j=4)), reads=["oc"], writes=["ytm"])
                to_featmajor(ytm, ycT, 8, "ycT")

                while bis_state["it"] < NBIS:
                    bis_iter()
                P.dma(lambda e: e.dma_start(out=wrot2, in_=W16V[2]), writes=["junk"])

                def dsa_mask(c, half, i=i):
                    m = mch[(2 * c + half) % 4]
                    mn = "mch%d" % ((2 * c + half) % 4)
                    if half == 0:
                        P.op("dve", lambda e, m=m, c=c: e.tensor_scalar(out=m[:, :], in0=scr[:, c * 128:(c + 1) * 128], scalar1=sm[:, 3:4], scalar2=-1.0,
                                                                        op0=ALU.is_ge, op1=ALU.add), reads=["scr", "smb"], writes=[mn])
                        dsa_mask.cur = (m, mn)
                    m, mn = dsa_mask.cur
                    return [(m[:, :], [mn])]

                attention("dsa", i, list(range(NK)),
                          kq=lambda c, half: (kk[0:64, c * 128:(c + 1) * 128], own[0:64, 40 + 4 * half:44 + 4 * half, :]),
                          vfun=lambda c, hh: bv1[:, c, :],
                          maskfun=dsa_mask, kreads=["kk"])
                rden_of(24)
                wmul(ytm, 24, "ytm")
                to_featmajor(ytm, ybT, 4, "ybT")

                if i + 1 < NBLK:
                    tn = slice((i + 1) * 128, (i + 2) * 128)
                    P.dma(lambda e, tn=tn: e.dma_start(out=own[:, 40:52, :], in_=OWNV[:, 40:52, tn]), reads=["OWNS"], writes=["ownB"])
            tail(slice((NBLK - 1) * 128, NBLK * 128), True)

        for l in range(DEPTH):
            lb = phase_A(l)
            P.barrier()
            phase_B(l, lb)
            P.barrier()
        P.finish()
        P.emit(block, es)
    return nc


_CACHE = {}


def _get(name, fn):
    if name not in _CACHE:
        _CACHE[name] = fn()
    return _CACHE[name]


def _blockdiag_w1(w1):
    w = w1.reshape(32, 64, 64)
    o = np.zeros((128, 32, 128), np.float32)
    for g in range(2):
        o[64 * g:64 * g + 64, :, 64 * g:64 * g + 64] = w.transpose(1, 0, 2)
    return o


def kernel(x, w_in, b_in, pool_w, pool_b, pool_scale, cmp_pos_k, cmp_pos_v, cmp_w1_k, cmp_w2_k,
           cmp_w1_v, cmp_w2_v, w_proj_a, w_proj_b, w_proj_c, w_o, ln_g, ln_b):
    f = lambda a: np.ascontiguousarray(np.asarray(a, dtype=np.float32))
    x = f(x)
    nc = _get("F", build_F)
    toks = [core_tokens(c) for c in range(8)]
    WP, BP, WV, BT, WSRC, POOLW, PVEC, W1K, W1V, W2, POS, LNV = ([] for _ in range(12))
    for l in range(DEPTH):
        Wp, bpk, Wv, bt = pack_w(f(w_in[l]), f(b_in[l]))
        WP.append(Wp.reshape(8, 128, NFC, 128).transpose(2, 1, 0, 3).reshape(NFC, 128, 1024)); BP.append(bpk); WV.append(Wv); BT.append(bt)
        wpc_perm = f(w_proj_c[l])[_CPERM, :]
        WSRC.append(np.concatenate([f(w_proj_a[l]), f(w_proj_b[l]), wpc_perm, f(w_o[l])], axis=0))
        w2 = np.zeros((128, 2, 128), np.float32)
        for kv, w in enumerate((f(cmp_w2_k[l]), f(cmp_w2_v[l]))):
            for g in range(2):
                w2[64 * g:64 * g + 64, kv, 64 * g:64 * g + 64] = w
        pos = np.zeros((128, 2, 32), np.float32)
        for kv, p in enumerate((f(cmp_pos_k[l]), f(cmp_pos_v[l]))):
            pos[0:64, kv, :] = p.T
            pos[64:128, kv, :] = p.T
        W2.append(w2); POS.append(pos)
        POOLW.append(f(pool_w[l]).transpose(1, 0, 2))
        PVEC.append(np.concatenate([f(pool_b[l]).T, f(pool_scale[l]).reshape(4, 128).T], axis=1))
        W1K.append(_blockdiag_w1(f(cmp_w1_k[l]))); W1V.append(_blockdiag_w1(f(cmp_w1_v[l])))
        LNV.append(np.concatenate([f(ln_g[l]).reshape(8, 128).T, f(ln_b[l]).reshape(8, 128).T], axis=1))
    st = lambda lst: np.ascontiguousarray(np.stack(lst, axis=0).astype(np.float32))
    common = {"WP": st(WP), "BP": st(BP), "WV": st(WV), "BT": st(BT), "WSRC": st(WSRC), "POOLW": st(POOLW), "PVEC": st(PVEC),
              "W1K": st(W1K), "W1V": st(W1V), "W2": st(W2), "POS": st(POS), "LNV": st(LNV)}
    in_maps = []
    for c in range(8):
        b, idx = toks[c]
        m = {"XT0": np.ascontiguousarray(x[b, idx, :].T)}
        m.update(common)
        m.update(make_consts(c % 4))
        in_maps.append(m)
    res = run_bass_kernel_spmd(nc, in_maps, core_ids=list(range(8)))
    out = np.zeros((2, S, D), np.float32)
    for c in range(8):
        b, idx = toks[c]
        out[b, idx, :] = np.asarray(res.results[c]["YT"], dtype=np.float32).T
    return out
```

```python
import numpy as np
import ml_dtypes
from contextlib import ExitStack
import concourse.bass as bass
import concourse.mybir as mybir
from concourse.bass_utils import run_bass_kernel_spmd

F32 = mybir.dt.float32
BF16 = mybir.dt.bfloat16
ALU = mybir.AluOpType
AF = mybir.ActivationFunctionType
AX = mybir.AxisListType
NPBF = ml_dtypes.bfloat16

D = 1024
S = 8192
DEPTH = 4
NLOC = 2048
NBLK = 16
NFC = 57
NF = NFC * 128
NTM = 352
ALPHA = (2 * DEPTH) ** 0.25
LN_EPS = 1e-5
BIGM = 30000.0
NBIS = 16
USE_XB = False
EARLY_BV = True
TOPK = 256


class Prog:
    ENG = ("pe", "act", "dve", "pool", "sp")

    def __init__(self, nc, ndma=8):
        self.nc = nc
        self.lists = {e: [] for e in self.ENG}
        self.cnt = {e: 0 for e in self.ENG}
        self.seen = {e: {} for e in self.ENG}
        self.state = {}
        self.ndma = ndma
        self.dma_i = {"sp": 0, "act": 0}
        self.dma_val = {}
        self.phase = 0
        self.keys = set()
        self.ncc = 0

    def ek(self, eng):
        k = "%s@%d" % (eng, self.phase)
        self.keys.add(k)
        return k

    def _need(self, eng, tok, waits):
        if tok is None:
            return
        k, v = tok
        if self.seen[eng].get(k, 0) >= v:
            return
        if eng == "pe" and k.startswith("pe@"):
            return
        waits[k] = max(waits.get(k, 0), v)

    def _deps(self, eng, reads, writes):
        waits = {}
        for b in reads:
            st = self.state.get(b)
            if st:
                self._need(eng, st[0], waits)
        for b in writes:
            st = self.state.get(b)
            if st:
                self._need(eng, st[0], waits)
                for r in st[1]:
                    self._need(eng, r, waits)
        for k, v in waits.items():
            self.seen[eng][k] = v
        return list(waits.items())

    def _commit(self, tok, reads, writes):
        for b in reads:
            st = self.state.setdefault(b, [None, []])
            st[1].append(tok)
            if len(st[1]) > 16:
                mx = {}
                for k, v in st[1]:
                    mx[k] = max(mx.get(k, 0), v)
                st[1] = list(mx.items())
        for b in writes:
            self.state[b] = [tok, []]

    def op(self, eng, fn, reads=(), writes=()):
        waits = self._deps(eng, reads, writes)
        self.cnt[eng] += 1
        key = self.ek(eng)
        tok = (key, self.cnt[eng])
        self.lists[eng].append(("op", fn, waits, key))
        self._commit(tok, reads, writes)
        return tok

    def dma(self, fn, reads=(), writes=(), q="sp"):
        waits = self._deps(q, reads, writes)
        i = self.dma_i[q]
        self.dma_i[q] += 1
        key = "dma_%s_%d" % (q, i % self.ndma)
        self.keys.add(key)
        prev = self.dma_val.get(key, 0)
        if prev > 0 and self.seen[q].get(key, 0) < prev:
            waits.append((key, prev))
            self.seen[q][key] = prev
        val = prev + 16
        self.dma_val[key] = val
        tok = (key, val)
        self.lists[q].append(("dma", fn, waits, key))
        self._commit(tok, reads, writes)
        return tok

    def cc(self, fn, reads=(), writes=()):
        waits = self._deps("pool", reads, writes)
        self.ncc += 1
        self.keys.add("cc")
        tok = ("cc", self.ncc)
        self.lists["pool"].append(("op", fn, waits, "cc"))
        for b in reads:
            st = self.state.setdefault(b, [None, []])
            st[1].append(tok)
        for b in writes:
            st = self.state.get(b)
            if st and st[0] and st[0][0] == "cc":
                st[0] = tok
            else:
                self.state[b] = [tok, []]
        return tok

    def barrier(self):
        cur = {e: (self.ek(e), self.cnt[e]) for e in self.ENG}
        for e in self.ENG:
            waits = {}
            for f in self.ENG:
                if f == "sp" or cur[f][1] == 0:
                    continue
                if f == e and e == "pe":
                    continue
                self._need(e, cur[f], waits)
            for k, v in self.dma_val.items():
                self._need(e, (k, v), waits)
            if self.ncc:
                self._need(e, ("cc", self.ncc), waits)
            for k, v in waits.items():
                self.seen[e][k] = v
            self.lists[e].append(("wait", None, list(waits.items()), None))
        self.state = {}
        self.phase += 1
        self.cnt = {e: 0 for e in self.ENG}

    def finish(self, eng="sp"):
        waits = {}
        for k, v in self.dma_val.items():
            self._need(eng, (k, v), waits)
        self.lists[eng].append(("wait", None, list(waits.items()), None))

    def emit(self, block, es):
        nc = self.nc
        sems = {k: es.enter_context(nc.semaphore("s_" + k.replace("@", "_"))) for k in sorted(self.keys)}

        def run(lst):
            def body(e):
                for kind, fn, waits, key in lst:
                    for k, v in waits:
                        e.wait_ge(sems[k], v)
                    if kind == "wait":
                        continue
                    ins = fn(e)
                    ins.then_inc(sems[key], 1 if kind == "op" else 16)
            return body
        block.tensor(run(self.lists["pe"]))
        block.scalar(run(self.lists["act"]))
        block.vector(run(self.lists["dve"]))
        block.gpsimd(run(self.lists["pool"]))
        block.sync(run(self.lists["sp"]))


_IN_W = (512, 512, 512, 64, 64, 512, 256, 32, 8, 512, 128, 128, 128, 128, 128, 128, 24, 512, 3072)
_NAMES = ("a_x", "a_z", "b_q", "b_k", "b_v", "b_z", "i_q", "i_k", "i_w", "c_q", "c_kc", "c_vc", "c_ks", "c_vs",
          "c_kw", "c_vw", "c_g", "c_z", "g_merge")
_OFF = {}
_o = 0
for _n, _w in zip(_NAMES, _IN_W):
    _OFF[_n] = _o
    _o += _w

_CPERM = np.concatenate([np.concatenate([np.arange(64) + 64 * j, np.arange(64) + 64 * (4 + j)]) for j in range(4)])


def packed_cols():
    cols = []
    r = lambda n: list(range(_OFF[n], _OFF[n] + _IN_W[_NAMES.index(n)]))
    cols += r("a_x") + r("a_z") + r("b_z")
    cz = np.array(r("c_z"))
    cols += list(cz[_CPERM])
    cols += r("g_merge")
    bq = r("b_q"); iq = r("i_q")
    for h in range(8):
        cols += bq[64 * h:64 * h + 64] + iq[32 * h:32 * h + 32] + [-1] * 32
    cq = np.array(r("c_q"))
    cols += list(cq[_CPERM])
    cols += r("b_k") + r("i_k") + [-1] * 32
    cols += r("c_ks") + r("c_kw") + r("c_kc") + r("c_vc")
    assert len(cols) == NF, len(cols)
    tm = r("b_v") + r("c_vs") + r("c_vw") + r("i_w") + r("c_g")
    assert len(tm) == NTM
    return np.array(cols), np.array(tm)


_PCOLS, _TCOLS = packed_cols()


def pack_w(w_in_l, b_in_l):
    Wp = np.zeros((D, NF), np.float32)
    bpk = np.zeros((NF,), np.float32)
    m = _PCOLS >= 0
    Wp[:, m] = w_in_l[:, _PCOLS[m]]
    bpk[m] = b_in_l[_PCOLS[m]]
    Wv = np.ascontiguousarray(w_in_l[:, _TCOLS])
    bt = np.ascontiguousarray(np.tile(b_in_l[_TCOLS][None, :], (128, 1)))
    return Wp, np.ascontiguousarray(bpk.reshape(NFC, 128).T), Wv, bt


def core_tokens(c):
    r = c % 4
    idx = np.concatenate([np.arange(128) + 128 * (4 * i + r) for i in range(NBLK)])
    return c // 4, idx


def make_consts(r):
    t = np.arange(128)[:, None]
    s = np.arange(128)[None, :]
    caus = np.zeros((128, 4, 128), np.float32)
    for j in range(4):
        if j == r:
            caus[:, j, :] = np.where(s > t, -1.0, 0.0)
        elif j > r:
            caus[:, j, :] = -1.0
    causbig = (caus * 1e30).reshape(128, 512).astype(np.float32)
    idbig = np.tile(np.eye(128, dtype=np.float32) * BIGM, (1, 4))
    col = np.arange(1024)[None, :]
    patc = np.where(16 * (col - 512 - 8 * r) + 31 <= t, 0.0, -1.0).astype(np.float32)
    colf = np.arange(256)[None, :] - 128
    blk = 2 * r + (t >= 64)
    fp = np.zeros((128, 256), np.float32)
    fp = np.where((colf == blk) | (colf == blk - 1), 1e4, fp)
    fp = np.where(colf > blk, -1e30, fp).astype(np.float32)
    f0 = np.full((128, NBLK), 1e4, np.float32)
    if r == 0:
        f0[:, 0] = 0.0
    wm = np.zeros((128, 8, 128), np.float32)
    for jj in range(8):
        dl = 128 * (r + 4 - jj) + t - s
        wm[:, jj, :] = np.where((dl >= 0) & (dl <= 511), 0.0, -1.0)
    n = np.arange(512)[:, None]
    jn = np.arange(128)[None, :]
    ovl = ((n >= 4 * jn - 1) & (n <= 4 * jn + 3) & (n <= 510)).astype(np.float32)
    ovl = ovl.reshape(4, 128, 128).transpose(1, 0, 2)
    invw = np.zeros((128, 4, 128), np.float32)
    for g, w in enumerate((2, 4, 8, 16)):
        if r == 0:
            invw[:, g, :] = 1.0 / np.minimum(np.arange(128) + 1, w)[None, :]
        else:
            invw[:, g, :] = 1.0 / w
    bisw = np.tile((0.5 ** (np.arange(NBIS) + 1))[None, :], (128, 1)).astype(np.float32)
    return {
        "c_caus": caus.astype(NPBF), "c_causbig": causbig, "c_idbig": idbig.astype(NPBF),
        "c_patc": patc.astype(NPBF), "c_fp": fp, "c_f0": f0, "c_wm": wm.astype(NPBF),
        "c_ovl": np.ascontiguousarray(ovl).astype(NPBF), "c_invw": invw, "c_bisw": bisw,
        "c_ident": np.eye(128, dtype=np.float32).astype(NPBF),
        "c_onesm": np.full((128, 128), 1.0 / D, np.float32),
        "c_sel": np.tile(np.eye(4, dtype=np.float32)[(r - 1) % 4][None, :], (128, 1)),
    }


def build_F(debug=False):
    nc = bass.Bass("TRN2", target_bir_lowering=False)
    di = lambda n, s, d: nc.dram_tensor(n, s, d, kind="ExternalInput").ap()
    XT0 = di("XT0", [D, NLOC], F32)
    WP = di("WP", [DEPTH, NFC, 128, 8 * 128], F32)
    BP = di("BP", [DEPTH, 128, NFC], F32)
    WV = di("WV", [DEPTH, D, NTM], F32)
    BT = di("BT", [DEPTH, 128, NTM], F32)
    WSRC = di("WSRC", [DEPTH, 2560, D], F32)
    POOLW_ = di("POOLW", [DEPTH, 128, 4, 128], F32)
    PVEC_ = di("PVEC", [DEPTH, 128, 8], F32)
    W1K_ = di("W1K", [DEPTH, 128, 32, 128], F32)
    W1V_ = di("W1V", [DEPTH, 128, 32, 128], F32)
    W2_ = di("W2", [DEPTH, 128, 2, 128], F32)
    POS_ = di("POS", [DEPTH, 128, 2, 32], F32)
    LNV_ = di("LNV", [DEPTH, 128, 16], F32)
    cst = {
        "c_caus": di("c_caus", [128, 4, 128], BF16), "c_causbig": di("c_causbig", [128, 512], F32),
        "c_idbig": di("c_idbig", [128, 512], BF16), "c_patc": di("c_patc", [128, 1024], BF16),
        "c_fp": di("c_fp", [128, 256], F32), "c_f0": di("c_f0", [128, NBLK], F32),
        "c_wm": di("c_wm", [128, 8, 128], BF16), "c_ovl": di("c_ovl", [128, 4, 128], BF16),
        "c_invw": di("c_invw", [128, 4, 128], F32), "c_bisw": di("c_bisw", [128, NBIS], F32),
        "c_ident": di("c_ident", [128, 128], BF16), "c_onesm": di("c_onesm", [128, 128], F32),
        "c_sel": di("c_sel", [128, 4], F32),
    }
    YT = nc.dram_tensor("YT", [D, NLOC], F32, kind="ExternalOutput").ap()
    DBG = nc.dram_tensor("DBG", [3, 128, 4, NLOC], BF16, kind="ExternalOutput").ap() if debug else None
    OWNF = nc.dram_tensor("OWNS", [52 * 128, NLOC], BF16).ap()
    XS = nc.dram_tensor("XS", [D, NLOC], F32).ap()
    XB = nc.dram_tensor("XB", [D, NLOC], BF16).ap()
    W16A = nc.dram_tensor("W16A", [DEPTH, 2560, D], BF16).ap()
    NCH = 10
    GS = nc.dram_tensor("GS", [NCH * 256, 1024], BF16).ap()
    GD = nc.dram_tensor("GD", [NCH * 4 * 256, 1024], BF16).ap()
    flat = lambda ap: ap.rearrange("a c -> (a c)")
    gdr = lambda k, r: GD[(k * 4 + r) * 256:(k * 4 + r + 1) * 256, :]
    GSF = [GS[q * 256:(q + 1) * 256, :].rearrange("(p a) c -> p (a c)", a=2) for q in range(5)]
    GSTk = [GS[(5 + k) * 256:(6 + k) * 256, :].rearrange("a (h f) -> (a h) f", h=2) for k in range(4)]
    GSA = flat(GS[9 * 256:9 * 256 + 128, :]).rearrange("(c p i s) -> p c i s", c=4, p=128, i=NBLK)
    GDF = [[gdr(q, r).rearrange("(p a) c -> p (a c)", a=2) for q in range(5)] for r in range(4)]
    GDTk = [[gdr(5 + k, r).rearrange("a (h f) -> (a h) f", h=2) for k in range(4)] for r in range(4)]
    GDA = [flat(gdr(9, r)[0:128, :]).rearrange("(c p i s) -> p c i s", c=4, p=128, i=NBLK) for r in range(4)]
    with ExitStack() as es:
        sb = lambda n, s, d: es.enter_context(nc.sbuf_tensor(n, s, d))
        ps = lambda n, s, d: es.enter_context(nc.psum_tensor(n, s, d))
        kk = sb("kk", [128, S], BF16)
        kk32 = kk[:].bitcast(F32)
        ks = sb("ks", [128, S], BF16)
        ks32 = ks[:].bitcast(F32)
        bv1f = sb("bv1", [128, 64 * 65], BF16)
        bv1 = bv1f[:, :].rearrange("p (c d) -> p c d", d=65)
        vs1f = sb("vs1", [128, 64 * 2 * 65], BF16)
        vs1 = vs1f[:, :].rearrange("p (c g d) -> p c g d", g=2, d=65)
        kcmp = sb("kcmp", [128, 512], BF16)
        vcmp1 = sb("vcmp1", [128, 4, 2, 65], BF16)
        scr = sb("scr", [128, S], F32)
        scr16 = scr[:].bitcast(BF16)
        C = {k: sb("t_" + k, list(v.shape), v.dtype) for k, v in cst.items()}
        wrot = [sb("wrot%d" % j, [128, 4, D], BF16) for j in range(2)]
        poolw = sb("poolw", [128, 4, 128], BF16)
        w2b = sb("w2b", [128, 2, 128], BF16)
        pvec = sb("pvec", [128, 8], F32)
        pbs = sb("pbs", [128, 4], F32)
        lnv = sb("lnv", [128, 16], F32)
        post = sb("post", [128, 2, 32], F32)
        posb = sb("posb", [128, 2, 32], BF16)
        cbias = sb("cbias", [128, 2], F32)
        hT = sb("hT", [128, 2, 512], BF16)
        own = sb("own", [128, 52, 128], BF16)
        cand = sb("cand", [128, 4, 4, 16], BF16)
        bpt = sb("bpt", [128, NFC], F32)
        btm = sb("btm", [128, NTM], F32)
        tmo = [sb("tmo%d" % j, [128, 320], BF16) for j in range(2)]
        iwcg = sb("iwcg", [128, NBLK, 32], F32)
        xt = sb("xt", [128, 8, 128], F32)
        kwb = sb("kwb", [128, 8, 128], BF16)
        vw1 = sb("vw1", [128, 8, 2, 65], BF16)
        relu = [sb("relu%d" % j, [128, 512], F32) for j in range(2)]
        pT = [sb("pT%d" % j, [128, 1024], BF16) for j in range(2)]
        mch = [sb("mch%d" % j, [128, 128], BF16) for j in range(4)]
        sm = sb("sm", [128, 64], F32)
        wkall = sb("wkall", [128, NBIS], F32)
        junk = sb("junk", [128, S], mybir.dt.uint8)
        wrot2 = junk[:].bitcast(BF16).rearrange("p (c d) -> p c d", c=4)
        axp = sb("axp", [128, 4, 144], BF16)
        axq = sb("axq", [128, 4, 144], F32)
        axr = sb("axr", [128, 4, 144], F32)
        pooled = sb("pooled", [128, 4, 128], BF16)
        yaT = sb("yaT", [128, 4, 128], BF16)
        ybT = sb("ybT", [128, 4, 128], BF16)
        ycT = sb("ycT", [128, 4, 128], BF16)
        gat = sb("gat", [128, 12, 128], BF16)
        sgm = sb("sgm", [128, 24, 128], BF16)
        ytm = sb("ytm", [128, 512], BF16)
        oc = sb("oc", [128, 512], F32)
        otmp = sb("otmp", [128, 512], F32)
        rd = sb("rd", [128, 40], F32)
        imp = sb("imp", [128, 2, 128], F32)
        impw = sb("impw", [128, 128], F32)
        mx = sb("mx", [128, 16], F32)
        selm = sb("selm", [128, 2, 128], BF16)
        mT = sb("mT", [128, 8, 128], F32)
        mTb = sb("mTb", [128, 8, 128], BF16)
        zt = sb("zt", [128, 8, 128], F32)
        zc = sb("zc", [128, 8, 128], F32)
        rstd = sb("rstd", [128, 128], F32)
        psS = [ps("psS%d" % j, [128, 1024], F32) for j in range(2)]
        psO = ps("psO", [128, 2, 512], F32)
        psI = [ps("psI%d" % j, [128, 512], F32) for j in range(2)]

        P = Prog(nc)
        block = es.enter_context(nc.Block())
        mm = lambda out, l, r, st, sp: (lambda e: e.matmul(out, l, r, start=st, stop=sp, skip_group_check=True))

        for k in cst:
            P.dma(lambda e, k=k: e.dma_start(out=C[k][:], in_=cst[k]), writes=[k])
        P.op("pool", lambda e: e.memset(bv1[:], 1.0), writes=["bv1"])
        xs = [kk32[:, 0:2048], kk32[:, 2048:4096]]
        xb = scr16[:, :].rearrange("p (k t) -> p k t", k=8)
        wsA = [ks32[:, 0:1024].rearrange("p (k f) -> p k f", k=8), ks32[:, 1024:2048].rearrange("p (k f) -> p k f", k=8)]
        wbA = [ks[:, 4096:5120].rearrange("p (k f) -> p k f", k=8), ks[:, 5120:6144].rearrange("p (k f) -> p k f", k=8)]
        otA = [vs1f[:, 0:2048], vs1f[:, 2048:4096]]
        wvA = vs1f[:, 4096:4096 + 8 * NTM].rearrange("p (k f) -> p k f", k=8)
        ppA = [psS[0][:, 0:512], psS[0][:, 512:1024], psS[1][:, 0:512], psS[1][:, 512:1024],
               psO[:, 0, :], psO[:, 1, :], psI[0][:, :], psI[1][:, :]]
        ppN = ["psS0", "psS0b", "psS1", "psS1b", "psO", "psOb", "psI0", "psI1"]
        for l in range(DEPTH):
            for rc in range(20):
                j = rc % 2
                P.dma(lambda e, l=l, rc=rc, j=j: e.dma_start(out=xs[j][:, 0:1024], in_=WSRC[l, rc * 128:(rc + 1) * 128, :]), writes=["xs%d" % j])
                P.op("pool" if j else "dve", lambda e, j=j: e.tensor_copy(otA[j][:, 0:1024], xs[j][:, 0:1024]), reads=["xs%d" % j], writes=["ot%d" % j])
                P.dma(lambda e, l=l, rc=rc, j=j: e.dma_start(out=W16A[l, rc * 128:(rc + 1) * 128, :], in_=otA[j][:, 0:1024]), reads=["ot%d" % j], q="act")
        P.barrier()

        def phase_A(l):
            gs_names = []

            def gsn():
                gs_names.append("GS_%d" % len(gs_names))
                return [gs_names[-1]]
            xsrc = XT0 if l == 0 else XS
            P.dma(lambda e: e.dma_start(out=bpt[:], in_=BP[l]), writes=["bpt"])
            P.dma(lambda e: e.dma_start(out=btm[:], in_=BT[l]), writes=["btm"])
            for kc in range(8):
                j = kc % 2
                if l == 0 or not USE_XB:
                    P.dma(lambda e, kc=kc, j=j: e.dma_start(out=xs[j], in_=xsrc[kc * 128:(kc + 1) * 128, :]), writes=["xs%d" % j])
                    P.op("dve", lambda e, kc=kc, j=j: e.tensor_copy(xb[:, kc, :], xs[j]), reads=["xs%d" % j], writes=["xb%d" % kc])
                else:
                    P.dma(lambda e, kc=kc: e.dma_start(out=xb[:, kc, :], in_=XB[kc * 128:(kc + 1) * 128, :]), writes=["xb%d" % kc])
            for kc in range(8):
                j = kc % 2
                P.dma(lambda e, kc=kc, j=j: e.dma_start(out=wsA[j][:, 0:3, :].rearrange("p a f -> p (a f)")[:, 0:NTM],
                                                        in_=WV[l, kc * 128:(kc + 1) * 128, :]), writes=["ws%d" % j])
                P.op("dve", lambda e, kc=kc, j=j: e.tensor_copy(wvA[:, kc, :], wsA[j][:, 0:3, :].rearrange("p a f -> p (a f)")[:, 0:NTM]),
                     reads=["ws%d" % j], writes=["wv"])
            for i in range(NBLK):
                pt = ppA[i % 8]
                pn = ppN[i % 8]
                j = i % 2
                for kc in range(8):
                    P.op("pe", mm(pt[:, 0:NTM], xb[:, kc, i * 128:(i + 1) * 128], wvA[:, kc, :], kc == 0, kc == 7),
                         reads=["xb%d" % kc, "wv"], writes=[pn])
                P.op("dve", lambda e, pt=pt, j=j: e.tensor_tensor(out=tmo[j][:, :], in0=pt[:, 0:320], in1=btm[:, 0:320], op=ALU.add),
                     reads=[pn, "btm"], writes=["tmo%d" % j])
                P.op("dve", lambda e, pt=pt, i=i: e.tensor_tensor(out=iwcg[:, i, :], in0=pt[:, 320:352], in1=btm[:, 320:352], op=ALU.add),
                     reads=[pn, "btm"], writes=["iwcg"])
                P.dma(lambda e, i=i, j=j: e.dma_start(out=GSTk[i // 4][(i % 4) * 128:(i % 4 + 1) * 128, 0:320], in_=tmo[j][:, :]), reads=["tmo%d" % j], writes=gsn(), q="act")
            fc_order = list(range(52, NFC)) + list(range(0, 52))
            for fi, fc in enumerate(fc_order):
                j = fi % 2
                if fi == 9:
                    for k in range(NCH):
                        P.cc(lambda e, k=k: e.collective_compute("AllGather", ALU.bypass, replica_groups=[[0, 1, 2, 3], [4, 5, 6, 7]],
                                                                 ins=[GS[k * 256:(k + 1) * 256, :].opt()], outs=[GD[k * 1024:(k + 1) * 1024, :].opt()]),
                             reads=list(gs_names), writes=["GD"])
                P.dma(lambda e, fc=fc, j=j: e.dma_start(out=wsA[j], in_=WP[l, fc].rearrange("p (k f) -> p k f", k=8)), writes=["ws%d" % j])
                P.op("dve", lambda e, j=j: e.tensor_copy(wbA[j], wsA[j]), reads=["ws%d" % j], writes=["wb%d" % j])
                for tg in range(4):
                    pt = ppA[j * 4 + tg]
                    pn = ppN[j * 4 + tg]
                    for kc in range(8):
                        P.op("pe", mm(pt[:, :], wbA[j][:, kc, :], xb[:, kc, tg * 512:(tg + 1) * 512], kc == 0, kc == 7),
                             reads=["wb%d" % j, "xb%d" % kc], writes=[pn])
                    if tg % 2 == 0:
                        P.op("act", lambda e, pt=pt, j=j, tg=tg, fc=fc: e.activation(otA[j][:, tg * 512:(tg + 1) * 512], pt[:, :], AF.Identity,
                                                                                     bias=bpt[:, fc:fc + 1], scale=1.0),
                             reads=[pn, "bpt"], writes=["ot%d_%d" % (j, tg)])
                    else:
                        P.op("dve", lambda e, pt=pt, j=j, tg=tg, fc=fc: e.tensor_scalar(out=otA[j][:, tg * 512:(tg + 1) * 512], in0=pt[:, :],
                                                                                        scalar1=bpt[:, fc:fc + 1], scalar2=None, op0=ALU.add),
                             reads=[pn, "bpt"], writes=["ot%d_%d" % (j, tg)])
                otn = ["ot%d_%d" % (j, t) for t in range(4)]
                if fc < 52:
                    P.dma(lambda e, fc=fc, j=j: e.dma_start(out=OWNF[fc * 128:(fc + 1) * 128, :], in_=otA[j]), reads=otn, writes=["OWNS"], q="act")
                else:
                    P.dma(lambda e, fc=fc, j=j: e.dma_start(out=GSF[fc - 52], in_=otA[j]), reads=otn, writes=gsn(), q="act")
                if fc < 4:
                    P.dma(lambda e, fc=fc, j=j: e.dma_start(out=GSA[:, fc, :, :], in_=otA[j].rearrange("p (i t) -> p i t", t=128)[:, :, 112:128]),
                          reads=otn, writes=gsn(), q="act")

            def load_bv():
                bv4 = bv1f[:, :].rearrange("p (i r d) -> p i r d", r=4, d=65)
                for r in range(4):
                    for k in range(4):
                        P.dma(lambda e, r=r, k=k: e.dma_start(out=bv4[:, 4 * k:4 * k + 4, r, 0:64],
                                                              in_=GDTk[r][k][:, 0:64].rearrange("(i p) d -> p i d", p=128)),
                              reads=["GD"], writes=["bv1"])
            if EARLY_BV:
                load_bv()
            return load_bv

        def phase_B(l, load_bv):
            if not EARLY_BV:
                load_bv()
            POOLW, PVEC, W1K, W1V, W2, POS, LNV = POOLW_[l], PVEC_[l], W1K_[l], W1V_[l], W2_[l], POS_[l], LNV_[l]
            W16 = W16A[l]
            XT = XT0 if l == 0 else XS
            YOUT = YT if l == DEPTH - 1 else XS
            KC, VC = None, None
            kk4 = kk[:, :].rearrange("p (i r t) -> p i r t", r=4, t=128)
            ks4 = ks[:, :].rearrange("p (i r t) -> p i r t", r=4, t=128)
            for r in range(4):
                P.dma(lambda e, r=r: e.dma_start(out=kk4[:, :, r, :], in_=GDF[r][0].rearrange("p (i t) -> p i t", t=128)), reads=["GD"], writes=["kk"])
                P.dma(lambda e, r=r: e.dma_start(out=ks4[:, :, r, :], in_=GDF[r][1].rearrange("p (i t) -> p i t", t=128)), reads=["GD"], writes=["ks"])
            P.op("pool", lambda e: e.memset(vs1[:], 1.0), writes=["vs1"])
            P.op("pool", lambda e: e.memset(vw1[:], 1.0), writes=["vw1"])
            P.op("pool", lambda e: e.memset(vcmp1[:], 1.0), writes=["vcmp1"])
            P.op("pool", lambda e: e.memset(hT[:], 0.0), writes=["hT"])
            P.op("pool", lambda e: e.memset(kcmp[:], 0.0), writes=["kcmp"])
            bv4 = bv1f[:, :].rearrange("p (i r d) -> p i r d", r=4, d=65)
            vs4 = vs1f[:, :].rearrange("p (i r g d) -> p i r g d", r=4, g=2, d=65)
            for r in range(4):
                for k in range(4):
                    for g in range(2):
                        P.dma(lambda e, r=r, g=g, k=k: e.dma_start(out=vs4[:, 4 * k:4 * k + 4, r, g, 0:64],
                                                                   in_=GDTk[r][k][:, 64 + 64 * g:128 + 64 * g].rearrange("(i p) d -> p i d", p=128)),
                              reads=["GD"], writes=["vs1"])
            P.dma(lambda e: e.dma_start(out=pvec[:], in_=PVEC), writes=["pvec"])
            P.dma(lambda e: e.dma_start(out=lnv[:], in_=LNV), writes=["lnv"])
            P.dma(lambda e: e.dma_start(out=post[:], in_=POS), writes=["post"])
            P.op("dve", lambda e: e.tensor_copy(posb[:], post[:]), reads=["post"], writes=["posb"])
            P.op("dve", lambda e: e.tensor_tensor(out=pbs[:], in0=pvec[:, 0:4], in1=pvec[:, 4:8], op=ALU.mult),
                 reads=["pvec"], writes=["pbs"])
            st32 = scr[:, 0:4096].rearrange("p (l e) -> p l e", e=128)
            P.dma(lambda e: e.dma_start(out=st32[:, 0:4, :], in_=POOLW), writes=["scr"])
            P.dma(lambda e: e.dma_start(out=st32[:, 4:6, :], in_=W2), writes=["scr"])
            P.op("dve", lambda e: e.tensor_copy(poolw[:], st32[:, 0:4, :]), reads=["scr"], writes=["poolw"])
            P.op("dve", lambda e: e.tensor_copy(w2b[:], st32[:, 4:6, :]), reads=["scr"], writes=["w2b"])

            w1b = wrot[0][:].rearrange("p a b -> p (a b)")[:, 0:4096].rearrange("p (l e) -> p l e", e=128)
            for kv, (W1, SRC) in enumerate(((W1K, KC), (W1V, VC))):
                P.dma(lambda e, W1=W1: e.dma_start(out=st32[:, :, :], in_=W1), writes=["scr"])
                P.op("dve", lambda e: e.tensor_copy(w1b, st32[:, :, :]), reads=["scr"], writes=["wrot0"])
                src16 = scr16[:, 8192:16384]
                src4 = src16.rearrange("p (i r t) -> p i r t", r=4, t=128)
                for r in range(4):
                    P.dma(lambda e, r=r, kv=kv: e.dma_start(out=src4[:, :, r, :], in_=GDF[r][3 + kv].rearrange("p (i t) -> p i t", t=128)),
                          reads=["GD"], writes=["scrhi"])
                for l in range(32):
                    P.op("pe", mm(psI[0][:, 0:1], w1b[:, l, :], posb[:, kv, l:l + 1], l == 0, l == 31),
                         reads=["wrot0", "posb"], writes=["psI0"])
                P.op("dve", lambda e, kv=kv: e.tensor_copy(cbias[:, kv:kv + 1], psI[0][:, 0:1]), reads=["psI0"], writes=["cbias"])
                for l in range(32):
                    rhs = src16[:, l:min(l + 16 * 511, 8192):16]
                    P.op("pe", mm(psI[1][:, 0:511], w1b[:, l, :], rhs, l == 0, l == 31),
                         reads=["wrot0", "scrhi"], writes=["psI1"])
                P.op("act", lambda e, kv=kv: e.activation(hT[:, kv, 0:511], psI[1][:, 0:511], AF.Silu, bias=cbias[:, kv:kv + 1], scale=1.0),
                     reads=["psI1", "cbias"], writes=["hT"])
                if kv == 0:
                    P.op("pe", mm(psI[0][:, 0:511], w2b[:, 0, :], hT[:, 0, 0:511], True, True), reads=["w2b", "hT"], writes=["psI0"])
                    P.op("dve", lambda e: e.tensor_copy(kcmp[:, 0:511], psI[0][:, 0:511]), reads=["psI0"], writes=["kcmp"])
                else:
                    for c in range(4):
                        P.op("pe", mm(psI[0][:, c * 128:(c + 1) * 128], hT[:, 1, c * 128:(c + 1) * 128], w2b[:, 1, :], True, True),
                             reads=["w2b", "hT"], writes=["psI0"])
                    P.op("dve", lambda e: e.tensor_copy(vcmp1[:, :, :, 0:64],
                                                        psI[0][:, :].rearrange("p (c g d) -> p c g d", c=4, g=2)),
                         reads=["psI0"], writes=["vcmp1"])

            OWNV = OWNF.rearrange("(c p) t -> p c t", p=128)
            XTV = XT.rearrange("(c p) t -> p c t", p=128)
            YTV = YOUT.rearrange("(c p) t -> p c t", p=128)
            W16V = W16.rearrange("(w c p) d -> w p c d", p=128, c=4)


            def attention(tag, i, chunks, kq, vfun, maskfun, kreads, between=None):
                n = len(chunks)

                def qk(ci):
                    c = chunks[ci]
                    j = ci % 2
                    pS = psS[j]
                    pn = "psS%d" % j
                    for half in range(2):
                        ml = maskfun(c, half)
                        for mi, (mlhs, mreads) in enumerate(ml):
                            P.op("pe", mm(pS[:, half * 512:(half + 1) * 512], mlhs, C["c_idbig"][:, :], mi == 0, False),
                                 reads=list(mreads) + ["c_idbig"], writes=[pn])
                        kl, qr = kq(c, half)
                        P.op("pe", mm(pS[:, half * 512:(half + 1) * 512], kl, qr, len(ml) == 0, True),
                             reads=list(kreads) + ["ownB"], writes=[pn])
                    P.op("act", lambda e, pS=pS, j=j: e.activation(pT[j][:, :], pS[:, :], AF.Exp, scale=0.125),
                         reads=[pn], writes=["pT%d" % j])

                def pv(ci):
                    c = chunks[ci]
                    j = ci % 2
                    for hh in range(8):
                        P.op("pe", mm(psO[:, hh // 4, (hh % 4) * 65:(hh % 4) * 65 + 65], pT[j][:, hh * 128:(hh + 1) * 128], vfun(c, hh),
                                      ci == 0 and hh % 4 == 0, ci == n - 1),
                             reads=["pT%d" % j, "bv1", "vs1", "vw1", "vcmp1"], writes=["psO"])

                qk(0)
                for ci in range(n):
                    if ci + 1 < n:
                        qk(ci + 1)
                    pv(ci)
                    if between is not None:
                        between(ci)

            def rden_of(col0):
                for hb in range(2):
                    P.op("dve", lambda e, hb=hb: e.tensor_scalar(out=rd[:, col0 + 4 * hb:col0 + 4 * hb + 4],
                                                                 in0=psO[:, hb, 0:260].rearrange("p (h d) -> p h d", d=65)[:, :, 64],
                                                                 scalar1=1e-30, scalar2=None, op0=ALU.max),
                         reads=["psO"], writes=["rd"])
                P.op("dve", lambda e: e.reciprocal(rd[:, col0:col0 + 8], rd[:, col0:col0 + 8]), reads=["rd"], writes=["rd"])

            def wmul(dst, wcol0, dname):
                for hb in range(2):
                    P.op("dve", lambda e, hb=hb: e.tensor_tensor(
                        out=dst[:, hb * 256:(hb + 1) * 256].rearrange("p (h d) -> p h d", d=64),
                        in0=psO[:, hb, 0:260].rearrange("p (h d) -> p h d", d=65)[:, :, 0:64],
                        in1=rd[:, wcol0 + 4 * hb:wcol0 + 4 * hb + 4].unsqueeze(2).to_broadcast([128, 4, 64]),
                        op=ALU.mult), reads=["psO", "rd"], writes=[dname])

            def to_featmajor(src, dstT, gate0, dname):
                pst = psI[0][:, 0:256].bitcast(BF16)
                for c in range(4):
                    P.op("pe", lambda e, c=c: e.transpose(pst[:, c * 128:(c + 1) * 128], src[:, c * 128:(c + 1) * 128], C["c_ident"][:, :]),
                         reads=["ytm", "c_ident"], writes=["psI0"])
                P.op("dve", lambda e: e.tensor_tensor(out=dstT[:].rearrange("p c t -> p (c t)"), in0=pst,
                                                      in1=gat[:, gate0:gate0 + 4, :].rearrange("p c t -> p (c t)"), op=ALU.mult),
                     reads=["psI0", "gat"], writes=[dname])

            def tail(tsl, last):
                P.dma(lambda e, tsl=tsl: e.dma_start(out=xt[:], in_=XTV[:, :, tsl]), reads=["XS"], writes=["xt"])
                if debug:
                    for bi, (yy, ynm) in enumerate(((yaT, "yaT"), (ybT, "ybT"), (ycT, "ycT"))):
                        P.dma(lambda e, bi=bi, yy=yy, tsl=tsl: e.dma_start(out=DBG[bi, :, :, tsl], in_=yy[:]), reads=[ynm])
                ys = (yaT, ybT, ycT)
                yn = ("yaT", "ybT", "ycT")
                for br in range(3):
                    wt = wrot[br][:] if br < 2 else wrot2
                    wn = ("wrot%d" % br) if br < 2 else "junk"
                    pM = psS[br % 2]
                    pmn = "psS%d" % (br % 2)
                    for dc in range(8):
                        for ec in range(4):
                            P.op("pe", mm(pM[:, dc * 128:(dc + 1) * 128], wt[:, ec, dc * 128:(dc + 1) * 128], ys[br][:, ec, :], ec == 0, ec == 3),
                                 reads=[wn, yn[br]], writes=[pmn])
                    if br == 0:
                        P.op("dve", lambda e, pM=pM: e.tensor_tensor(out=mT[:].rearrange("p c t -> p (c t)"), in0=pM[:, :],
                                                                     in1=sgm[:, 0:8, :].rearrange("p c t -> p (c t)"), op=ALU.mult),
                             reads=[pmn, "sgm"], writes=["mT"])
                    else:
                        P.op("dve", lambda e, pM=pM, br=br: e.tensor_tensor(out=zc[:].rearrange("p c t -> p (c t)"), in0=pM[:, :],
                                                                            in1=sgm[:, 8 * br:8 * br + 8, :].rearrange("p c t -> p (c t)"), op=ALU.mult),
                             reads=[pmn, "sgm"], writes=["zc"])
                        P.op("pool", lambda e: e.tensor_tensor(out=mT[:], in0=mT[:], in1=zc[:], op=ALU.add), reads=["mT", "zc"], writes=["mT"])
                P.op("dve", lambda e: e.tensor_copy(mTb[:], mT[:]), reads=["mT"], writes=["mTb"])
                pOut = psS[1]
                for hf in range(2):
                    wt = wrot[(3 + hf) % 2]
                    wn = "wrot%d" % ((3 + hf) % 2)
                    P.dma(lambda e, wt=wt, hf=hf: e.dma_start(out=wt[:], in_=W16V[3 + hf]), writes=[wn])
                for hf in range(2):
                    wt = wrot[1] if hf == 0 else wrot[0]
                    wn = "wrot1" if hf == 0 else "wrot0"
                    for ec in range(8):
                        for dc in range(4 * hf, 4 * hf + 4):
                            P.op("pe", mm(pOut[:, ec * 128:(ec + 1) * 128], wt[:, dc % 4, ec * 128:(ec + 1) * 128], mTb[:, dc, :],
                                          hf == 0 and dc == 0 and ec % 4 == 0, hf == 1 and dc == 7),
                                 reads=[wn, "mTb"], writes=["psS1"])
                P.op("dve", lambda e: e.scalar_tensor_tensor(out=zt[:].rearrange("p c t -> p (c t)"), in0=xt[:].rearrange("p c t -> p (c t)"),
                                                             scalar=float(ALPHA), in1=pOut[:, :], op0=ALU.mult, op1=ALU.add),
                     reads=["xt", "psS1"], writes=["zt"])
                for dc in range(8):
                    P.op("pe", mm(psI[0][:, 0:128], C["c_onesm"][:, :], zt[:, dc, :], dc == 0, dc == 7), reads=["c_onesm", "zt"], writes=["psI0"])
                P.op("dve", lambda e: e.tensor_tensor(out=zc[:], in0=zt[:], in1=psI[0][:, 0:128].unsqueeze(1).to_broadcast([128, 8, 128]), op=ALU.subtract),
                     reads=["zt", "psI0"], writes=["zc"])
                P.op("act", lambda e: e.activation(zt[:], zc[:], AF.Square), reads=["zc"], writes=["zt"])
                for dc in range(8):
                    P.op("pe", mm(psI[1][:, 0:128], C["c_onesm"][:, :], zt[:, dc, :], dc == 0, dc == 7), reads=["c_onesm", "zt"], writes=["psI1"])
                P.op("dve", lambda e: e.tensor_scalar(out=rstd[:, :], in0=psI[1][:, 0:128], scalar1=float(LN_EPS), scalar2=None, op0=ALU.add),
                     reads=["psI1"], writes=["rstd"])
                P.op("act", lambda e: e.activation(rstd[:, :], rstd[:, :], AF.Sqrt), reads=["rstd"], writes=["rstd"])
                P.op("dve", lambda e: e.reciprocal(rstd[:, :], rstd[:, :]), reads=["rstd"], writes=["rstd"])
                P.op("dve", lambda e: e.tensor_tensor(out=zc[:], in0=zc[:], in1=rstd[:, :].unsqueeze(1).to_broadcast([128, 8, 128]), op=ALU.mult),
                     reads=["zc", "rstd"], writes=["zc"])
                for dc in range(8):
                    P.op("dve" if dc % 2 else "pool", lambda e, dc=dc: e.tensor_scalar(out=zt[:, dc, :], in0=zc[:, dc, :], scalar1=lnv[:, dc:dc + 1], scalar2=lnv[:, 8 + dc:9 + dc],
                                                          op0=ALU.mult, op1=ALU.add), reads=["zc", "lnv"], writes=["zt"])
                P.dma(lambda e, tsl=tsl: e.dma_start(out=YTV[:, :, tsl], in_=zt[:]), reads=["zt"], writes=["XS"])
                if l < DEPTH - 1 and USE_XB:
                    P.op("pool", lambda e: e.tensor_copy(mTb[:], zt[:]), reads=["zt"], writes=["mTb"])
                    P.dma(lambda e, tsl=tsl: e.dma_start(out=XB.rearrange("(c p) t -> p c t", p=128)[:, :, tsl], in_=mTb[:]), reads=["mTb"], writes=["XB"])
                if not last:
                    for br in range(2):
                        P.dma(lambda e, br=br: e.dma_start(out=wrot[br][:], in_=W16V[br]), writes=["wrot%d" % br])

            for br in range(2):
                P.dma(lambda e, br=br: e.dma_start(out=wrot[br][:], in_=W16V[br]), writes=["wrot%d" % br])
            for i in range(NBLK):
                NK = 4 * i + 4
                tsl = slice(i * 128, (i + 1) * 128)
                if i == 0:
                    P.dma(lambda e, tsl=tsl: e.dma_start(out=own[:, 0:40, :], in_=OWNV[:, 0:40, tsl]), reads=["OWNS"], writes=["ownA"])
                    P.dma(lambda e, tsl=tsl: e.dma_start(out=own[:, 40:52, :], in_=OWNV[:, 40:52, tsl]), reads=["OWNS"], writes=["ownB"])
                j0 = 4 if i == 0 else 0
                kc0 = 4 * i - 4 + j0
                for jj in range(j0, 8):
                    rr, li = jj % 4, i - 1 + jj // 4
                    P.dma(lambda e, jj=jj, rr=rr, li=li: e.dma_start(out=kwb[:, jj, :], in_=GDF[rr][2][:, li * 128:(li + 1) * 128]),
                          reads=["GD"], writes=["kwb"])
                    P.dma(lambda e, jj=jj, rr=rr, li=li: e.dma_start(out=vw1[:, jj, :, 0:64],
                                                                     in_=GDTk[rr][li // 4][(li % 4) * 128:(li % 4 + 1) * 128, 192:320].rearrange("p (g d) -> p g d", g=2)),
                          reads=["GD"], writes=["vw1"])
                for k in range(4):
                    li = i if k < 3 else i - 1
                    if li < 0:
                        continue
                    P.dma(lambda e, k=k, li=li: e.dma_start(out=cand[:, k, :, :], in_=GDA[k][:, :, li, :]), reads=["GD"], writes=["cand"])
                for seg in range(i + 1):
                    last = seg == i
                    for h in range(8):
                        jj = (seg * 8 + h) % 2
                        P.op("pe", mm(psI[jj][:, :], own[64:96, 40 + h, :], kk[64:96, seg * 512:(seg + 1) * 512], True, True),
                             reads=["ownB", "kk"], writes=["psI%d" % jj])
                        P.op("act", lambda e, jj=jj: e.activation(relu[jj][:, :], psI[jj][:, :], AF.Relu),
                             reads=["psI%d" % jj], writes=["relu%d" % jj])
                        scs = scr[:, seg * 512:(seg + 1) * 512]
                        if h == 0 and last:
                            P.op("dve", lambda e, jj=jj, scs=scs, i=i: e.scalar_tensor_tensor(out=scs, in0=relu[jj][:, :], scalar=iwcg[:, i, 0:1],
                                                                                             in1=C["c_causbig"][:, :], op0=ALU.mult, op1=ALU.add),
                                 reads=["relu%d" % jj, "iwcg", "c_causbig"], writes=["scr"])
                        elif h == 0:
                            P.op("dve", lambda e, jj=jj, scs=scs, i=i: e.tensor_scalar(out=scs, in0=relu[jj][:, :], scalar1=iwcg[:, i, 0:1], scalar2=None,
                                                                                      op0=ALU.mult),
                                 reads=["relu%d" % jj, "iwcg"], writes=["scr"])
                        else:
                            P.op("dve", lambda e, jj=jj, scs=scs, i=i, h=h: e.scalar_tensor_tensor(out=scs, in0=relu[jj][:, :], scalar=iwcg[:, i, h:h + 1],
                                                                                                  in1=scs, op0=ALU.mult, op1=ALU.add),
                                 reads=["relu%d" % jj, "iwcg", "scr"], writes=["scr"])
                NC = NK * 128
                P.op("dve", lambda e, NC=NC: e.tensor_reduce(out=sm[:, 0:1], in_=scr[:, 0:NC], axis=AX.X, op=ALU.max), reads=["scr"], writes=["smb"])
                P.op("dve", lambda e, NC=NC: e.scalar_tensor_tensor(out=relu[0][:, :], in0=C["c_causbig"][:, :], scalar=-2.0,
                                                                    in1=scr[:, NC - 512:NC], op0=ALU.mult, op1=ALU.add),
                     reads=["scr", "c_causbig", "relu0"], writes=["relu0"])
                P.op("dve", lambda e: e.tensor_reduce(out=sm[:, 1:2], in_=relu[0][:, :], axis=AX.X, op=ALU.min), reads=["relu0"], writes=["smb"])
                if i > 0:
                    P.op("dve", lambda e, NC=NC: e.tensor_reduce(out=sm[:, 7:8], in_=scr[:, 0:NC - 512], axis=AX.X, op=ALU.min), reads=["scr"], writes=["smb"])
                    P.op("dve", lambda e: e.tensor_tensor(out=sm[:, 1:2], in0=sm[:, 1:2], in1=sm[:, 7:8], op=ALU.min), reads=["smb"], writes=["smb"])
                P.op("dve", lambda e: e.tensor_scalar(out=sm[:, 3:4], in0=sm[:, 1:2], scalar1=-1e-3, scalar2=None, op0=ALU.add), reads=["smb"], writes=["smb"])
                P.op("dve", lambda e: e.scalar_tensor_tensor(out=sm[:, 2:3], in0=sm[:, 0:1], scalar=2e-3, in1=sm[:, 3:4], op0=ALU.add, op1=ALU.subtract),
                     reads=["smb"], writes=["smb"])
                P.op("dve", lambda e: e.tensor_scalar(out=wkall[:, :], in0=C["c_bisw"][:, :], scalar1=sm[:, 2:3], scalar2=None, op0=ALU.mult),
                     reads=["smb", "c_bisw"], writes=["wkall"])
                bis_state = {"it": 0}

                def bis_iter(NC=NC):
                    it = bis_state["it"]
                    if it >= NBIS:
                        return
                    bis_state["it"] = it + 1
                    P.op("dve", lambda e, it=it: e.tensor_tensor(out=sm[:, 4:5], in0=sm[:, 3:4], in1=wkall[:, it:it + 1], op=ALU.add),
                         reads=["smb", "wkall"], writes=["smb"])
                    P.op("dve", lambda e, NC=NC: e.tensor_scalar(out=junk[:, 0:NC], in0=scr[:, 0:NC], scalar1=sm[:, 4:5], scalar2=None,
                                                                 op0=ALU.is_ge, op1=ALU.add, accum_out=sm[:, 5:6]),
                         reads=["scr", "smb"], writes=["junk", "smb"])
                    P.op("dve", lambda e, it=it: e.scalar_tensor_tensor(out=sm[:, 6:7], in0=sm[:, 5:6], scalar=float(TOPK) - 0.5, in1=wkall[:, it:it + 1],
                                                                        op0=ALU.is_ge, op1=ALU.mult), reads=["smb", "wkall"], writes=["smb"])
                    P.op("dve", lambda e: e.tensor_tensor(out=sm[:, 3:4], in0=sm[:, 3:4], in1=sm[:, 6:7], op=ALU.add), reads=["smb"], writes=["smb"])

                def bis_between(ci, per=(2 if NK < 12 else 1)):
                    for _ in range(per):
                        bis_iter()

                kl = [0, 1, 2] + ([3] if i > 0 else [])
                for kn, k in enumerate(kl):
                    if kn == 0:
                        P.op("dve", lambda e, k=k: e.tensor_scalar(out=axp[:, :, 0:16], in0=cand[:, k, :, :], scalar1=C["c_sel"][:, k:k + 1], scalar2=None,
                                                                   op0=ALU.mult), reads=["cand", "c_sel"], writes=["axp"])
                    else:
                        P.op("dve", lambda e, k=k: e.scalar_tensor_tensor(out=axp[:, :, 0:16], in0=cand[:, k, :, :], scalar=C["c_sel"][:, k:k + 1],
                                                                          in1=axp[:, :, 0:16], op0=ALU.mult, op1=ALU.add),
                             reads=["cand", "c_sel", "axp"], writes=["axp"])
                P.op("pool", lambda e: e.tensor_copy(axp[:, :, 16:144], own[:, 0:4, :]), reads=["ownA"], writes=["axp"])
                P.op("pool", lambda e: e.tensor_tensor(out=axq[:, 0:4, 1:144], in0=axp[:, 0:4, 1:144], in1=axp[:, 0:4, 0:143], op=ALU.add),
                     reads=["axp"], writes=["axq"])
                P.op("pool", lambda e: e.tensor_tensor(out=axr[:, 1:4, 3:144], in0=axq[:, 1:4, 3:144], in1=axq[:, 1:4, 1:142], op=ALU.add),
                     reads=["axq"], writes=["axr"])
                P.op("pool", lambda e: e.tensor_tensor(out=axq[:, 2:4, 7:144], in0=axr[:, 2:4, 7:144], in1=axr[:, 2:4, 3:140], op=ALU.add),
                     reads=["axr"], writes=["axq"])
                P.op("pool", lambda e: e.tensor_tensor(out=axr[:, 3:4, 15:144], in0=axq[:, 3:4, 15:144], in1=axq[:, 3:4, 7:136], op=ALU.add),
                     reads=["axq"], writes=["axr"])
                cur = {0: axq, 1: axr, 2: axq, 3: axr}
                for g, w in enumerate((2, 4, 8, 16)):
                    srcg = cur[g]
                    if i == 0:
                        P.op("dve", lambda e, g=g, srcg=srcg: e.tensor_tensor(out=otmp[:, 0:128], in0=srcg[:, g, 16:144],
                                                                              in1=C["c_invw"][:, g, :], op=ALU.mult),
                             reads=["axq", "axr", "c_invw"], writes=["otmp"])
                        P.op("dve", lambda e, g=g: e.tensor_tensor(out=pooled[:, g, :], in0=otmp[:, 0:128], in1=own[:, g, :], op=ALU.subtract),
                             reads=["otmp", "ownA"], writes=["pooled"])
                    else:
                        P.op("dve", lambda e, g=g, w=w, srcg=srcg: e.scalar_tensor_tensor(out=pooled[:, g, :], in0=srcg[:, g, 16:144], scalar=1.0 / w,
                                                                                          in1=own[:, g, :], op0=ALU.mult, op1=ALU.subtract),
                             reads=["axq", "axr", "ownA"], writes=["pooled"])
                if i > 0:
                    tail(slice((i - 1) * 128, i * 128), False)
                P.op("act", lambda e: e.activation(gat[:], own[:, 4:16, :], AF.Silu), reads=["ownA"], writes=["gat"])
                P.op("act", lambda e: e.activation(sgm[:], own[:, 16:40, :], AF.Sigmoid), reads=["ownA"], writes=["sgm"])
                P.op("act", lambda e, i=i: e.activation(rd[:, 0:24], iwcg[:, i, 8:32], AF.Sigmoid), reads=["iwcg"], writes=["rd"])

                for g in range(4):
                    P.op("pe", mm(psI[1][:, g * 128:(g + 1) * 128], poolw[:, g, :], pooled[:, g, :], True, True),
                         reads=["poolw", "pooled"], writes=["psI1"])
                for g in range(4):
                    P.op("act", lambda e, g=g: e.activation(otmp[:, g * 128:(g + 1) * 128], psI[1][:, g * 128:(g + 1) * 128], AF.Identity,
                                                            bias=pbs[:, g:g + 1], scale=pvec[:, 4 + g:5 + g]),
                         reads=["psI1", "pbs", "pvec"], writes=["otmp"])
                P.op("dve", lambda e: e.tensor_tensor(out=yaT[:].rearrange("p c t -> p (c t)"), in0=otmp[:, :],
                                                      in1=gat[:, 0:4, :].rearrange("p c t -> p (c t)"), op=ALU.mult),
                     reads=["otmp", "gat"], writes=["yaT"])

                if i + 1 < NBLK:
                    tn = slice((i + 1) * 128, (i + 2) * 128)
                    P.dma(lambda e, tn=tn: e.dma_start(out=own[:, 0:40, :], in_=OWNV[:, 0:40, tn]), reads=["OWNS"], writes=["ownA"])
                NCc = (32 * i + 30) // 128 + 1
                kq = lambda hh: own[64 * (hh // 4):64 * (hh // 4) + 64, 48 + (hh % 4), :]

                def cmp_mask(c, half, i=i):
                    o = 512 - 32 * i + 128 * c
                    return [(C["c_patc"][:, o:o + 128], ["c_patc"])]

                for ci in range(NCc):
                    c = ci
                    j = ci % 2
                    pS = psS[j]
                    pn = "psS%d" % j
                    for half in range(2):
                        (mlhs, mreads), = cmp_mask(c, half)
                        P.op("pe", mm(pS[:, half * 512:(half + 1) * 512], mlhs, C["c_idbig"][:, :], True, False),
                             reads=["c_patc", "c_idbig"], writes=[pn])
                        P.op("pe", mm(pS[:, half * 512:(half + 1) * 512], kcmp[64 * half:64 * half + 64, c * 128:(c + 1) * 128],
                                      own[64 * half:64 * half + 64, 48:52, :], False, True),
                             reads=["kcmp", "ownB"], writes=[pn])
                    P.op("act", lambda e, pS=pS, j=j: e.activation(pT[j][:, :], pS[:, :], AF.Exp, scale=0.125), reads=[pn], writes=["pT%d" % j])
                    for hh in range(8):
                        P.op("pe", mm(psO[:, hh // 4, (hh % 4) * 65:(hh % 4) * 65 + 65], pT[j][:, hh * 128:(hh + 1) * 128], vcmp1[:, c, hh // 4, :],
                                      ci == 0 and hh % 4 == 0, ci == NCc - 1), reads=["pT%d" % j, "vcmp1"], writes=["psO"])
                        P.op("pe", mm(psI[hh // 4][:, (hh % 4) * 128:(hh % 4) * 128 + 128], pT[j][:, hh * 128:(hh + 1) * 128], C["c_ovl"][:, c, :],
                                      ci == 0 and hh % 4 == 0, ci == NCc - 1), reads=["pT%d" % j, "c_ovl"], writes=["psI%d" % (hh // 4)])
                rden_of(32)
                sig3 = rd[:, 0:24].rearrange("p (h b) -> p h b", b=3)
                P.op("dve", lambda e: e.tensor_tensor(out=sm[:, 8:16], in0=sig3[:, :, 0], in1=rd[:, 32:40], op=ALU.mult), reads=["rd"], writes=["sm"])
                for hb in range(2):
                    P.op("dve", lambda e, hb=hb: e.tensor_tensor(
                        out=oc[:, hb * 256:(hb + 1) * 256].rearrange("p (h d) -> p h d", d=64),
                        in0=psO[:, hb, 0:260].rearrange("p (h d) -> p h d", d=65)[:, :, 0:64],
                        in1=sm[:, 8 + 4 * hb:12 + 4 * hb].unsqueeze(2).to_broadcast([128, 4, 64]), op=ALU.mult),
                        reads=["psO", "sm"], writes=["oc"])
                fo = 128 - 8 * i
                for g in range(2):
                    for r_ in range(4):
                        if r_ == 0:
                            P.op("dve", lambda e, g=g, fo=fo: e.scalar_tensor_tensor(out=imp[:, g, :], in0=psI[g][:, 0:128], scalar=rd[:, 32 + 4 * g:33 + 4 * g],
                                                                                     in1=C["c_fp"][:, fo:fo + 128], op0=ALU.mult, op1=ALU.add),
                                 reads=["psI%d" % g, "rd", "c_fp"], writes=["imp"])
                        else:
                            P.op("dve", lambda e, g=g, r_=r_: e.scalar_tensor_tensor(out=imp[:, g, :], in0=psI[g][:, r_ * 128:(r_ + 1) * 128],
                                                                                     scalar=rd[:, 32 + 4 * g + r_:33 + 4 * g + r_],
                                                                                     in1=imp[:, g, :], op0=ALU.mult, op1=ALU.add),
                                 reads=["psI%d" % g, "rd", "imp"], writes=["imp"])
                    P.op("dve", lambda e, g=g, i=i: e.tensor_tensor(out=imp[:, g, 0:1], in0=imp[:, g, 0:1], in1=C["c_f0"][:, i:i + 1], op=ALU.add),
                         reads=["imp", "c_f0"], writes=["imp"])
                    P.op("dve", lambda e, g=g: e.max(out=mx[:, 0:8], in_=imp[:, g, :]), reads=["imp"], writes=["mx"])
                    P.op("dve", lambda e, g=g: e.match_replace(out=impw[:, :], in_to_replace=mx[:, 0:8], in_values=imp[:, g, :], imm_value=-3e30),
                         reads=["imp", "mx"], writes=["impw"])
                    P.op("dve", lambda e: e.max(out=mx[:, 8:16], in_=impw[:, :]), reads=["impw"], writes=["mx"])
                    P.op("dve", lambda e, g=g: e.tensor_scalar(out=selm[:, g, :], in0=imp[:, g, :], scalar1=mx[:, 15:16], scalar2=-1.0,
                                                               op0=ALU.is_ge, op1=ALU.add), reads=["imp", "mx"], writes=["selm"])

                for _ in range(min(NBIS, int(20.0 / (NC / 960.0 + 0.6)))):
                    bis_iter()
                def win_mask(c, half, i=i):
                    return [(C["c_wm"][:, c, :], ["c_wm"])]

                attention("win", i, list(range(j0, 8)),
                          kq=lambda c, g: (kwb[64 * g:64 * g + 64, c, :], own[64 * g:64 * g + 64, 48:52, :]),
                          vfun=lambda c, hh: vw1[:, c, hh // 4, :],
                          maskfun=win_mask, kreads=["kwb"])
                rden_of(32)
                P.op("dve", lambda e: e.tensor_tensor(out=sm[:, 8:16], in0=sig3[:, :, 2], in1=rd[:, 32:40], op=ALU.mult), reads=["rd"], writes=["sm"])
                for hb in range(2):
                    P.op("dve", lambda e, hb=hb: e.tensor_tensor(
                        out=otmp[:, hb * 256:(hb + 1) * 256].rearrange("p (h d) -> p h d", d=64),
                        in0=psO[:, hb, 0:260].rearrange("p (h d) -> p h d", d=65)[:, :, 0:64],
                        in1=sm[:, 8 + 4 * hb:12 + 4 * hb].unsqueeze(2).to_broadcast([128, 4, 64]), op=ALU.mult),
                        reads=["psO", "sm"], writes=["otmp"])
                P.op("pool", lambda e: e.tensor_tensor(out=oc[:, :], in0=oc[:, :], in1=otmp[:, :], op=ALU.add), reads=["oc", "otmp"], writes=["oc"])
                def slc_mask(c, half, i=i):
                    g = half
                    m = mch[(2 * c + half) % 4]
                    mn = "mch%d" % ((2 * c + half) % 4)
                    P.op("pool", lambda e, m=m, c=c, g=g: e.tensor_copy(
                        m[:, :].rearrange("p (b s) -> p b s", s=64),
                        selm[:, g, 2 * c:2 * c + 2].unsqueeze(2).to_broadcast([128, 2, 64])), reads=["selm"], writes=[mn])
                    res = [(m[:, :], [mn])]
                    jl = c - 4 * i
                    if jl >= 0:
                        res.append((C["c_caus"][:, jl, :], ["c_caus"]))
                    return res

                attention("slc", i, list(range(NK)),
                          kq=lambda c, g: (ks[64 * g:64 * g + 64, c * 128:(c + 1) * 128], own[64 * g:64 * g + 64, 48:52, :]),
                          vfun=lambda c, hh: vs1[:, c, hh // 4, :],
                          maskfun=slc_mask, kreads=["ks"], between=bis_between)
                rden_of(32)
                P.op("dve", lambda e: e.tensor_tensor(out=sm[:, 8:16], in0=sig3[:, :, 1], in1=rd[:, 32:40], op=ALU.mult), reads=["rd"], writes=["sm"])
                for hb in range(2):
                    P.op("dve", lambda e, hb=hb: e.tensor_tensor(
                        out=otmp[:, hb * 256:(hb + 1) * 256].rearrange("p (h d) -> p h d", d=64),
                        in0=psO[:, hb, 0:260].rearrange("p (h d) -> p h d", d=65)[:, :, 0:64],
                        in1=sm[:, 8 + 4 * hb:12 + 4 * hb].unsqueeze(2).to_broadcast([128, 4, 64]), op=ALU.mult),
                        reads=["psO", "sm"], writes=["otmp"])
                P.op("pool", lambda e: e.tensor_tensor(out=oc[:, :], in0=oc[:, :], in1=otmp[:, :], op=ALU.add), reads=["oc", "otmp"], writes=["oc"])

                ocv = oc[:, :].rearrange("p (g j d) -> p g j d", g=2, j=4)
                P.op("dve", lambda e: e.tensor_copy(ytm[:, :].rearrange("p (j g d) -> p j g d", j=4, g=2),
                                                    oc[:, :].rearrange("p (g j d) -> p j g d", g=2, j=4)), reads=["oc"], writes=["ytm"])
                to_featmajor(ytm, ycT, 8, "ycT")

                while bis_state["it"] < NBIS:
                    bis_iter()
                P.dma(lambda e: e.dma_start(out=wrot2, in_=W16V[2]), writes=["junk"])

                def dsa_mask(c, half, i=i):
                    m = mch[(2 * c + half) % 4]
                    mn = "mch%d" % ((2 * c + half) % 4)
                    if half == 0:
                        P.op("dve", lambda e, m=m, c=c: e.tensor_scalar(out=m[:, :], in0=scr[:, c * 128:(c + 1) * 128], scalar1=sm[:, 3:4], scalar2=-1.0,
                                                                        op0=ALU.is_ge, op1=ALU.add), reads=["scr", "smb"], writes=[mn])
                        dsa_mask.cur = (m, mn)
                    m, mn = dsa_mask.cur
                    return [(m[:, :], [mn])]

                attention("dsa", i, list(range(NK)),
                          kq=lambda c, half: (kk[0:64, c * 128:(c + 1) * 128], own[0:64, 40 + 4 * half:44 + 4 * half, :]),
                          vfun=lambda c, hh: bv1[:, c, :],
                          maskfun=dsa_mask, kreads=["kk"])
                rden_of(24)
                wmul(ytm, 24, "ytm")
                to_featmajor(ytm, ybT, 4, "ybT")

                if i + 1 < NBLK:
                    tn = slice((i + 1) * 128, (i + 2) * 128)
                    P.dma(lambda e, tn=tn: e.dma_start(out=own[:, 40:52, :], in_=OWNV[:, 40:52, tn]), reads=["OWNS"], writes=["ownB"])
            tail(slice((NBLK - 1) * 128, NBLK * 128), True)

        for l in range(DEPTH):
            lb = phase_A(l)
            P.barrier()
            phase_B(l, lb)
            P.barrier()
        P.finish()
        P.emit(block, es)
    return nc


_CACHE = {}


def _get(name, fn):
    if name not in _CACHE:
        _CACHE[name] = fn()
    return _CACHE[name]


def _blockdiag_w1(w1):
    w = w1.reshape(32, 64, 64)
    o = np.zeros((128, 32, 128), np.float32)
    for g in range(2):
        o[64 * g:64 * g + 64, :, 64 * g:64 * g + 64] = w.transpose(1, 0, 2)
    return o


def kernel(x, w_in, b_in, pool_w, pool_b, pool_scale, cmp_pos_k, cmp_pos_v, cmp_w1_k, cmp_w2_k,
           cmp_w1_v, cmp_w2_v, w_proj_a, w_proj_b, w_proj_c, w_o, ln_g, ln_b):
    f = lambda a: np.ascontiguousarray(np.asarray(a, dtype=np.float32))
    x = f(x)
    nc = _get("F", build_F)
    toks = [core_tokens(c) for c in range(8)]
    WP, BP, WV, BT, WSRC, POOLW, PVEC, W1K, W1V, W2, POS, LNV = ([] for _ in range(12))
    for l in range(DEPTH):
        Wp, bpk, Wv, bt = pack_w(f(w_in[l]), f(b_in[l]))
        WP.append(Wp.reshape(8, 128, NFC, 128).transpose(2, 1, 0, 3).reshape(NFC, 128, 1024)); BP.append(bpk); WV.append(Wv); BT.append(bt)
        wpc_perm = f(w_proj_c[l])[_CPERM, :]
        WSRC.append(np.concatenate([f(w_proj_a[l]), f(w_proj_b[l]), wpc_perm, f(w_o[l])], axis=0))
        w2 = np.zeros((128, 2, 128), np.float32)
        for kv, w in enumerate((f(cmp_w2_k[l]), f(cmp_w2_v[l]))):
            for g in range(2):
                w2[64 * g:64 * g + 64, kv, 64 * g:64 * g + 64] = w
        pos = np.zeros((128, 2, 32), np.float32)
        for kv, p in enumerate((f(cmp_pos_k[l]), f(cmp_pos_v[l]))):
            pos[0:64, kv, :] = p.T
            pos[64:128, kv, :] = p.T
        W2.append(w2); POS.append(pos)
        POOLW.append(f(pool_w[l]).transpose(1, 0, 2))
        PVEC.append(np.concatenate([f(pool_b[l]).T, f(pool_scale[l]).reshape(4, 128).T], axis=1))
        W1K.append(_blockdiag_w1(f(cmp_w1_k[l]))); W1V.append(_blockdiag_w1(f(cmp_w1_v[l])))
        LNV.append(np.concatenate([f(ln_g[l]).reshape(8, 128).T, f(ln_b[l]).reshape(8, 128).T], axis=1))
    st = lambda lst: np.ascontiguousarray(np.stack(lst, axis=0).astype(np.float32))
    common = {"WP": st(WP), "BP": st(BP), "WV": st(WV), "BT": st(BT), "WSRC": st(WSRC), "POOLW": st(POOLW), "PVEC": st(PVEC),
              "W1K": st(W1K), "W1V": st(W1V), "W2": st(W2), "POS": st(POS), "LNV": st(LNV)}
    in_maps = []
    for c in range(8):
        b, idx = toks[c]
        m = {"XT0": np.ascontiguousarray(x[b, idx, :].T)}
        m.update(common)
        m.update(make_consts(c % 4))
        in_maps.append(m)
    res = run_bass_kernel_spmd(nc, in_maps, core_ids=list(range(8)))
    out = np.zeros((2, S, D), np.float32)
    for c in range(8):
        b, idx = toks[c]
        out[b, idx, :] = np.asarray(res.results[c]["YT"], dtype=np.float32).T
    return out
```

```python
import numpy as np
import ml_dtypes
from contextlib import ExitStack
import concourse.bass as bass
import concourse.mybir as mybir
from concourse.bass_utils import run_bass_kernel_spmd

F32 = mybir.dt.float32
BF16 = mybir.dt.bfloat16
ALU = mybir.AluOpType
AF = mybir.ActivationFunctionType
AX = mybir.AxisListType
NPBF = ml_dtypes.bfloat16

D = 1024
S = 8192
DEPTH = 4
NLOC = 2048
NBLK = 16
NFC = 57
NF = NFC * 128
NTM = 352
ALPHA = (2 * DEPTH) ** 0.25
LN_EPS = 1e-5
BIGM = 30000.0
NBIS = 16
USE_XB = False
EARLY_BV = True
TOPK = 256


class Prog:
    ENG = ("pe", "act", "dve", "pool", "sp")

    def __init__(self, nc, ndma=8):
        self.nc = nc
        self.lists = {e: [] for e in self.ENG}
        self.cnt = {e: 0 for e in self.ENG}
        self.seen = {e: {} for e in self.ENG}
        self.state = {}
        self.ndma = ndma
        self.dma_i = {"sp": 0, "act": 0}
        self.dma_val = {}
        self.phase = 0
        self.keys = set()
        self.ncc = 0

    def ek(self, eng):
        k = "%s@%d" % (eng, self.phase)
        self.keys.add(k)
        return k

    def _need(self, eng, tok, waits):
        if tok is None:
            return
        k, v = tok
        if self.seen[eng].get(k, 0) >= v:
            return
        if eng == "pe" and k.startswith("pe@"):
            return
        waits[k] = max(waits.get(k, 0), v)

    def _deps(self, eng, reads, writes):
        waits = {}
        for b in reads:
            st = self.state.get(b)
            if st:
                self._need(eng, st[0], waits)
        for b in writes:
            st = self.state.get(b)
            if st:
                self._need(eng, st[0], waits)
                for r in st[1]:
                    self._need(eng, r, waits)
        for k, v in waits.items():
            self.seen[eng][k] = v
        return list(waits.items())

    def _commit(self, tok, reads, writes):
        for b in reads:
            st = self.state.setdefault(b, [None, []])
            st[1].append(tok)
            if len(st[1]) > 16:
                mx = {}
                for k, v in st[1]:
                    mx[k] = max(mx.get(k, 0), v)
                st[1] = list(mx.items())
        for b in writes:
            self.state[b] = [tok, []]

    def op(self, eng, fn, reads=(), writes=()):
        waits = self._deps(eng, reads, writes)
        self.cnt[eng] += 1
        key = self.ek(eng)
        tok = (key, self.cnt[eng])
        self.lists[eng].append(("op", fn, waits, key))
        self._commit(tok, reads, writes)
        return tok

    def dma(self, fn, reads=(), writes=(), q="sp"):
        waits = self._deps(q, reads, writes)
        i = self.dma_i[q]
        self.dma_i[q] += 1
        key = "dma_%s_%d" % (q, i % self.ndma)
        self.keys.add(key)
        prev = self.dma_val.get(key, 0)
        if prev > 0 and self.seen[q].get(key, 0) < prev:
            waits.append((key, prev))
            self.seen[q][key] = prev
        val = prev + 16
        self.dma_val[key] = val
        tok = (key, val)
        self.lists[q].append(("dma", fn, waits, key))
        self._commit(tok, reads, writes)
        return tok

    def cc(self, fn, reads=(), writes=()):
        waits = self._deps("pool", reads, writes)
        self.ncc += 1
        self.keys.add("cc")
        tok = ("cc", self.ncc)
        self.lists["pool"].append(("op", fn, waits, "cc"))
        for b in reads:
            st = self.state.setdefault(b, [None, []])
            st[1].append(tok)
        for b in writes:
            st = self.state.get(b)
            if st and st[0] and st[0][0] == "cc":
                st[0] = tok
            else:
                self.state[b] = [tok, []]
        return tok

    def barrier(self):
        cur = {e: (self.ek(e), self.cnt[e]) for e in self.ENG}
        for e in self.ENG:
            waits = {}
            for f in self.ENG:
                if f == "sp" or cur[f][1] == 0:
                    continue
                if f == e and e == "pe":
                    continue
                self._need(e, cur[f], waits)
            for k, v in self.dma_val.items():
                self._need(e, (k, v), waits)
            if self.ncc:
                self._need(e, ("cc", self.ncc), waits)
            for k, v in waits.items():
                self.seen[e][k] = v
            self.lists[e].append(("wait", None, list(waits.items()), None))
        self.state = {}
        self.phase += 1
        self.cnt = {e: 0 for e in self.ENG}

    def finish(self, eng="sp"):
        waits = {}
        for k, v in self.dma_val.items():
            self._need(eng, (k, v), waits)
        self.lists[eng].append(("wait", None, list(waits.items()), None))

    def emit(self, block, es):
        nc = self.nc
        sems = {k: es.enter_context(nc.semaphore("s_" + k.replace("@", "_"))) for k in sorted(self.keys)}

        def run(lst):
            def body(e):
                for kind, fn, waits, key in lst:
                    for k, v in waits:
                        e.wait_ge(sems[k], v)
                    if kind == "wait":
                        continue
                    ins = fn(e)
                    ins.then_inc(sems[key], 1 if kind == "op" else 16)
            return body
        block.tensor(run(self.lists["pe"]))
        block.scalar(run(self.lists["act"]))
        block.vector(run(self.lists["dve"]))
        block.gpsimd(run(self.lists["pool"]))
        block.sync(run(self.lists["sp"]))


_IN_W = (512, 512, 512, 64, 64, 512, 256, 32, 8, 512, 128, 128, 128, 128, 128, 128, 24, 512, 3072)
_NAMES = ("a_x", "a_z", "b_q", "b_k", "b_v", "b_z", "i_q", "i_k", "i_w", "c_q", "c_kc", "c_vc", "c_ks", "c_vs",
          "c_kw", "c_vw", "c_g", "c_z", "g_merge")
_OFF = {}
_o = 0
for _n, _w in zip(_NAMES, _IN_W):
    _OFF[_n] = _o
    _o += _w

_CPERM = np.concatenate([np.concatenate([np.arange(64) + 64 * j, np.arange(64) + 64 * (4 + j)]) for j in range(4)])


def packed_cols():
    cols = []
    r = lambda n: list(range(_OFF[n], _OFF[n] + _IN_W[_NAMES.index(n)]))
    cols += r("a_x") + r("a_z") + r("b_z")
    cz = np.array(r("c_z"))
    cols += list(cz[_CPERM])
    cols += r("g_merge")
    bq = r("b_q"); iq = r("i_q")
    for h in range(8):
        cols += bq[64 * h:64 * h + 64] + iq[32 * h:32 * h + 32] + [-1] * 32
    cq = np.array(r("c_q"))
    cols += list(cq[_CPERM])
    cols += r("b_k") + r("i_k") + [-1] * 32
    cols += r("c_ks") + r("c_kw") + r("c_kc") + r("c_vc")
    assert len(cols) == NF, len(cols)
    tm = r("b_v") + r("c_vs") + r("c_vw") + r("i_w") + r("c_g")
    assert len(tm) == NTM
    return np.array(cols), np.array(tm)


_PCOLS, _TCOLS = packed_cols()


def pack_w(w_in_l, b_in_l):
    Wp = np.zeros((D, NF), np.float32)
    bpk = np.zeros((NF,), np.float32)
    m = _PCOLS >= 0
    Wp[:, m] = w_in_l[:, _PCOLS[m]]
    bpk[m] = b_in_l[_PCOLS[m]]
    Wv = np.ascontiguousarray(w_in_l[:, _TCOLS])
    bt = np.ascontiguousarray(np.tile(b_in_l[_TCOLS][None, :], (128, 1)))
    return Wp, np.ascontiguousarray(bpk.reshape(NFC, 128).T), Wv, bt


def core_tokens(c):
    r = c % 4
    idx = np.concatenate([np.arange(128) + 128 * (4 * i + r) for i in range(NBLK)])
    return c // 4, idx


def make_consts(r):
    t = np.arange(128)[:, None]
    s = np.arange(128)[None, :]
    caus = np.zeros((128, 4, 128), np.float32)
    for j in range(4):
        if j == r:
            caus[:, j, :] = np.where(s > t, -1.0, 0.0)
        elif j > r:
            caus[:, j, :] = -1.0
    causbig = (caus * 1e30).reshape(128, 512).astype(np.float32)
    idbig = np.tile(np.eye(128, dtype=np.float32) * BIGM, (1, 4))
    col = np.arange(1024)[None, :]
    patc = np.where(16 * (col - 512 - 8 * r) + 31 <= t, 0.0, -1.0).astype(np.float32)
    colf = np.arange(256)[None, :] - 128
    blk = 2 * r + (t >= 64)
    fp = np.zeros((128, 256), np.float32)
    fp = np.where((colf == blk) | (colf == blk - 1), 1e4, fp)
    fp = np.where(colf > blk, -1e30, fp).astype(np.float32)
    f0 = np.full((128, NBLK), 1e4, np.float32)
    if r == 0:
        f0[:, 0] = 0.0
    wm = np.zeros((128, 8, 128), np.float32)
    for jj in range(8):
        dl = 128 * (r + 4 - jj) + t - s
        wm[:, jj, :] = np.where((dl >= 0) & (dl <= 511), 0.0, -1.0)
    n = np.arange(512)[:, None]
    jn = np.arange(128)[None, :]
    ovl = ((n >= 4 * jn - 1) & (n <= 4 * jn + 3) & (n <= 510)).astype(np.float32)
    ovl = ovl.reshape(4, 128, 128).transpose(1, 0, 2)
    invw = np.zeros((128, 4, 128), np.float32)
    for g, w in enumerate((2, 4, 8, 16)):
        if r == 0:
            invw[:, g, :] = 1.0 / np.minimum(np.arange(128) + 1, w)[None, :]
        else:
            invw[:, g, :] = 1.0 / w
    bisw = np.tile((0.5 ** (np.arange(NBIS) + 1))[None, :], (128, 1)).astype(np.float32)
    return {
        "c_caus": caus.astype(NPBF), "c_causbig": causbig, "c_idbig": idbig.astype(NPBF),
        "c_patc": patc.astype(NPBF), "c_fp": fp, "c_f0": f0, "c_wm": wm.astype(NPBF),
        "c_ovl": np.ascontiguousarray(ovl).astype(NPBF), "c_invw": invw, "c_bisw": bisw,
        "c_ident": np.eye(128, dtype=np.float32).astype(NPBF),
        "c_onesm": np.full((128, 128), 1.0 / D, np.float32),
        "c_sel": np.tile(np.eye(4, dtype=np.float32)[(r - 1) % 4][None, :], (128, 1)),
    }


def build_F(debug=False):
    nc = bass.Bass("TRN2", target_bir_lowering=False)
    di = lambda n, s, d: nc.dram_tensor(n, s, d, kind="ExternalInput").ap()
    XT0 = di("XT0", [D, NLOC], F32)
    WP = di("WP", [DEPTH, NFC, 128, 8 * 128], F32)
    BP = di("BP", [DEPTH, 128, NFC], F32)
    WV = di("WV", [DEPTH, D, NTM], F32)
    BT = di("BT", [DEPTH, 128, NTM], F32)
    WSRC = di("WSRC", [DEPTH, 2560, D], F32)
    POOLW_ = di("POOLW", [DEPTH, 128, 4, 128], F32)
    PVEC_ = di("PVEC", [DEPTH, 128, 8], F32)
    W1K_ = di("W1K", [DEPTH, 128, 32, 128], F32)
    W1V_ = di("W1V", [DEPTH, 128, 32, 128], F32)
    W2_ = di("W2", [DEPTH, 128, 2, 128], F32)
    POS_ = di("POS", [DEPTH, 128, 2, 32], F32)
    LNV_ = di("LNV", [DEPTH, 128, 16], F32)
    cst = {
        "c_caus": di("c_caus", [128, 4, 128], BF16), "c_causbig": di("c_causbig", [128, 512], F32),
        "c_idbig": di("c_idbig", [128, 512], BF16), "c_patc": di("c_patc", [128, 1024], BF16),
        "c_fp": di("c_fp", [128, 256], F32), "c_f0": di("c_f0", [128, NBLK], F32),
        "c_wm": di("c_wm", [128, 8, 128], BF16), "c_ovl": di("c_ovl", [128, 4, 128], BF16),
        "c_invw": di("c_invw", [128, 4, 128], F32), "c_bisw": di("c_bisw", [128, NBIS], F32),
        "c_ident": di("c_ident", [128, 128], BF16), "c_onesm": di("c_onesm", [128, 128], F32),
        "c_sel": di("c_sel", [128, 4], F32),
    }
    YT = nc.dram_tensor("YT", [D, NLOC], F32, kind="ExternalOutput").ap()
    DBG = nc.dram_tensor("DBG", [3, 128, 4, NLOC], BF16, kind="ExternalOutput").ap() if debug else None
    OWNF = nc.dram_tensor("OWNS", [52 * 128, NLOC], BF16).ap()
    XS = nc.dram_tensor("XS", [D, NLOC], F32).ap()
    XB = nc.dram_tensor("XB", [D, NLOC], BF16).ap()
    W16A = nc.dram_tensor("W16A", [DEPTH, 2560, D], BF16).ap()
    NCH = 10
    GS = nc.dram_tensor("GS", [NCH * 256, 1024], BF16).ap()
    GD = nc.dram_tensor("GD", [NCH * 4 * 256, 1024], BF16).ap()
    flat = lambda ap: ap.rearrange("a c -> (a c)")
    gdr = lambda k, r: GD[(k * 4 + r) * 256:(k * 4 + r + 1) * 256, :]
    GSF = [GS[q * 256:(q + 1) * 256, :].rearrange("(p a) c -> p (a c)", a=2) for q in range(5)]
    GSTk = [GS[(5 + k) * 256:(6 + k) * 256, :].rearrange("a (h f) -> (a h) f", h=2) for k in range(4)]
    GSA = flat(GS[9 * 256:9 * 256 + 128, :]).rearrange("(c p i s) -> p c i s", c=4, p=128, i=NBLK)
    GDF = [[gdr(q, r).rearrange("(p a) c -> p (a c)", a=2) for q in range(5)] for r in range(4)]
    GDTk = [[gdr(5 + k, r).rearrange("a (h f) -> (a h) f", h=2) for k in range(4)] for r in range(4)]
    GDA = [flat(gdr(9, r)[0:128, :]).rearrange("(c p i s) -> p c i s", c=4, p=128, i=NBLK) for r in range(4)]
    with ExitStack() as es:
        sb = lambda n, s, d: es.enter_context(nc.sbuf_tensor(n, s, d))
        ps = lambda n, s, d: es.enter_context(nc.psum_tensor(n, s, d))
        kk = sb("kk", [128, S], BF16)
        kk32 = kk[:].bitcast(F32)
        ks = sb("ks", [128, S], BF16)
        ks32 = ks[:].bitcast(F32)
        bv1f = sb("bv1", [128, 64 * 65], BF16)
        bv1 = bv1f[:, :].rearrange("p (c d) -> p c d", d=65)
        vs1f = sb("vs1", [128, 64 * 2 * 65], BF16)
        vs1 = vs1f[:, :].rearrange("p (c g d) -> p c g d", g=2, d=65)
        kcmp = sb("kcmp", [128, 512], BF16)
        vcmp1 = sb("vcmp1", [128, 4, 2, 65], BF16)
        scr = sb("scr", [128, S], F32)
        scr16 = scr[:].bitcast(BF16)
        C = {k: sb("t_" + k, list(v.shape), v.dtype) for k, v in cst.items()}
        wrot = [sb("wrot%d" % j, [128, 4, D], BF16) for j in range(2)]
        poolw = sb("poolw", [128, 4, 128], BF16)
        w2b = sb("w2b", [128, 2, 128], BF16)
        pvec = sb("pvec", [128, 8], F32)
        pbs = sb("pbs", [128, 4], F32)
        lnv = sb("lnv", [128, 16], F32)
        post = sb("post", [128, 2, 32], F32)
        posb = sb("posb", [128, 2, 32], BF16)
        cbias = sb("cbias", [128, 2], F32)
        hT = sb("hT", [128, 2, 512], BF16)
        own = sb("own", [128, 52, 128], BF16)
        cand = sb("cand", [128, 4, 4, 16], BF16)
        bpt = sb("bpt", [128, NFC], F32)
        btm = sb("btm", [128, NTM], F32)
        tmo = [sb("tmo%d" % j, [128, 320], BF16) for j in range(2)]
        iwcg = sb("iwcg", [128, NBLK, 32], F32)
        xt = sb("xt", [128, 8, 128], F32)
        kwb = sb("kwb", [128, 8, 128], BF16)
        vw1 = sb("vw1", [128, 8, 2, 65], BF16)
        relu = [sb("relu%d" % j, [128, 512], F32) for j in range(2)]
        pT = [sb("pT%d" % j, [128, 1024], BF16) for j in range(2)]
        mch = [sb("mch%d" % j, [128, 128], BF16) for j in range(4)]
        sm = sb("sm", [128, 64], F32)
        wkall = sb("wkall", [128, NBIS], F32)
        junk = sb("junk", [128, S], mybir.dt.uint8)
        wrot2 = junk[:].bitcast(BF16).rearrange("p (c d) -> p c d", c=4)
        axp = sb("axp", [128, 4, 144], BF16)
        axq = sb("axq", [128, 4, 144], F32)
        axr = sb("axr", [128, 4, 144], F32)
        pooled = sb("pooled", [128, 4, 128], BF16)
        yaT = sb("yaT", [128, 4, 128], BF16)
        ybT = sb("ybT", [128, 4, 128], BF16)
        ycT = sb("ycT", [128, 4, 128], BF16)
        gat = sb("gat", [128, 12, 128], BF16)
        sgm = sb("sgm", [128, 24, 128], BF16)
        ytm = sb("ytm", [128, 512], BF16)
        oc = sb("oc", [128, 512], F32)
        otmp = sb("otmp", [128, 512], F32)
        rd = sb("rd", [128, 40], F32)
        imp = sb("imp", [128, 2, 128], F32)
        impw = sb("impw", [128, 128], F32)
        mx = sb("mx", [128, 16], F32)
        selm = sb("selm", [128, 2, 128], BF16)
        mT = sb("mT", [128, 8, 128], F32)
        mTb = sb("mTb", [128, 8, 128], BF16)
        zt = sb("zt", [128, 8, 128], F32)
        zc = sb("zc", [128, 8, 128], F32)
        rstd = sb("rstd", [128, 128], F32)
        psS = [ps("psS%d" % j, [128, 1024], F32) for j in range(2)]
        psO = ps("psO", [128, 2, 512], F32)
        psI = [ps("psI%d" % j, [128, 512], F32) for j in range(2)]

        P = Prog(nc)
        block = es.enter_context(nc.Block())
        mm = lambda out, l, r, st, sp: (lambda e: e.matmul(out, l, r, start=st, stop=sp, skip_group_check=True))

        for k in cst:
            P.dma(lambda e, k=k: e.dma_start(out=C[k][:], in_=cst[k]), writes=[k])
        P.op("pool", lambda e: e.memset(bv1[:], 1.0), writes=["bv1"])
        xs = [kk32[:, 0:2048], kk32[:, 2048:4096]]
        xb = scr16[:, :].rearrange("p (k t) -> p k t", k=8)
        wsA = [ks32[:, 0:1024].rearrange("p (k f) -> p k f", k=8), ks32[:, 1024:2048].rearrange("p (k f) -> p k f", k=8)]
        wbA = [ks[:, 4096:5120].rearrange("p (k f) -> p k f", k=8), ks[:, 5120:6144].rearrange("p (k f) -> p k f", k=8)]
        otA = [vs1f[:, 0:2048], vs1f[:, 2048:4096]]
        wvA = vs1f[:, 4096:4096 + 8 * NTM].rearrange("p (k f) -> p k f", k=8)
        ppA = [psS[0][:, 0:512], psS[0][:, 512:1024], psS[1][:, 0:512], psS[1][:, 512:1024],
               psO[:, 0, :], psO[:, 1, :], psI[0][:, :], psI[1][:, :]]
        ppN = ["psS0", "psS0b", "psS1", "psS1b", "psO", "psOb", "psI0", "psI1"]
        for l in range(DEPTH):
            for rc in range(20):
                j = rc % 2
                P.dma(lambda e, l=l, rc=rc, j=j: e.dma_start(out=xs[j][:, 0:1024], in_=WSRC[l, rc * 128:(rc + 1) * 128, :]), writes=["xs%d" % j])
                P.op("pool" if j else "dve", lambda e, j=j: e.tensor_copy(otA[j][:, 0:1024], xs[j][:, 0:1024]), reads=["xs%d" % j], writes=["ot%d" % j])
                P.dma(lambda e, l=l, rc=rc, j=j: e.dma_start(out=W16A[l, rc * 128:(rc + 1) * 128, :], in_=otA[j][:, 0:1024]), reads=["ot%d" % j], q="act")
        P.barrier()

        def phase_A(l):
            gs_names = []

            def gsn():
                gs_names.append("GS_%d" % len(gs_names))
                return [gs_names[-1]]
            xsrc = XT0 if l == 0 else XS
            P.dma(lambda e: e.dma_start(out=bpt[:], in_=BP[l]), writes=["bpt"])
            P.dma(lambda e: e.dma_start(out=btm[:], in_=BT[l]), writes=["btm"])
            for kc in range(8):
                j = kc % 2
                if l == 0 or not USE_XB:
                    P.dma(lambda e, kc=kc, j=j: e.dma_start(out=xs[j], in_=xsrc[kc * 128:(kc + 1) * 128, :]), writes=["xs%d" % j])
                    P.op("dve", lambda e, kc=kc, j=j: e.tensor_copy(xb[:, kc, :], xs[j]), reads=["xs%d" % j], writes=["xb%d" % kc])
                else:
                    P.dma(lambda e, kc=kc: e.dma_start(out=xb[:, kc, :], in_=XB[kc * 128:(kc + 1) * 128, :]), writes=["xb%d" % kc])
            for kc in range(8):
                j = kc % 2
                P.dma(lambda e, kc=kc, j=j: e.dma_start(out=wsA[j][:, 0:3, :].rearrange("p a f -> p (a f)")[:, 0:NTM],
                                                        in_=WV[l, kc * 128:(kc + 1) * 128, :]), writes=["ws%d" % j])
                P.op("dve", lambda e, kc=kc, j=j: e.tensor_copy(wvA[:, kc, :], wsA[j][:, 0:3, :].rearrange("p a f -> p (a f)")[:, 0:NTM]),
                     reads=["ws%d" % j], writes=["wv"])
            for i in range(NBLK):
                pt = ppA[i % 8]
                pn = ppN[i % 8]
                j = i % 2
                for kc in range(8):
                    P.op("pe", mm(pt[:, 0:NTM], xb[:, kc, i * 128:(i + 1) * 128], wvA[:, kc, :], kc == 0, kc == 7),
                         reads=["xb%d" % kc, "wv"], writes=[pn])
                P.op("dve", lambda e, pt=pt, j=j: e.tensor_tensor(out=tmo[j][:, :], in0=pt[:, 0:320], in1=btm[:, 0:320], op=ALU.add),
                     reads=[pn, "btm"], writes=["tmo%d" % j])
                P.op("dve", lambda e, pt=pt, i=i: e.tensor_tensor(out=iwcg[:, i, :], in0=pt[:, 320:352], in1=btm[:, 320:352], op=ALU.add),
                     reads=[pn, "btm"], writes=["iwcg"])
                P.dma(lambda e, i=i, j=j: e.dma_start(out=GSTk[i // 4][(i % 4) * 128:(i % 4 + 1) * 128, 0:320], in_=tmo[j][:, :]), reads=["tmo%d" % j], writes=gsn(), q="act")
            fc_order = list(range(52, NFC)) + list(range(0, 52))
            for fi, fc in enumerate(fc_order):
                j = fi % 2
                if fi == 9:
                    for k in range(NCH):
                        P.cc(lambda e, k=k: e.collective_compute("AllGather", ALU.bypass, replica_groups=[[0, 1, 2, 3], [4, 5, 6, 7]],
                                                                 ins=[GS[k * 256:(k + 1) * 256, :].opt()], outs=[GD[k * 1024:(k + 1) * 1024, :].opt()]),
                             reads=list(gs_names), writes=["GD"])
                P.dma(lambda e, fc=fc, j=j: e.dma_start(out=wsA[j], in_=WP[l, fc].rearrange("p (k f) -> p k f", k=8)), writes=["ws%d" % j])
                P.op("dve", lambda e, j=j: e.tensor_copy(wbA[j], wsA[j]), reads=["ws%d" % j], writes=["wb%d" % j])
                for tg in range(4):
                    pt = ppA[j * 4 + tg]
                    pn = ppN[j * 4 + tg]
                    for kc in range(8):
                        P.op("pe", mm(pt[:, :], wbA[j][:, kc, :], xb[:, kc, tg * 512:(tg + 1) * 512], kc == 0, kc == 7),
                             reads=["wb%d" % j, "xb%d" % kc], writes=[pn])
                    if tg % 2 == 0:
                        P.op("act", lambda e, pt=pt, j=j, tg=tg, fc=fc: e.activation(otA[j][:, tg * 512:(tg + 1) * 512], pt[:, :], AF.Identity,
                                                                                     bias=bpt[:, fc:fc + 1], scale=1.0),
                             reads=[pn, "bpt"], writes=["ot%d_%d" % (j, tg)])
                    else:
                        P.op("dve", lambda e, pt=pt, j=j, tg=tg, fc=fc: e.tensor_scalar(out=otA[j][:, tg * 512:(tg + 1) * 512], in0=pt[:, :],
                                                                                        scalar1=bpt[:, fc:fc + 1], scalar2=None, op0=ALU.add),
                             reads=[pn, "bpt"], writes=["ot%d_%d" % (j, tg)])
                otn = ["ot%d_%d" % (j, t) for t in range(4)]
                if fc < 52:
                    P.dma(lambda e, fc=fc, j=j: e.dma_start(out=OWNF[fc * 128:(fc + 1) * 128, :], in_=otA[j]), reads=otn, writes=["OWNS"], q="act")
                else:
                    P.dma(lambda e, fc=fc, j=j: e.dma_start(out=GSF[fc - 52], in_=otA[j]), reads=otn, writes=gsn(), q="act")
                if fc < 4:
                    P.dma(lambda e, fc=fc, j=j: e.dma_start(out=GSA[:, fc, :, :], in_=otA[j].rearrange("p (i t) -> p i t", t=128)[:, :, 112:128]),
                          reads=otn, writes=gsn(), q="act")

            def load_bv():
                bv4 = bv1f[:, :].rearrange("p (i r d) -> p i r d", r=4, d=65)
                for r in range(4):
                    for k in range(4):
                        P.dma(lambda e, r=r, k=k: e.dma_start(out=bv4[:, 4 * k:4 * k + 4, r, 0:64],
                                                              in_=GDTk[r][k][:, 0:64].rearrange("(i p) d -> p i d", p=128)),
                              reads=["GD"], writes=["bv1"])
            if EARLY_BV:
                load_bv()
            return load_bv

        def phase_B(l, load_bv):
            if not EARLY_BV:
                load_bv()
            POOLW, PVEC, W1K, W1V, W2, POS, LNV = POOLW_[l], PVEC_[l], W1K_[l], W1V_[l], W2_[l], POS_[l], LNV_[l]
            W16 = W16A[l]
            XT = XT0 if l == 0 else XS
            YOUT = YT if l == DEPTH - 1 else XS
            KC, VC = None, None
            kk4 = kk[:, :].rearrange("p (i r t) -> p i r t", r=4, t=128)
            ks4 = ks[:, :].rearrange("p (i r t) -> p i r t", r=4, t=128)
            for r in range(4):
                P.dma(lambda e, r=r: e.dma_start(out=kk4[:, :, r, :], in_=GDF[r][0].rearrange("p (i t) -> p i t", t=128)), reads=["GD"], writes=["kk"])
                P.dma(lambda e, r=r: e.dma_start(out=ks4[:, :, r, :], in_=GDF[r][1].rearrange("p (i t) -> p i t", t=128)), reads=["GD"], writes=["ks"])
            P.op("pool", lambda e: e.memset(vs1[:], 1.0), writes=["vs1"])
            P.op("pool", lambda e: e.memset(vw1[:], 1.0), writes=["vw1"])
            P.op("pool", lambda e: e.memset(vcmp1[:], 1.0), writes=["vcmp1"])
            P.op("pool", lambda e: e.memset(hT[:], 0.0), writes=["hT"])
            P.op("pool", lambda e: e.memset(kcmp[:], 0.0), writes=["kcmp"])
            bv4 = bv1f[:, :].rearrange("p (i r d) -> p i r d", r=4, d=65)
            vs4 = vs1f[:, :].rearrange("p (i r g d) -> p i r g d", r=4, g=2, d=65)
            for r in range(4):
                for k in range(4):
                    for g in range(2):
                        P.dma(lambda e, r=r, g=g, k=k: e.dma_start(out=vs4[:, 4 * k:4 * k + 4, r, g, 0:64],
                                                                   in_=GDTk[r][k][:, 64 + 64 * g:128 + 64 * g].rearrange("(i p) d -> p i d", p=128)),
                              reads=["GD"], writes=["vs1"])
            P.dma(lambda e: e.dma_start(out=pvec[:], in_=PVEC), writes=["pvec"])
            P.dma(lambda e: e.dma_start(out=lnv[:], in_=LNV), writes=["lnv"])
            P.dma(lambda e: e.dma_start(out=post[:], in_=POS), writes=["post"])
            P.op("dve", lambda e: e.tensor_copy(posb[:], post[:]), reads=["post"], writes=["posb"])
            P.op("dve", lambda e: e.tensor_tensor(out=pbs[:], in0=pvec[:, 0:4], in1=pvec[:, 4:8], op=ALU.mult),
                 reads=["pvec"], writes=["pbs"])
            st32 = scr[:, 0:4096].rearrange("p (l e) -> p l e", e=128)
            P.dma(lambda e: e.dma_start(out=st32[:, 0:4, :], in_=POOLW), writes=["scr"])
            P.dma(lambda e: e.dma_start(out=st32[:, 4:6, :], in_=W2), writes=["scr"])
            P.op("dve", lambda e: e.tensor_copy(poolw[:], st32[:, 0:4, :]), reads=["scr"], writes=["poolw"])
            P.op("dve", lambda e: e.tensor_copy(w2b[:], st32[:, 4:6, :]), reads=["scr"], writes=["w2b"])

            w1b = wrot[0][:].rearrange("p a b -> p (a b)")[:, 0:4096].rearrange("p (l e) -> p l e", e=128)
            for kv, (W1, SRC) in enumerate(((W1K, KC), (W1V, VC))):
                P.dma(lambda e, W1=W1: e.dma_start(out=st32[:, :, :], in_=W1), writes=["scr"])
                P.op("dve", lambda e: e.tensor_copy(w1b, st32[:, :, :]), reads=["scr"], writes=["wrot0"])
                src16 = scr16[:, 8192:16384]
                src4 = src16.rearrange("p (i r t) -> p i r t", r=4, t=128)
                for r in range(4):
                    P.dma(lambda e, r=r, kv=kv: e.dma_start(out=src4[:, :, r, :], in_=GDF[r][3 + kv].rearrange("p (i t) -> p i t", t=128)),
                          reads=["GD"], writes=["scrhi"])
                for l in range(32):
                    P.op("pe", mm(psI[0][:, 0:1], w1b[:, l, :], posb[:, kv, l:l + 1], l == 0, l == 31),
                         reads=["wrot0", "posb"], writes=["psI0"])
                P.op("dve", lambda e, kv=kv: e.tensor_copy(cbias[:, kv:kv + 1], psI[0][:, 0:1]), reads=["psI0"], writes=["cbias"])
                for l in range(32):
                    rhs = src16[:, l:min(l + 16 * 511, 8192):16]
                    P.op("pe", mm(psI[1][:, 0:511], w1b[:, l, :], rhs, l == 0, l == 31),
                         reads=["wrot0", "scrhi"], writes=["psI1"])
                P.op("act", lambda e, kv=kv: e.activation(hT[:, kv, 0:511], psI[1][:, 0:511], AF.Silu, bias=cbias[:, kv:kv + 1], scale=1.0),
                     reads=["psI1", "cbias"], writes=["hT"])
                if kv == 0:
                    P.op("pe", mm(psI[0][:, 0:511], w2b[:, 0, :], hT[:, 0, 0:511], True, True), reads=["w2b", "hT"], writes=["psI0"])
                    P.op("dve", lambda e: e.tensor_copy(kcmp[:, 0:511], psI[0][:, 0:511]), reads=["psI0"], writes=["kcmp"])
                else:
                    for c in range(4):
                        P.op("pe", mm(psI[0][:, c * 128:(c + 1) * 128], hT[:, 1, c * 128:(c + 1) * 128], w2b[:, 1, :], True, True),
                             reads=["w2b", "hT"], writes=["psI0"])
                    P.op("dve", lambda e: e.tensor_copy(vcmp1[:, :, :, 0:64],
                                                        psI[0][:, :].rearrange("p (c g d) -> p c g d", c=4, g=2)),
                         reads=["psI0"], writes=["vcmp1"])

            OWNV = OWNF.rearrange("(c p) t -> p c t", p=128)
            XTV = XT.rearrange("(c p) t -> p c t", p=128)
            YTV = YOUT.rearrange("(c p) t -> p c t", p=128)
            W16V = W16.rearrange("(w c p) d -> w p c d", p=128, c=4)


            def attention(tag, i, chunks, kq, vfun, maskfun, kreads, between=None):
                n = len(chunks)

                def qk(ci):
                    c = chunks[ci]
                    j = ci % 2
                    pS = psS[j]
                    pn = "psS%d" % j
                    for half in range(2):
                        ml = maskfun(c, half)
                        for mi, (mlhs, mreads) in enumerate(ml):
                            P.op("pe", mm(pS[:, half * 512:(half + 1) * 512], mlhs, C["c_idbig"][:, :], mi == 0, False),
                                 reads=list(mreads) + ["c_idbig"], writes=[pn])
                        kl, qr = kq(c, half)
                        P.op("pe", mm(pS[:, half * 512:(half + 1) * 512], kl, qr, len(ml) == 0, True),
                             reads=list(kreads) + ["ownB"], writes=[pn])
                    P.op("act", lambda e, pS=pS, j=j: e.activation(pT[j][:, :], pS[:, :], AF.Exp, scale=0.125),
                         reads=[pn], writes=["pT%d" % j])

                def pv(ci):
                    c = chunks[ci]
                    j = ci % 2
                    for hh in range(8):
                        P.op("pe", mm(psO[:, hh // 4, (hh % 4) * 65:(hh % 4) * 65 + 65], pT[j][:, hh * 128:(hh + 1) * 128], vfun(c, hh),
                                      ci == 0 and hh % 4 == 0, ci == n - 1),
                             reads=["pT%d" % j, "bv1", "vs1", "vw1", "vcmp1"], writes=["psO"])

                qk(0)
                for ci in range(n):
                    if ci + 1 < n:
                        qk(ci + 1)
                    pv(ci)
                    if between is not None:
                        between(ci)

            def rden_of(col0):
                for hb in range(2):
                    P.op("dve", lambda e, hb=hb: e.tensor_scalar(out=rd[:, col0 + 4 * hb:col0 + 4 * hb + 4],
                                                                 in0=psO[:, hb, 0:260].rearrange("p (h d) -> p h d", d=65)[:, :, 64],
                                                                 scalar1=1e-30, scalar2=None, op0=ALU.max),
                         reads=["psO"], writes=["rd"])
                P.op("dve", lambda e: e.reciprocal(rd[:, col0:col0 + 8], rd[:, col0:col0 + 8]), reads=["rd"], writes=["rd"])

            def wmul(dst, wcol0, dname):
                for hb in range(2):
                    P.op("dve", lambda e, hb=hb: e.tensor_tensor(
                        out=dst[:, hb * 256:(hb + 1) * 256].rearrange("p (h d) -> p h d", d=64),
                        in0=psO[:, hb, 0:260].rearrange("p (h d) -> p h d", d=65)[:, :, 0:64],
                        in1=rd[:, wcol0 + 4 * hb:wcol0 + 4 * hb + 4].unsqueeze(2).to_broadcast([128, 4, 64]),
                        op=ALU.mult), reads=["psO", "rd"], writes=[dname])

            def to_featmajor(src, dstT, gate0, dname):
                pst = psI[0][:, 0:256].bitcast(BF16)
                for c in range(4):
                    P.op("pe", lambda e, c=c: e.transpose(pst[:, c * 128:(c + 1) * 128], src[:, c * 128:(c + 1) * 128], C["c_ident"][:, :]),
                         reads=["ytm", "c_ident"], writes=["psI0"])
                P.op("dve", lambda e: e.tensor_tensor(out=dstT[:].rearrange("p c t -> p (c t)"), in0=pst,
                                                      in1=gat[:, gate0:gate0 + 4, :].rearrange("p c t -> p (c t)"), op=ALU.mult),
                     reads=["psI0", "gat"], writes=[dname])

            def tail(tsl, last):
                P.dma(lambda e, tsl=tsl: e.dma_start(out=xt[:], in_=XTV[:, :, tsl]), reads=["XS"], writes=["xt"])
                if debug:
                    for bi, (yy, ynm) in enumerate(((yaT, "yaT"), (ybT, "ybT"), (ycT, "ycT"))):
                        P.dma(lambda e, bi=bi, yy=yy, tsl=tsl: e.dma_start(out=DBG[bi, :, :, tsl], in_=yy[:]), reads=[ynm])
                ys = (yaT, ybT, ycT)
                yn = ("yaT", "ybT", "ycT")
                for br in range(3):
                    wt = wrot[br][:] if br < 2 else wrot2
                    wn = ("wrot%d" % br) if br < 2 else "junk"
                    pM = psS[br % 2]
                    pmn = "psS%d" % (br % 2)
                    for dc in range(8):
                        for ec in range(4):
                            P.op("pe", mm(pM[:, dc * 128:(dc + 1) * 128], wt[:, ec, dc * 128:(dc + 1) * 128], ys[br][:, ec, :], ec == 0, ec == 3),
                                 reads=[wn, yn[br]], writes=[pmn])
                    if br == 0:
                        P.op("dve", lambda e, pM=pM: e.tensor_tensor(out=mT[:].rearrange("p c t -> p (c t)"), in0=pM[:, :],
                                                                     in1=sgm[:, 0:8, :].rearrange("p c t -> p (c t)"), op=ALU.mult),
                             reads=[pmn, "sgm"], writes=["mT"])
                    else:
                        P.op("dve", lambda e, pM=pM, br=br: e.tensor_tensor(out=zc[:].rearrange("p c t -> p (c t)"), in0=pM[:, :],
                                                                            in1=sgm[:, 8 * br:8 * br + 8, :].rearrange("p c t -> p (c t)"), op=ALU.mult),
                             reads=[pmn, "sgm"], writes=["zc"])
                        P.op("pool", lambda e: e.tensor_tensor(out=mT[:], in0=mT[:], in1=zc[:], op=ALU.add), reads=["mT", "zc"], writes=["mT"])
                P.op("dve", lambda e: e.tensor_copy(mTb[:], mT[:]), reads=["mT"], writes=["mTb"])
                pOut = psS[1]
                for hf in range(2):
                    wt = wrot[(3 + hf) % 2]
                    wn = "wrot%d" % ((3 + hf) % 2)
                    P.dma(lambda e, wt=wt, hf=hf: e.dma_start(out=wt[:], in_=W16V[3 + hf]), writes=[wn])
                for hf in range(2):
                    wt = wrot[1] if hf == 0 else wrot[0]
                    wn = "wrot1" if hf == 0 else "wrot0"
                    for ec in range(8):
                        for dc in range(4 * hf, 4 * hf + 4):
                            P.op("pe", mm(pOut[:, ec * 128:(ec + 1) * 128], wt[:, dc % 4, ec * 128:(ec + 1) * 128], mTb[:, dc, :],
                                          hf == 0 and dc == 0 and ec % 4 == 0, hf == 1 and dc == 7),
                                 reads=[wn, "mTb"], writes=["psS1"])
                P.op("dve", lambda e: e.scalar_tensor_tensor(out=zt[:].rearrange("p c t -> p (c t)"), in0=xt[:].rearrange("p c t -> p (c t)"),
                                                             scalar=float(ALPHA), in1=pOut[:, :], op0=ALU.mult, op1=ALU.add),
                     reads=["xt", "psS1"], writes=["zt"])
                for dc in range(8):
                    P.op("pe", mm(psI[0][:, 0:128], C["c_onesm"][:, :], zt[:, dc, :], dc == 0, dc == 7), reads=["c_onesm", "zt"], writes=["psI0"])
                P.op("dve", lambda e: e.tensor_tensor(out=zc[:], in0=zt[:], in1=psI[0][:, 0:128].unsqueeze(1).to_broadcast([128, 8, 128]), op=ALU.subtract),
                     reads=["zt", "psI0"], writes=["zc"])
                P.op("act", lambda e: e.activation(zt[:], zc[:], AF.Square), reads=["zc"], writes=["zt"])
                for dc in range(8):
                    P.op("pe", mm(psI[1][:, 0:128], C["c_onesm"][:, :], zt[:, dc, :], dc == 0, dc == 7), reads=["c_onesm", "zt"], writes=["psI1"])
                P.op("dve", lambda e: e.tensor_scalar(out=rstd[:, :], in0=psI[1][:, 0:128], scalar1=float(LN_EPS), scalar2=None, op0=ALU.add),
                     reads=["psI1"], writes=["rstd"])
                P.op("act", lambda e: e.activation(rstd[:, :], rstd[:, :], AF.Sqrt), reads=["rstd"], writes=["rstd"])
                P.op("dve", lambda e: e.reciprocal(rstd[:, :], rstd[:, :]), reads=["rstd"], writes=["rstd"])
                P.op("dve", lambda e: e.tensor_tensor(out=zc[:], in0=zc[:], in1=rstd[:, :].unsqueeze(1).to_broadcast([128, 8, 128]), op=ALU.mult),
                     reads=["zc", "rstd"], writes=["zc"])
                for dc in range(8):
                    P.op("dve" if dc % 2 else "pool", lambda e, dc=dc: e.tensor_scalar(out=zt[:, dc, :], in0=zc[:, dc, :], scalar1=lnv[:, dc:dc + 1], scalar2=lnv[:, 8 + dc:9 + dc],
                                                          op0=ALU.mult, op1=ALU.add), reads=["zc", "lnv"], writes=["zt"])
                P.dma(lambda e, tsl=tsl: e.dma_start(out=YTV[:, :, tsl], in_=zt[:]), reads=["zt"], writes=["XS"])
                if l < DEPTH - 1 and USE_XB:
                    P.op("pool", lambda e: e.tensor_copy(mTb[:], zt[:]), reads=["zt"], writes=["mTb"])
                    P.dma(lambda e, tsl=tsl: e.dma_start(out=XB.rearrange("(c p) t -> p c t", p=128)[:, :, tsl], in_=mTb[:]), reads=["mTb"], writes=["XB"])
                if not last:
                    for br in range(2):
                        P.dma(lambda e, br=br: e.dma_start(out=wrot[br][:], in_=W16V[br]), writes=["wrot%d" % br])

            for br in range(2):
                P.dma(lambda e, br=br: e.dma_start(out=wrot[br][:], in_=W16V[br]), writes=["wrot%d" % br])
            for i in range(NBLK):
                NK = 4 * i + 4
                tsl = slice(i * 128, (i + 1) * 128)
                if i == 0:
                    P.dma(lambda e, tsl=tsl: e.dma_start(out=own[:, 0:40, :], in_=OWNV[:, 0:40, tsl]), reads=["OWNS"], writes=["ownA"])
                    P.dma(lambda e, tsl=tsl: e.dma_start(out=own[:, 40:52, :], in_=OWNV[:, 40:52, tsl]), reads=["OWNS"], writes=["ownB"])
                j0 = 4 if i == 0 else 0
                kc0 = 4 * i - 4 + j0
                for jj in range(j0, 8):
                    rr, li = jj % 4, i - 1 + jj // 4
                    P.dma(lambda e, jj=jj, rr=rr, li=li: e.dma_start(out=kwb[:, jj, :], in_=GDF[rr][2][:, li * 128:(li + 1) * 128]),
                          reads=["GD"], writes=["kwb"])
                    P.dma(lambda e, jj=jj, rr=rr, li=li: e.dma_start(out=vw1[:, jj, :, 0:64],
                                                                     in_=GDTk[rr][li // 4][(li % 4) * 128:(li % 4 + 1) * 128, 192:320].rearrange("p (g d) -> p g d", g=2)),
                          reads=["GD"], writes=["vw1"])
                for k in range(4):
                    li = i if k < 3 else i - 1
                    if li < 0:
                        continue
                    P.dma(lambda e, k=k, li=li: e.dma_start(out=cand[:, k, :, :], in_=GDA[k][:, :, li, :]), reads=["GD"], writes=["cand"])
                for seg in range(i + 1):
                    last = seg == i
                    for h in range(8):
                        jj = (seg * 8 + h) % 2
                        P.op("pe", mm(psI[jj][:, :], own[64:96, 40 + h, :], kk[64:96, seg * 512:(seg + 1) * 512], True, True),
                             reads=["ownB", "kk"], writes=["psI%d" % jj])
                        P.op("act", lambda e, jj=jj: e.activation(relu[jj][:, :], psI[jj][:, :], AF.Relu),
                             reads=["psI%d" % jj], writes=["relu%d" % jj])
                        scs = scr[:, seg * 512:(seg + 1) * 512]
                        if h == 0 and last:
                            P.op("dve", lambda e, jj=jj, scs=scs, i=i: e.scalar_tensor_tensor(out=scs, in0=relu[jj][:, :], scalar=iwcg[:, i, 0:1],
                                                                                             in1=C["c_causbig"][:, :], op0=ALU.mult, op1=ALU.add),
                                 reads=["relu%d" % jj, "iwcg", "c_causbig"], writes=["scr"])
                        elif h == 0:
                            P.op("dve", lambda e, jj=jj, scs=scs, i=i: e.tensor_scalar(out=scs, in0=relu[jj][:, :], scalar1=iwcg[:, i, 0:1], scalar2=None,
                                                                                      op0=ALU.mult),
                                 reads=["relu%d" % jj, "iwcg"], writes=["scr"])
                        else:
                            P.op("dve", lambda e, jj=jj, scs=scs, i=i, h=h: e.scalar_tensor_tensor(out=scs, in0=relu[jj][:, :], scalar=iwcg[:, i, h:h + 1],
                                                                                                  in1=scs, op0=ALU.mult, op1=ALU.add),
                                 reads=["relu%d" % jj, "iwcg", "scr"], writes=["scr"])
                NC = NK * 128
                P.op("dve", lambda e, NC=NC: e.tensor_reduce(out=sm[:, 0:1], in_=scr[:, 0:NC], axis=AX.X, op=ALU.max), reads=["scr"], writes=["smb"])
                P.op("dve", lambda e, NC=NC: e.scalar_tensor_tensor(out=relu[0][:, :], in0=C["c_causbig"][:, :], scalar=-2.0,
                                                                    in1=scr[:, NC - 512:NC], op0=ALU.mult, op1=ALU.add),
                     reads=["scr", "c_causbig", "relu0"], writes=["relu0"])
                P.op("dve", lambda e: e.tensor_reduce(out=sm[:, 1:2], in_=relu[0][:, :], axis=AX.X, op=ALU.min), reads=["relu0"], writes=["smb"])
                if i > 0:
                    P.op("dve", lambda e, NC=NC: e.tensor_reduce(out=sm[:, 7:8], in_=scr[:, 0:NC - 512], axis=AX.X, op=ALU.min), reads=["scr"], writes=["smb"])
                    P.op("dve", lambda e: e.tensor_tensor(out=sm[:, 1:2], in0=sm[:, 1:2], in1=sm[:, 7:8], op=ALU.min), reads=["smb"], writes=["smb"])
                P.op("dve", lambda e: e.tensor_scalar(out=sm[:, 3:4], in0=sm[:, 1:2], scalar1=-1e-3, scalar2=None, op0=ALU.add), reads=["smb"], writes=["smb"])
                P.op("dve", lambda e: e.scalar_tensor_tensor(out=sm[:, 2:3], in0=sm[:, 0:1], scalar=2e-3, in1=sm[:, 3:4], op0=ALU.add, op1=ALU.subtract),
                     reads=["smb"], writes=["smb"])
                P.op("dve", lambda e: e.tensor_scalar(out=wkall[:, :], in0=C["c_bisw"][:, :], scalar1=sm[:, 2:3], scalar2=None, op0=ALU.mult),
                     reads=["smb", "c_bisw"], writes=["wkall"])
                bis_state = {"it": 0}

                def bis_iter(NC=NC):
                    it = bis_state["it"]
                    if it >= NBIS:
                        return
                    bis_state["it"] = it + 1
                    P.op("dve", lambda e, it=it: e.tensor_tensor(out=sm[:, 4:5], in0=sm[:, 3:4], in1=wkall[:, it:it + 1], op=ALU.add),
                         reads=["smb", "wkall"], writes=["smb"])
                    P.op("dve", lambda e, NC=NC: e.tensor_scalar(out=junk[:, 0:NC], in0=scr[:, 0:NC], scalar1=sm[:, 4:5], scalar2=None,
                                                                 op0=ALU.is_ge, op1=ALU.add, accum_out=sm[:, 5:6]),
                         reads=["scr", "smb"], writes=["junk", "smb"])
                    P.op("dve", lambda e, it=it: e.scalar_tensor_tensor(out=sm[:, 6:7], in0=sm[:, 5:6], scalar=float(TOPK) - 0.5, in1=wkall[:, it:it + 1],
                                                                        op0=ALU.is_ge, op1=ALU.mult), reads=["smb", "wkall"], writes=["smb"])
                    P.op("dve", lambda e: e.tensor_tensor(out=sm[:, 3:4], in0=sm[:, 3:4], in1=sm[:, 6:7], op=ALU.add), reads=["smb"], writes=["smb"])

                def bis_between(ci, per=(2 if NK < 12 else 1)):
                    for _ in range(per):
                        bis_iter()

                kl = [0, 1, 2] + ([3] if i > 0 else [])
                for kn, k in enumerate(kl):
                    if kn == 0:
                        P.op("dve", lambda e, k=k: e.tensor_scalar(out=axp[:, :, 0:16], in0=cand[:, k, :, :], scalar1=C["c_sel"][:, k:k + 1], scalar2=None,
                                                                   op0=ALU.mult), reads=["cand", "c_sel"], writes=["axp"])
                    else:
                        P.op("dve", lambda e, k=k: e.scalar_tensor_tensor(out=axp[:, :, 0:16], in0=cand[:, k, :, :], scalar=C["c_sel"][:, k:k + 1],
                                                                          in1=axp[:, :, 0:16], op0=ALU.mult, op1=ALU.add),
                             reads=["cand", "c_sel", "axp"], writes=["axp"])
                P.op("pool", lambda e: e.tensor_copy(axp[:, :, 16:144], own[:, 0:4, :]), reads=["ownA"], writes=["axp"])
                P.op("pool", lambda e: e.tensor_tensor(out=axq[:, 0:4, 1:144], in0=axp[:, 0:4, 1:144], in1=axp[:, 0:4, 0:143], op=ALU.add),
                     reads=["axp"], writes=["axq"])
                P.op("pool", lambda e: e.tensor_tensor(out=axr[:, 1:4, 3:144], in0=axq[:, 1:4, 3:144], in1=axq[:, 1:4, 1:142], op=ALU.add),
                     reads=["axq"], writes=["axr"])
                P.op("pool", lambda e: e.tensor_tensor(out=axq[:, 2:4, 7:144], in0=axr[:, 2:4, 7:144], in1=axr[:, 2:4, 3:140], op=ALU.add),
                     reads=["axr"], writes=["axq"])
                P.op("pool", lambda e: e.tensor_tensor(out=axr[:, 3:4, 15:144], in0=axq[:, 3:4, 15:144], in1=axq[:, 3:4, 7:136], op=ALU.add),
                     reads=["axq"], writes=["axr"])
                cur = {0: axq, 1: axr, 2: axq, 3: axr}
                for g, w in enumerate((2, 4, 8, 16)):
                    srcg = cur[g]
                    if i == 0:
                        P.op("dve", lambda e, g=g, srcg=srcg: e.tensor_tensor(out=otmp[:, 0:128], in0=srcg[:, g, 16:144],
                                                                              in1=C["c_invw"][:, g, :], op=ALU.mult),
                             reads=["axq", "axr", "c_invw"], writes=["otmp"])
                        P.op("dve", lambda e, g=g: e.tensor_tensor(out=pooled[:, g, :], in0=otmp[:, 0:128], in1=own[:, g, :], op=ALU.subtract),
                             reads=["otmp", "ownA"], writes=["pooled"])
                    else:
                        P.op("dve", lambda e, g=g, w=w, srcg=srcg: e.scalar_tensor_tensor(out=pooled[:, g, :], in0=srcg[:, g, 16:144], scalar=1.0 / w,
                                                                                          in1=own[:, g, :], op0=ALU.mult, op1=ALU.subtract),
                             reads=["axq", "axr", "ownA"], writes=["pooled"])
                if i > 0:
                    tail(slice((i - 1) * 128, i * 128), False)
                P.op("act", lambda e: e.activation(gat[:], own[:, 4:16, :], AF.Silu), reads=["ownA"], writes=["gat"])
                P.op("act", lambda e: e.activation(sgm[:], own[:, 16:40, :], AF.Sigmoid), reads=["ownA"], writes=["sgm"])
                P.op("act", lambda e, i=i: e.activation(rd[:, 0:24], iwcg[:, i, 8:32], AF.Sigmoid), reads=["iwcg"], writes=["rd"])

                for g in range(4):
                    P.op("pe", mm(psI[1][:, g * 128:(g + 1) * 128], poolw[:, g, :], pooled[:, g, :], True, True),
                         reads=["poolw", "pooled"], writes=["psI1"])
                for g in range(4):
                    P.op("act", lambda e, g=g: e.activation(otmp[:, g * 128:(g + 1) * 128], psI[1][:, g * 128:(g + 1) * 128], AF.Identity,
                                                            bias=pbs[:, g:g + 1], scale=pvec[:, 4 + g:5 + g]),
                         reads=["psI1", "pbs", "pvec"], writes=["otmp"])
                P.op("dve", lambda e: e.tensor_tensor(out=yaT[:].rearrange("p c t -> p (c t)"), in0=otmp[:, :],
                                                      in1=gat[:, 0:4, :].rearrange("p c t -> p (c t)"), op=ALU.mult),
                     reads=["otmp", "gat"], writes=["yaT"])

                if i + 1 < NBLK:
                    tn = slice((i + 1) * 128, (i + 2) * 128)
                    P.dma(lambda e, tn=tn: e.dma_start(out=own[:, 0:40, :], in_=OWNV[:, 0:40, tn]), reads=["OWNS"], writes=["ownA"])
                NCc = (32 * i + 30) // 128 + 1
                kq = lambda hh: own[64 * (hh // 4):64 * (hh // 4) + 64, 48 + (hh % 4), :]

                def cmp_mask(c, half, i=i):
                    o = 512 - 32 * i + 128 * c
                    return [(C["c_patc"][:, o:o + 128], ["c_patc"])]

                for ci in range(NCc):
                    c = ci
                    j = ci % 2
                    pS = psS[j]
                    pn = "psS%d" % j
                    for half in range(2):
                        (mlhs, mreads), = cmp_mask(c, half)
                        P.op("pe", mm(pS[:, half * 512:(half + 1) * 512], mlhs, C["c_idbig"][:, :], True, False),
                             reads=["c_patc", "c_idbig"], writes=[pn])
                        P.op("pe", mm(pS[:, half * 512:(half + 1) * 512], kcmp[64 * half:64 * half + 64, c * 128:(c + 1) * 128],
                                      own[64 * half:64 * half + 64, 48:52, :], False, True),
                             reads=["kcmp", "ownB"], writes=[pn])
                    P.op("act", lambda e, pS=pS, j=j: e.activation(pT[j][:, :], pS[:, :], AF.Exp, scale=0.125), reads=[pn], writes=["pT%d" % j])
                    for hh in range(8):
                        P.op("pe", mm(psO[:, hh // 4, (hh % 4) * 65:(hh % 4) * 65 + 65], pT[j][:, hh * 128:(hh + 1) * 128], vcmp1[:, c, hh // 4, :],
                                      ci == 0 and hh % 4 == 0, ci == NCc - 1), reads=["pT%d" % j, "vcmp1"], writes=["psO"])
                        P.op("pe", mm(psI[hh // 4][:, (hh % 4) * 128:(hh % 4) * 128 + 128], pT[j][:, hh * 128:(hh + 1) * 128], C["c_ovl"][:, c, :],
                                      ci == 0 and hh % 4 == 0, ci == NCc - 1), reads=["pT%d" % j, "c_ovl"], writes=["psI%d" % (hh // 4)])
                rden_of(32)
                sig3 = rd[:, 0:24].rearrange("p (h b) -> p h b", b=3)
                P.op("dve", lambda e: e.tensor_tensor(out=sm[:, 8:16], in0=sig3[:, :, 0], in1=rd[:, 32:40], op=ALU.mult), reads=["rd"], writes=["sm"])
                for hb in range(2):
                    P.op("dve", lambda e, hb=hb: e.tensor_tensor(
                        out=oc[:, hb * 256:(hb + 1) * 256].rearrange("p (h d) -> p h d", d=64),
                        in0=psO[:, hb, 0:260].rearrange("p (h d) -> p h d", d=65)[:, :, 0:64],
                        in1=sm[:, 8 + 4 * hb:12 + 4 * hb].unsqueeze(2).to_broadcast([128, 4, 64]), op=ALU.mult),
                        reads=["psO", "sm"], writes=["oc"])
                fo = 128 - 8 * i
                for g in range(2):
                    for r_ in range(4):
                        if r_ == 0:
                            P.op("dve", lambda e, g=g, fo=fo: e.scalar_tensor_tensor(out=imp[:, g, :], in0=psI[g][:, 0:128], scalar=rd[:, 32 + 4 * g:33 + 4 * g],
                                                                                     in1=C["c_fp"][:, fo:fo + 128], op0=ALU.mult, op1=ALU.add),
                                 reads=["psI%d" % g, "rd", "c_fp"], writes=["imp"])
                        else:
                            P.op("dve", lambda e, g=g, r_=r_: e.scalar_tensor_tensor(out=imp[:, g, :], in0=psI[g][:, r_ * 128:(r_ + 1) * 128],
                                                                                     scalar=rd[:, 32 + 4 * g + r_:33 + 4 * g + r_],
                                                                                     in1=imp[:, g, :], op0=ALU.mult, op1=ALU.add),
                                 reads=["psI%d" % g, "rd", "imp"], writes=["imp"])
                    P.op("dve", lambda e, g=g, i=i: e.tensor_tensor(out=imp[:, g, 0:1], in0=imp[:, g, 0:1], in1=C["c_f0"][:, i:i + 1], op=ALU.add),
                         reads=["imp", "c_f0"], writes=["imp"])
                    P.op("dve", lambda e, g=g: e.max(out=mx[:, 0:8], in_=imp[:, g, :]), reads=["imp"], writes=["mx"])
                    P.op("dve", lambda e, g=g: e.match_replace(out=impw[:, :], in_to_replace=mx[:, 0:8], in_values=imp[:, g, :], imm_value=-3e30),
                         reads=["imp", "mx"], writes=["impw"])
                    P.op("dve", lambda e: e.max(out=mx[:, 8:16], in_=impw[:, :]), reads=["impw"], writes=["mx"])
                    P.op("dve", lambda e, g=g: e.tensor_scalar(out=selm[:, g, :], in0=imp[:, g, :], scalar1=mx[:, 15:16], scalar2=-1.0,
                                                               op0=ALU.is_ge, op1=ALU.add), reads=["imp", "mx"], writes=["selm"])

                def slc_mask(c, half, i=i):
                    g = half
                    m = mch[(2 * c + half) % 4]
                    mn = "mch%d" % ((2 * c + half) % 4)
                    P.op("pool", lambda e, m=m, c=c, g=g: e.tensor_copy(
                        m[:, :].rearrange("p (b s) -> p b s", s=64),
                        selm[:, g, 2 * c:2 * c + 2].unsqueeze(2).to_broadcast([128, 2, 64])), reads=["selm"], writes=[mn])
                    res = [(m[:, :], [mn])]
                    jl = c - 4 * i
                    if jl >= 0:
                        res.append((C["c_caus"][:, jl, :], ["c_caus"]))
                    return res

                attention("slc", i, list(range(NK)),
                          kq=lambda c, g: (ks[64 * g:64 * g + 64, c * 128:(c + 1) * 128], own[64 * g:64 * g + 64, 48:52, :]),
                          vfun=lambda c, hh: vs1[:, c, hh // 4, :],
                          maskfun=slc_mask, kreads=["ks"], between=bis_between)
                rden_of(32)
                P.op("dve", lambda e: e.tensor_tensor(out=sm[:, 8:16], in0=sig3[:, :, 1], in1=rd[:, 32:40], op=ALU.mult), reads=["rd"], writes=["sm"])
                for hb in range(2):
                    P.op("dve", lambda e, hb=hb: e.tensor_tensor(
                        out=otmp[:, hb * 256:(hb + 1) * 256].rearrange("p (h d) -> p h d", d=64),
                        in0=psO[:, hb, 0:260].rearrange("p (h d) -> p h d", d=65)[:, :, 0:64],
                        in1=sm[:, 8 + 4 * hb:12 + 4 * hb].unsqueeze(2).to_broadcast([128, 4, 64]), op=ALU.mult),
                        reads=["psO", "sm"], writes=["otmp"])
                P.op("pool", lambda e: e.tensor_tensor(out=oc[:, :], in0=oc[:, :], in1=otmp[:, :], op=ALU.add), reads=["oc", "otmp"], writes=["oc"])

                def win_mask(c, half, i=i):
                    return [(C["c_wm"][:, c, :], ["c_wm"])]

                attention("win", i, list(range(j0, 8)),
                          kq=lambda c, g: (kwb[64 * g:64 * g + 64, c, :], own[64 * g:64 * g + 64, 48:52, :]),
                          vfun=lambda c, hh: vw1[:, c, hh // 4, :],
                          maskfun=win_mask, kreads=["kwb"], between=bis_between)
                rden_of(32)
                P.op("dve", lambda e: e.tensor_tensor(out=sm[:, 8:16], in0=sig3[:, :, 2], in1=rd[:, 32:40], op=ALU.mult), reads=["rd"], writes=["sm"])
                for hb in range(2):
                    P.op("dve", lambda e, hb=hb: e.tensor_tensor(
                        out=otmp[:, hb * 256:(hb + 1) * 256].rearrange("p (h d) -> p h d", d=64),
                        in0=psO[:, hb, 0:260].rearrange("p (h d) -> p h d", d=65)[:, :, 0:64],
                        in1=sm[:, 8 + 4 * hb:12 + 4 * hb].unsqueeze(2).to_broadcast([128, 4, 64]), op=ALU.mult),
                        reads=["psO", "sm"], writes=["otmp"])
                P.op("pool", lambda e: e.tensor_tensor(out=oc[:, :], in0=oc[:, :], in1=otmp[:, :], op=ALU.add), reads=["oc", "otmp"], writes=["oc"])
                ocv = oc[:, :].rearrange("p (g j d) -> p g j d", g=2, j=4)
                P.op("dve", lambda e: e.tensor_copy(ytm[:, :].rearrange("p (j g d) -> p j g d", j=4, g=2),
                                                    oc[:, :].rearrange("p (g j d) -> p j g d", g=2, j=4)), reads=["oc"], writes=["ytm"])
                to_featmajor(ytm, ycT, 8, "ycT")

                while bis_state["it"] < NBIS:
                    bis_iter()
                P.dma(lambda e: e.dma_start(out=wrot2, in_=W16V[2]), writes=["junk"])

                def dsa_mask(c, half, i=i):
                    m = mch[(2 * c + half) % 4]
                    mn = "mch%d" % ((2 * c + half) % 4)
                    if half == 0:
                        P.op("dve", lambda e, m=m, c=c: e.tensor_scalar(out=m[:, :], in0=scr[:, c * 128:(c + 1) * 128], scalar1=sm[:, 3:4], scalar2=-1.0,
                                                                        op0=ALU.is_ge, op1=ALU.add), reads=["scr", "smb"], writes=[mn])
                        dsa_mask.cur = (m, mn)
                    m, mn = dsa_mask.cur
                    return [(m[:, :], [mn])]

                attention("dsa", i, list(range(NK)),
                          kq=lambda c, half: (kk[0:64, c * 128:(c + 1) * 128], own[0:64, 40 + 4 * half:44 + 4 * half, :]),
                          vfun=lambda c, hh: bv1[:, c, :],
                          maskfun=dsa_mask, kreads=["kk"])
                rden_of(24)
                wmul(ytm, 24, "ytm")
                to_featmajor(ytm, ybT, 4, "ybT")

                if i + 1 < NBLK:
                    tn = slice((i + 1) * 128, (i + 2) * 128)
                    P.dma(lambda e, tn=tn: e.dma_start(out=own[:, 40:52, :], in_=OWNV[:, 40:52, tn]), reads=["OWNS"], writes=["ownB"])
            tail(slice((NBLK - 1) * 128, NBLK * 128), True)

        for l in range(DEPTH):
            lb = phase_A(l)
            P.barrier()
            phase_B(l, lb)
            P.barrier()
        P.finish()
        P.emit(block, es)
    return nc


_CACHE = {}


def _get(name, fn):
    if name not in _CACHE:
        _CACHE[name] = fn()
    return _CACHE[name]


def _blockdiag_w1(w1):
    w = w1.reshape(32, 64, 64)
    o = np.zeros((128, 32, 128), np.float32)
    for g in range(2):
        o[64 * g:64 * g + 64, :, 64 * g:64 * g + 64] = w.transpose(1, 0, 2)
    return o


def kernel(x, w_in, b_in, pool_w, pool_b, pool_scale, cmp_pos_k, cmp_pos_v, cmp_w1_k, cmp_w2_k,
           cmp_w1_v, cmp_w2_v, w_proj_a, w_proj_b, w_proj_c, w_o, ln_g, ln_b):
    f = lambda a: np.ascontiguousarray(np.asarray(a, dtype=np.float32))
    x = f(x)
    nc = _get("F", build_F)
    toks = [core_tokens(c) for c in range(8)]
    WP, BP, WV, BT, WSRC, POOLW, PVEC, W1K, W1V, W2, POS, LNV = ([] for _ in range(12))
    for l in range(DEPTH):
        Wp, bpk, Wv, bt = pack_w(f(w_in[l]), f(b_in[l]))
        WP.append(Wp.reshape(8, 128, NFC, 128).transpose(2, 1, 0, 3).reshape(NFC, 128, 1024)); BP.append(bpk); WV.append(Wv); BT.append(bt)
        wpc_perm = f(w_proj_c[l])[_CPERM, :]
        WSRC.append(np.concatenate([f(w_proj_a[l]), f(w_proj_b[l]), wpc_perm, f(w_o[l])], axis=0))
        w2 = np.zeros((128, 2, 128), np.float32)
        for kv, w in enumerate((f(cmp_w2_k[l]), f(cmp_w2_v[l]))):
            for g in range(2):
                w2[64 * g:64 * g + 64, kv, 64 * g:64 * g + 64] = w
        pos = np.zeros((128, 2, 32), np.float32)
        for kv, p in enumerate((f(cmp_pos_k[l]), f(cmp_pos_v[l]))):
            pos[0:64, kv, :] = p.T
            pos[64:128, kv, :] = p.T
        W2.append(w2); POS.append(pos)
        POOLW.append(f(pool_w[l]).transpose(1, 0, 2))
        PVEC.append(np.concatenate([f(pool_b[l]).T, f(pool_scale[l]).reshape(4, 128).T], axis=1))
        W1K.append(_blockdiag_w1(f(cmp_w1_k[l]))); W1V.append(_blockdiag_w1(f(cmp_w1_v[l])))
        LNV.append(np.concatenate([f(ln_g[l]).reshape(8, 128).T, f(ln_b[l]).reshape(8, 128).T], axis=1))
    st = lambda lst: np.ascontiguousarray(np.stack(lst, axis=0).astype(np.float32))
    common = {"WP": st(WP), "BP": st(BP), "WV": st(WV), "BT": st(BT), "WSRC": st(WSRC), "POOLW": st(POOLW), "PVEC": st(PVEC),
              "W1K": st(W1K), "W1V": st(W1V), "W2": st(W2), "POS": st(POS), "LNV": st(LNV)}
    in_maps = []
    for c in range(8):
        b, idx = toks[c]
        m = {"XT0": np.ascontiguousarray(x[b, idx, :].T)}
        m.update(common)
        m.update(make_consts(c % 4))
        in_maps.append(m)
    res = run_bass_kernel_spmd(nc, in_maps, core_ids=list(range(8)))
    out = np.zeros((2, S, D), np.float32)
    for c in range(8):
        b, idx = toks[c]
        out[b, idx, :] = np.asarray(res.results[c]["YT"], dtype=np.float32).T
    return out
```
